# Optimizing a Trainium2 kernel written in Bass

```python
import jax, jax.numpy as jnp
from jax import lax
import numpy as np

D_MODEL = 2048
BATCH = 4
SEQ = 8192
DEPTH = 1
DEC_BATCH = 8
DEC_SEQ = 32
PAST_LEN = 2048

CHUNK = 64
N_SUB = 3
FFN_RES = 0.5
D_FF = 5632
GMLP_CHUNK = 128
D_A = D_MODEL // 2
N_GROUPS_A = 8
GROUP_A = D_A // N_GROUPS_A
HEAD_DIM = 64
N_HEADS = (D_MODEL // 2) // HEAD_DIM
N_KV_HEADS = 4
Q_PER_KV = N_HEADS // N_KV_HEADS
D_B = N_HEADS * HEAD_DIM
D_KV = N_KV_HEADS * HEAD_DIM
WINDOW = 128
WINDOW_CHUNKS = WINDOW // CHUNK
BAND = (WINDOW_CHUNKS + 1) * CHUNK
ROT_DIM = HEAD_DIM // 4
ROPE_THETA = 500000.0
ATTN_SCALE = HEAD_DIM ** -0.5
NEG_INF = -1e30
EPS = 1e-6
SPLITS = (D_A, 2 * D_A, 2 * D_A + D_B, 2 * D_A + D_B + D_KV, 2 * D_A + D_B + 2 * D_KV,
          2 * D_A + D_B + 2 * D_KV + D_MODEL)
D_IN = 2 * D_A + D_B + 2 * D_KV + 2 * D_MODEL

kernel_name = 'chunk_stream_gmlp_swa_hybrid'


def _rmsnorm(x, g):
    xf = x.astype(jnp.float32)
    y = xf * lax.rsqrt(jnp.mean(xf * xf, axis=-1, keepdims=True) + EPS)
    return (y * g.astype(jnp.float32)).astype(x.dtype)


def _layernorm(x, g, b):
    xf = x.astype(jnp.float32)
    mu = jnp.mean(xf, axis=-1, keepdims=True)
    var = jnp.mean(jnp.square(xf - mu), axis=-1, keepdims=True)
    y = (xf - mu) * lax.rsqrt(var + EPS)
    return (y * g.astype(jnp.float32) + b.astype(jnp.float32)).astype(x.dtype)


def _adaln(c, w_ada, b_ada):
    mod = jax.nn.silu(c) @ w_ada + b_ada
    return mod.reshape(c.shape[0], N_SUB, 3, D_MODEL)


def _modulate(h, shift, scale):
    return h * (1 + scale[:, None, :]) + shift[:, None, :]


def _ffn_half(x, mod, g, w1, w3, w2):
    h = _modulate(_rmsnorm(x, g), mod[:, 0], mod[:, 1])
    y = (jax.nn.silu(h @ w1) * (h @ w3)) @ w2
    return x + FFN_RES * mod[:, 2][:, None, :] * y


def _rope(x, pos):
    inv_freq = ROPE_THETA ** (-jnp.arange(0, ROT_DIM, 2, dtype=jnp.float32) / ROT_DIM)
    ang = pos.astype(jnp.float32)[:, None] * inv_freq[None, :]
    cos = jnp.cos(ang)[:, None, :]
    sin = jnp.sin(ang)[:, None, :]
    xr = x[..., :ROT_DIM].astype(jnp.float32)
    x1, x2 = xr[..., :ROT_DIM // 2], xr[..., ROT_DIM // 2:]
    rot = jnp.concatenate([x1 * cos - x2 * sin, x2 * cos + x1 * sin], axis=-1).astype(x.dtype)
    return jnp.concatenate([rot, x[..., ROT_DIM:]], axis=-1)


def _mixer_inputs(x, mod, lw, pos):
    b, s, _ = x.shape
    h = _modulate(_rmsnorm(x, lw['g_mix']), mod[:, 0], mod[:, 1])
    proj = h @ lw['w_in']
    u, v, q, k, va, ga, gb = jnp.split(proj, SPLITS, axis=-1)
    u = jax.nn.gelu(u, approximate=False)
    v_n = _layernorm(jax.nn.gelu(v, approximate=False), lw['ln_v_g'], lw['ln_v_b'])
    q = _rope(_rmsnorm(q.reshape(b, s, N_HEADS, HEAD_DIM), lw['g_q']), pos)
    k = _rope(_rmsnorm(k.reshape(b, s, N_KV_HEADS, HEAD_DIM), lw['g_k']), pos)
    va = va.reshape(b, s, N_KV_HEADS, HEAD_DIM)
    return u, v_n, q, k, va, ga, gb


def _gmlp_mask(dtype):
    i = np.arange(GMLP_CHUNK)
    return jnp.asarray((i[None, :] // CHUNK) <= (i[:, None] // CHUNK), dtype=dtype)


def _spatial_gate(u, v_n, w_s, b_s):
    b, s, _ = u.shape
    l = min(s, GMLP_CHUNK)
    w = (w_s * _gmlp_mask(w_s.dtype))[:, :l, :l]
    vb = v_n.reshape(b, s // l, l, N_GROUPS_A, GROUP_A)
    sv = jnp.einsum('gij,bnjgc->bnigc', w, vb) + b_s[:, :l].T[None, None, :, :, None]
    return u * sv.reshape(b, s, D_A)


def _sink_attend(qb, kb, vb, sinks, valid):
    s = jnp.einsum('bnqhgd,bnkhd->bnhgqk', qb, kb).astype(jnp.float32) * ATTN_SCALE
    s = jnp.where(valid[None, :, None, None, None, :], s, NEG_INF)
    sink = sinks.astype(jnp.float32).reshape(N_KV_HEADS, Q_PER_KV)[None, None, :, :, None, None]
    sink = jnp.broadcast_to(sink, s.shape[:-1] + (1,))
    p = jax.nn.softmax(jnp.concatenate([s, sink], axis=-1), axis=-1)[..., :-1]
    return jnp.einsum('bnhgqk,bnkhd->bnqhgd', p.astype(vb.dtype), vb)


def _swa_prompt(q, k, v, sinks):
    b, s = q.shape[:2]
    nc = s // CHUNK
    pad = WINDOW_CHUNKS * CHUNK

    def band(t):
        tp = jnp.pad(t, ((0, 0), (pad, 0), (0, 0), (0, 0)))
        tp = tp.reshape(b, nc + WINDOW_CHUNKS, CHUNK, N_KV_HEADS, HEAD_DIM)
        return jnp.concatenate([tp[:, i:i + nc] for i in range(WINDOW_CHUNKS + 1)], axis=2)

    key_pos = jnp.arange(nc)[:, None] * CHUNK + jnp.arange(BAND)[None, :] - pad
    qb = q.reshape(b, nc, CHUNK, N_KV_HEADS, Q_PER_KV, HEAD_DIM)
    o = _sink_attend(qb, band(k), band(v), sinks, key_pos >= 0)
    return o.reshape(b, s, D_B)


def _swa_sample(q, k, v, cache_k, cache_v, sinks):
    b, n = q.shape[:2]
    k_all = jnp.concatenate([cache_k.astype(k.dtype), k], axis=1)
    v_all = jnp.concatenate([cache_v.astype(v.dtype), v], axis=1)
    valid = jnp.ones((1, k_all.shape[1]), dtype=bool)
    qb = q.reshape(b, 1, n, N_KV_HEADS, Q_PER_KV, HEAD_DIM)
    o = _sink_attend(qb, k_all[:, None], v_all[:, None], sinks, valid)
    win = cache_k.shape[1]
    return o.reshape(b, n, D_B), k_all[:, -win:], v_all[:, -win:]


def _merge(x, mod, a, o, ga, gb, lw):
    y = jax.nn.sigmoid(ga) * (a @ lw['w_pa']) + jax.nn.sigmoid(gb) * (o @ lw['w_pb'])
    return x + mod[:, 2][:, None, :] * (y @ lw['w_o'])


def _layer_prompt(x, c, lw):
    mod = _adaln(c, lw['w_ada'], lw['b_ada'])
    x = _ffn_half(x, mod[:, 0], lw['g_ffn1'], lw['w1_ffn1'], lw['w3_ffn1'], lw['w2_ffn1'])
    pos = jnp.arange(x.shape[1])
    u, v_n, q, k, v, ga, gb = _mixer_inputs(x, mod[:, 1], lw, pos)
    a = _spatial_gate(u, v_n, lw['w_s'], lw['b_s'])
    o = _swa_prompt(q, k, v, lw['sinks'])
    x = _merge(x, mod[:, 1], a, o, ga, gb, lw)
    x = _ffn_half(x, mod[:, 2], lw['g_ffn2'], lw['w1_ffn2'], lw['w3_ffn2'], lw['w2_ffn2'])
    return x, k, v


def _layer_sample(x, c, cache_k, cache_v, lw):
    mod = _adaln(c, lw['w_ada'], lw['b_ada'])
    x = _ffn_half(x, mod[:, 0], lw['g_ffn1'], lw['w1_ffn1'], lw['w3_ffn1'], lw['w2_ffn1'])
    pos = PAST_LEN + jnp.arange(x.shape[1])
    u, v_n, q, k, v, ga, gb = _mixer_inputs(x, mod[:, 1], lw, pos)
    a = _spatial_gate(u, v_n, lw['w_s'], lw['b_s'])
    o, k_win, v_win = _swa_sample(q, k, v, cache_k, cache_v, lw['sinks'])
    x = _merge(x, mod[:, 1], a, o, ga, gb, lw)
    x = _ffn_half(x, mod[:, 2], lw['g_ffn2'], lw['w1_ffn2'], lw['w3_ffn2'], lw['w2_ffn2'])
    return x, k_win, v_win, v_n


def setup_inputs(seed: int = 0) -> dict:
    key = jax.random.key(seed)
    ks = jax.random.split(key, 28)

    def nrm(k, shape, scale):
        return jax.random.normal(k, shape, jnp.float32) * scale

    win = min(WINDOW, PAST_LEN)
    L = DEPTH
    return {
        'x_prompt': nrm(ks[0], (BATCH, SEQ, D_MODEL), 1.0),
        'x_sample': nrm(ks[1], (DEC_BATCH, DEC_SEQ, D_MODEL), 1.0),
        'cache_swa_k': nrm(ks[2], (L, DEC_BATCH, win, N_KV_HEADS, HEAD_DIM), 1.0),
        'cache_swa_v': nrm(ks[3], (L, DEC_BATCH, win, N_KV_HEADS, HEAD_DIM), 1.0),
        'c_prompt': nrm(ks[4], (BATCH, D_MODEL), 1.0),
        'c_sample': nrm(ks[5], (DEC_BATCH, D_MODEL), 1.0),
        'w_ada': nrm(ks[6], (L, D_MODEL, N_SUB * 3 * D_MODEL), 0.5 * D_MODEL ** -0.5),
        'b_ada': nrm(ks[7], (L, N_SUB * 3 * D_MODEL), 0.02),
        'g_ffn1': 1.0 + nrm(ks[8], (L, D_MODEL), 0.05),
        'w1_ffn1': nrm(ks[9], (L, D_MODEL, D_FF), D_MODEL ** -0.5),
        'w3_ffn1': nrm(ks[10], (L, D_MODEL, D_FF), D_MODEL ** -0.5),
        'w2_ffn1': nrm(ks[11], (L, D_FF, D_MODEL), D_FF ** -0.5),
        'g_mix': 1.0 + nrm(ks[12], (L, D_MODEL), 0.05),
        'w_in': nrm(ks[13], (L, D_MODEL, D_IN), D_MODEL ** -0.5),
        'g_q': 1.0 + nrm(ks[14], (L, HEAD_DIM), 0.05),
        'g_k': 1.0 + nrm(ks[15], (L, HEAD_DIM), 0.05),
        'ln_v_g': 1.0 + nrm(ks[16], (L, D_A), 0.05),
        'ln_v_b': nrm(ks[17], (L, D_A), 0.02),
        'w_s': nrm(ks[18], (L, N_GROUPS_A, GMLP_CHUNK, GMLP_CHUNK), GMLP_CHUNK ** -0.5),
        'b_s': 1.0 + nrm(ks[19], (L, N_GROUPS_A, GMLP_CHUNK), 0.05),
        'sinks': nrm(ks[20], (L, N_HEADS), 0.5),
        'w_pa': nrm(ks[21], (L, D_A, D_MODEL), D_A ** -0.5),
        'w_pb': nrm(ks[22], (L, D_B, D_MODEL), D_B ** -0.5),
        'w_o': nrm(ks[23], (L, D_MODEL, D_MODEL), D_MODEL ** -0.5),
        'g_ffn2': 1.0 + nrm(ks[24], (L, D_MODEL), 0.05),
        'w1_ffn2': nrm(ks[25], (L, D_MODEL, D_FF), D_MODEL ** -0.5),
        'w3_ffn2': nrm(ks[26], (L, D_MODEL, D_FF), D_MODEL ** -0.5),
        'w2_ffn2': nrm(ks[27], (L, D_FF, D_MODEL), D_FF ** -0.5),
    }


def reference(x_prompt, x_sample, cache_swa_k, cache_swa_v, c_prompt, c_sample, w_ada, b_ada,
              g_ffn1, w1_ffn1, w3_ffn1, w2_ffn1, g_mix, w_in, g_q, g_k, ln_v_g, ln_v_b, w_s, b_s,
              sinks, w_pa, w_pb, w_o, g_ffn2, w1_ffn2, w3_ffn2, w2_ffn2):
    win = cache_swa_k.shape[2]
    y_p, y_s = x_prompt, x_sample
    kp_l, vp_l, ks_l, vs_l, gv_l = [], [], [], [], []
    for l in range(DEPTH):
        lw = {'w_ada': w_ada[l], 'b_ada': b_ada[l],
              'g_ffn1': g_ffn1[l], 'w1_ffn1': w1_ffn1[l], 'w3_ffn1': w3_ffn1[l], 'w2_ffn1': w2_ffn1[l],
              'g_mix': g_mix[l], 'w_in': w_in[l], 'g_q': g_q[l], 'g_k': g_k[l],
              'ln_v_g': ln_v_g[l], 'ln_v_b': ln_v_b[l], 'w_s': w_s[l], 'b_s': b_s[l],
              'sinks': sinks[l], 'w_pa': w_pa[l], 'w_pb': w_pb[l], 'w_o': w_o[l],
              'g_ffn2': g_ffn2[l], 'w1_ffn2': w1_ffn2[l], 'w3_ffn2': w3_ffn2[l], 'w2_ffn2': w2_ffn2[l]}
        y_p, k_p, v_p = _layer_prompt(y_p, c_prompt, lw)
        y_s, k_s, v_s, vn_s = _layer_sample(y_s, c_sample, cache_swa_k[l], cache_swa_v[l], lw)
        kp_l.append(k_p[:, -win:])
        vp_l.append(v_p[:, -win:])
        ks_l.append(k_s)
        vs_l.append(v_s)
        gv_l.append(vn_s)
    swa_k_prompt = jnp.stack(kp_l)
    swa_v_prompt = jnp.stack(vp_l)
    swa_k_sample = jnp.stack(ks_l)
    swa_v_sample = jnp.stack(vs_l)
    gmlp_v_sample = jnp.stack(gv_l)
    return (y_p, y_s, swa_k_prompt, swa_v_prompt, swa_k_sample, swa_v_sample, gmlp_v_sample)
```

```python
import os
import numpy as np
from contextlib import ExitStack
import concourse.bass as bass
import concourse.mybir as mybir
from concourse.bass_utils import run_bass_kernel_spmd

F32 = mybir.dt.float32
BF16 = mybir.dt.bfloat16
ALU = mybir.AluOpType
AF = mybir.ActivationFunctionType
AX = mybir.AxisListType

D = 2048
KC = 16
DA = 1024
NHD = 16
NKV = 4
HD = 64
TP = 512
TS = 32
TH = 128
T0 = TS + TH
UB = 32
RSL = 4
CONVG = 4
EPS = 1e-6
ROPE_THETA = 500000.0
PAST_LEN = 2048
ENGS = ("pe", "act", "dve", "pool", "sp")


class Buf:
    __slots__ = ("writers", "readers")

    def __init__(self):
        self.writers = {}
        self.readers = {}


class Ins:
    __slots__ = ("eng", "fn", "deps", "needed", "sem", "val", "is_dma", "key")


class Prog:
    def __init__(self):
        self.streams = {e: [] for e in ENGS}
        self.dma_counts = {}

    def add(self, eng, fn, reads=(), writes=(), dma_sem=None, extra_deps=()):
        ins = Ins()
        ins.eng = eng
        ins.fn = fn
        ins.needed = False
        ins.is_dma = dma_sem is not None
        ins.sem = None
        ins.val = None
        if ins.is_dma:
            c = self.dma_counts.get(id(dma_sem), 0) + 16
            self.dma_counts[id(dma_sem)] = c
            ins.sem = dma_sem
            ins.val = c
            ins.key = ("dma", id(dma_sem))
        else:
            ins.key = eng
        deps = {}
        is_dma = ins.is_dma

        def dep(p):
            if p is None or p is ins:
                return
            if (not p.is_dma) and (not is_dma) and p.eng == eng == "pe":
                return
            deps[id(p)] = p

        for b in reads:
            for p in b.writers.values():
                dep(p)
        for b in writes:
            for k, p in b.readers.items():
                if k == ins.key and not is_dma:
                    continue
                dep(p)
            for k, p in b.writers.items():
                if k == ins.key and not is_dma:
                    continue
                dep(p)
        for p in extra_deps:
            dep(p)
        ins.deps = list(deps.values())
        for p in ins.deps:
            p.needed = True
        for b in reads:
            b.readers[ins.key] = ins
        for b in writes:
            b.writers[ins.key] = ins
        self.streams[eng].append(ins)
        return ins

    def emit(self, block, sems):
        for e in ENGS:
            c = 0
            for ins in self.streams[e]:
                if ins.is_dma:
                    continue
                if ins.needed:
                    c += 1
                    ins.sem = sems[e]
                    ins.val = c
        streams = self.streams

        def run_stream(e, h):
            waited = {}
            for ins in streams[e]:
                need = {}
                for p in ins.deps:
                    k = id(p.sem)
                    if waited.get(k, 0) >= p.val:
                        continue
                    if k not in need or need[k][1] < p.val:
                        need[k] = (p.sem, p.val)
                for k, (s, v) in need.items():
                    h.wait_ge(s, v)
                    waited[k] = v
                r = ins.fn(h)
                if r is None:
                    continue
                if ins.is_dma:
                    r.then_inc(ins.sem, 16)
                elif ins.needed:
                    r.then_inc(ins.sem, 1)

        @block.tensor
        def _(h):
            run_stream("pe", h)

        @block.scalar
        def _(h):
            run_stream("act", h)

        @block.vector
        def _(h):
            run_stream("dve", h)

        @block.gpsimd
        def _(h):
            run_stream("pool", h)

        @block.sync
        def _(h):
            run_stream("sp", h)


def sections(NF):
    secs = [("ADA0", 768), ("F1A", 32 * NF), ("F1B", 16 * NF), ("ADA1", 768), ("MIXA", 256), ("MIXU", 128), ("MIXK", 64),
            ("MIXM", 768), ("MIXO", 256), ("ADA2", 768), ("F2A", 32 * NF), ("F2B", 16 * NF)]
    off = {}
    o = 0
    for n, c in secs:
        assert c % UB == 0
        off[n] = (o // UB, c // UB)
        o += c
    return secs, off, o // UB


def vec_layout(NT):
    o = {}
    c = 0
    for n, w in [("c", 32), ("bada", 144), ("g", 48), ("halo", 1), ("sink", 16), ("gq", 64), ("gk", 64),
                 ("lng", 8), ("lnb", 8), ("ropeP", 16 * NT * 4), ("ropeS", 16), ("ropeH", 16)]:
        o[n] = c
        c += w
    return o, c


def build(NT, NF):
    NFA = max(NF, 44)
    secs, soff, NU = sections(NF)
    vo, NV = vec_layout(NT)
    nc = bass.Bass("TRN2", target_bir_lowering=False)
    dt_in = lambda n, s, t=F32: nc.dram_tensor(n, s, t, kind="ExternalInput").ap()
    dt_out = lambda n, s: nc.dram_tensor(n, s, F32, kind="ExternalOutput").ap()
    wsrc = dt_in("wsrc", [NU, 128, UB * 128])
    xT_d = dt_in("xT", [D, NT * TP])
    xsh_d = dt_in("xsh", [D, T0])
    vecs_d = dt_in("vecs", [128, NV])
    wsT_d = dt_in("wsT", [128, 8 * 128])
    bsrow_d = dt_in("bsrow", [1, 1024])
    ck_d = dt_in("cache_k", [128, 256])
    cv_d = dt_in("cache_v", [128, 256])
    wbf = nc.dram_tensor("wbf", [NU, 128, UB * 128], BF16, kind="Internal").ap()
    yT_d = dt_out("yT", [D, NT * TP])
    yTs_d = dt_out("yTs", [D, TS])
    klast_d = dt_out("klast", [128, 256])
    vlast_d = dt_out("vlast", [128, 256])
    ks_d = dt_out("ks", [128, 256])
    vs_d = dt_out("vs", [128, 256])
    vnT_d = dt_out("vnT", [DA, TS])

    with ExitStack() as es:
        def sb(name, shape, dt):
            return es.enter_context(nc.sbuf_tensor(name, shape, dt))

        def sem(name):
            return es.enter_context(nc.semaphore(name))

        P = Prog()
        xT = sb("xTs", [128, KC, TP], F32)
        hT = sb("hTs", [128, KC, TP], BF16)
        gT = sb("gTs", [128, NFA, TP], BF16)
        ring = sb("ring", [128, RSL, UB * 128], BF16)
        vecs = sb("vecs_s", [128, NV], F32)
        gv = sb("gv", [128, 4, 1024], F32)
        qkf = sb("qkf", [128, 256], F32)
        sqn = sb("sqn", [128, 1280], F32)
        qkb = sb("qkb", [128, 1280], BF16)
        vaf = sb("vaf", [128, 256], F32)
        ropet = sb("ropet", [128, 4, 160], F32)
        kTr = sb("kTr", [128, 3, 512], BF16)
        vaDr = sb("vaDr", [128, 3, 512], BF16)
        kTc = sb("kTc", [128, 512], BF16)
        vaDc = sb("vaDc", [128, 512], BF16)
        pT = sb("pTs", [128, 2, 2, 512], BF16)
        pTS = sb("pTSs", [128, 2, 128], BF16)
        NTMP = 8
        tmp = sb("tmp", [128, NTMP, TP], F32)
        sqb = sb("sqb", [128, 2, TP], BF16)
        rstd = sb("rstd", [128, TP], F32)
        small = sb("small", [128, 64], F32)
        modT = sb("modT", [128, 144, 2], F32)
        scal = sb("scal", [128, 288], F32)
        scT = sb("scT", [128, 32], BF16)
        esink = sb("esink", [128, 16], F32)
        identb = sb("identb", [128, 128], BF16)
        identf = sb("identf", [128, 128], F32)
        onesb = sb("onesb", [128, 128], BF16)
        wsTb = sb("wsTb", [128, 8, 128], BF16)
        Cg = sb("Cg", [128, 8, 128], F32)
        CgS = sb("CgS", [128, 8, 32], F32)
        vnTs = sb("vnTs", [128, 8, 32], F32)
        epsc = sb("epsc", [128, 1], F32)
        banks = [es.enter_context(nc.psum_tensor(f"bank{i}", [128, 512], F32)) for i in range(8)]
        bankB = [Buf() for _ in range(8)]
        bank_ctr = [0]

        def alloc():
            b = bank_ctr[0] % 8
            bank_ctr[0] += 1
            return banks[b], bankB[b]

        s_eng = {e: sem("s_" + e) for e in ("pe", "act", "dve", "pool", "sp")}
        s_ring = [sem(f"s_ring{i}") for i in range(RSL)]
        s_wb = [sem(f"s_wb{i}") for i in range(RSL)]
        s_ringp = [sem(f"s_ringp{i}") for i in range(RSL)]
        s_x = [sem(f"s_x{i}") for i in range(KC)]
        s_tmp = [sem(f"s_tmp{i}") for i in range(NTMP)]
        s_misc = [sem(f"s_misc{i}") for i in range(16)]
        blk = es.enter_context(nc.Block())

        xB = [Buf() for _ in range(KC)]
        hB = [Buf() for _ in range(KC)]
        gB = [Buf() for _ in range(NFA)]
        ringB = [Buf() for _ in range(RSL)]
        tmpB = [Buf() for _ in range(NTMP)]
        B = {k: Buf() for k in ("vecs", "gv0", "gv1", "gv2", "gv3", "qkf", "sqn", "qkb", "vaf", "ropet", "kTc", "vaDc",
                                "pT0", "pT1", "pTS", "sqb0", "sqb1", "rstd", "small", "modT", "scal", "scT", "esink",
                                "identb", "identf", "onesb", "wsTb", "Cg", "CgS", "vnTs", "epsc", "kT0", "kT1", "kT2",
                                "vaD0", "vaD1", "vaD2", "wsTf", "bsrow", "ckf", "cvf", "ss", "dram_out")}
        gvB = [B["gv0"], B["gv1"], B["gv2"], B["gv3"]]
        kTB = [B["kT0"], B["kT1"], B["kT2"]]
        vaDB = [B["vaD0"], B["vaD1"], B["vaD2"]]
        tmp_ctr = [0]

        def talloc():
            i = tmp_ctr[0] % NTMP
            tmp_ctr[0] += 1
            return i

        def A(eng, meth, *args, reads=(), writes=(), **kw):
            ins = P.add(eng, lambda e: getattr(e, meth)(*args, **kw), reads=reads, writes=writes)
            if os.environ.get("KDBG"):
                ins.fn.__dict__["lbl"] = (meth, [str(getattr(a, "ap", a)) + "@" + str(getattr(a, "offset", "")) for a in args],
                                          {k: (str(v.ap) + "@" + str(v.offset)) if hasattr(v, "ap") else v for k, v in kw.items()})
            return ins

        def DMA(eng, out, in_, s, reads=(), writes=(), extra=()):
            return P.add(eng, lambda e: e.dma_start(out=out, in_=in_), reads=reads, writes=writes, dma_sem=s,
                         extra_deps=extra)

        out_dmas = []

        useq = list(range(NU))
        non_ada = [u for n, c in secs if not n.startswith("ADA") for u in range(soff[n][0], soff[n][0] + soff[n][1])]
        for _ in range(NT):
            useq += non_ada
        pos_of = {}
        p_ = 0
        for u in range(NU):
            pos_of[(0, u)] = p_
            p_ += 1
        for t in range(1, NT + 1):
            for u in non_ada:
                pos_of[(t, u)] = p_
                p_ += 1
        NSEQ = len(useq)
        non_ada_set = set(non_ada)
        wbB = [Buf() for _ in range(NU)]
        rs = {"next": 0, "released": 0}

        def ring_advance():
            while rs["next"] < NSEQ and rs["next"] < rs["released"] + RSL:
                k = rs["next"]
                u = useq[k]
                sl = k % RSL
                if k < NU:
                    DMA("pool", ring[:, sl, :], wsrc[u], s_ringp[sl], writes=[ringB[sl]])
                    if u in non_ada_set:
                        DMA("sp", wbf[u], ring[:, sl, :], s_wb[sl], reads=[ringB[sl]], writes=[wbB[u]])
                else:
                    DMA("sp", ring[:, sl, :], wbf[u], s_ring[sl], reads=[wbB[u]], writes=[ringB[sl]])
                rs["next"] += 1

        def ring_release():
            rs["released"] += 1
            ring_advance()

        class Cur:
            def __init__(self, pss, sec):
                self.pss = pss
                self.u0 = soff[sec][0]
                self.i = 0
                self.cur_unit = None

            def _touch(self, bi):
                u = self.u0 + bi // UB
                k = pos_of[(self.pss, u)]
                assert k >= rs["released"], (k, rs)
                assert k < rs["next"], ("unit not loaded", k, rs)
                return k % RSL, bi % UB

            def blk(self, bi, n=1):
                sl, o = self._touch(bi)
                return ring[:, sl, o * 128:(o + n) * 128], ringB[sl]

        DMA("sp", vecs[:], vecs_d, s_misc[0], writes=[B["vecs"]])
        wsTf = gv[:, 0, :]
        bsr = gv[0:1, 1, :]
        ckf = gv[:, 2, 0:256]
        cvf = gv[:, 2, 256:512]
        DMA("sp", wsTf, wsT_d, s_misc[1], writes=[gvB[0]])
        DMA("sp", bsr, bsrow_d, s_misc[2], writes=[gvB[1]])
        DMA("sp", ckf, ck_d, s_misc[3], writes=[gvB[2]])
        DMA("sp", cvf, cv_d, s_misc[4], writes=[gvB[2]])
        A("pool", "memset", identb[:], 0.0, writes=[B["identb"]])
        A("pool", "affine_select", out=identb[:], in_=identb[:], pattern=[[-1, 128]], compare_op=ALU.not_equal,
          fill=1.0, base=0, channel_multiplier=1, reads=[B["identb"]], writes=[B["identb"]])
        A("pool", "memset", identf[:], 0.0, writes=[B["identf"]])
        A("pool", "affine_select", out=identf[:], in_=identf[:], pattern=[[-1, 128]], compare_op=ALU.not_equal,
          fill=1.0, base=0, channel_multiplier=1, reads=[B["identf"]], writes=[B["identf"]])
        A("pool", "memset", onesb[:], 1.0, writes=[B["onesb"]])
        A("pool", "memset", epsc[:], EPS, writes=[B["epsc"]])
        A("pool", "memset", pT[:], 0.0, writes=[B["pT0"], B["pT1"]])
        A("pool", "memset", gv[:, 3, :], 1.0, writes=[gvB[3]])
        ring_advance()

        A("act", "activation", out=scT[:], in_=vecs[:, vo["c"]:vo["c"] + 32], func=AF.Silu, reads=[B["vecs"]],
          writes=[B["scT"]])
        A("act", "activation", out=esink[:], in_=vecs[:, vo["sink"]:vo["sink"] + 16], func=AF.Exp, reads=[B["vecs"]],
          writes=[B["esink"]])
        wsTf3 = wsTf.rearrange("p (g i) -> p g i", g=8)
        A("dve", "memset", wsTf3[64:128, :, 0:64], 0.0, reads=[], writes=[gvB[0]])
        A("dve", "tensor_copy", out=wsTb[:], in_=wsTf3, reads=[gvB[0]], writes=[B["wsTb"]])
        onesf = gv[:, 3, 0:128]
        lnb = lambda g: vecs[:, vo["lnb"] + g:vo["lnb"] + g + 1]
        lng = lambda g: vecs[:, vo["lng"] + g:vo["lng"] + g + 1]
        for half in range(2):
            bk, bkB = alloc()
            A("pe", "matmul", bk[:, :], lhsT=onesf, rhs=wsTf[:, half * 512:(half + 1) * 512], start=True, stop=True,
              reads=[gvB[3], gvB[0]], writes=[bkB])
            bk2, bk2B = alloc()
            A("pe", "matmul", bk2[:, :], lhsT=gv[0:1, 3, 0:128], rhs=bsr[:, half * 512:(half + 1) * 512], start=True,
              stop=True, reads=[gvB[3], gvB[1]], writes=[bk2B])
            ti = talloc()
            A("act", "activation", out=tmp[:, ti, :], in_=bk2[:, :], func=AF.Copy, reads=[bk2B], writes=[tmpB[ti]])
            for gg in range(4):
                g = half * 4 + gg
                A("dve", "scalar_tensor_tensor", out=Cg[:, g, :], in0=bk[:, gg * 128:(gg + 1) * 128], scalar=lnb(g),
                  in1=tmp[:, ti, gg * 128:(gg + 1) * 128], op0=ALU.mult, op1=ALU.add,
                  reads=[bkB, tmpB[ti], B["vecs"]], writes=[B["Cg"]])
        bk, bkB = alloc()
        bk2, bk2B = alloc()
        bsr3 = bsr.rearrange("p (g i) -> p g i", g=8)
        for g in range(8):
            A("pe", "matmul", bk[:, g * 32:(g + 1) * 32], lhsT=gv[0:32, 3, 0:128], rhs=wsTf3[0:32, g, 0:32], start=True,
              stop=True, reads=[gvB[3], gvB[0]], writes=[bkB])
            A("pe", "matmul", bk2[:, g * 32:(g + 1) * 32], lhsT=gv[0:1, 3, 0:128], rhs=bsr3[:, g, 0:32], start=True,
              stop=True, reads=[gvB[3], gvB[1]], writes=[bk2B])
        ti = talloc()
        A("act", "activation", out=tmp[:, ti, 0:256], in_=bk2[:, 0:256], func=AF.Copy, reads=[bk2B], writes=[tmpB[ti]])
        for g in range(8):
            A("dve", "scalar_tensor_tensor", out=CgS[:, g, :], in0=bk[:, g * 32:(g + 1) * 32], scalar=lnb(g),
              in1=tmp[:, ti, g * 32:(g + 1) * 32], op0=ALU.mult, op1=ALU.add, reads=[bkB, tmpB[ti], B["vecs"]],
              writes=[B["CgS"]])
        A("dve", "tensor_copy", out=qkb[:, 0:256], in_=ckf, reads=[gvB[2]], writes=[B["qkb"]])
        bk, bkB = alloc()
        bkb = bk[:, :].bitcast(BF16)
        for h in range(NKV):
            A("pe", "transpose", bkb[0:64, h * 128:(h + 1) * 128], qkb[:, h * 64:(h + 1) * 64], identb[:],
              reads=[B["qkb"], B["identb"]], writes=[bkB])
        A("dve", "tensor_copy", out=kTc[0:64, :], in_=bkb[0:64, 0:512], reads=[bkB], writes=[B["kTc"]])
        vaDc3 = vaDc[:, :].rearrange("p (h d) -> p h d", h=4)
        cv3 = cvf.rearrange("p (h d) -> p h d", h=4)
        A("dve", "tensor_copy", out=vaDc3[:, :, 0:64], in_=cv3, reads=[gvB[2]], writes=[B["vaDc"]])
        A("dve", "tensor_copy", out=vaDc3[:, :, 64:128], in_=cv3, reads=[gvB[2]], writes=[B["vaDc"]])
        out_dmas.append(DMA("act", ks_d[0:96, :], ck_d[32:128, :], s_misc[5]))
        out_dmas.append(DMA("act", vs_d[0:96, :], cv_d[32:128, :], s_misc[6]))

        def sc(s, r, kind, dc):
            c = ((s * 2 + r) * 3 + kind) * 16 + dc
            return scal[:, c:c + 1]

        def ada(pss, s):
            cur = Cur(pss, "ADA%d" % s)
            bk, bkB = alloc()
            for ci in range(48):
                for kc in range(KC):
                    bi = ci * KC + kc
                    w, wB = cur.blk(bi)
                    A("pe", "matmul", bk[:, 2 * ci:2 * ci + 2], lhsT=w, rhs=scT[:, 2 * kc:2 * kc + 2], start=(kc == 0),
                      stop=(kc == KC - 1), reads=[wB, B["scT"]], writes=[bkB])
                    if bi % UB == UB - 1:
                        ring_release()
            bo = vo["bada"] + s * 48
            A("dve", "tensor_tensor", out=modT[:, s * 48:(s + 1) * 48, :],
              in0=bk[:, 0:96].rearrange("p (c r) -> p c r", r=2),
              in1=vecs[:, bo:bo + 48].unsqueeze(2).to_broadcast([128, 48, 2]), op=ALU.add,
              reads=[bkB, B["vecs"]], writes=[B["modT"]])
            for r in range(2):
                base = ((s * 2 + r) * 3) * 16
                A("dve", "scalar_tensor_tensor", out=scal[:, base:base + 16], in0=modT[:, s * 48 + 16:s * 48 + 32, r],
                  scalar=1.0, in1=vecs[:, vo["g"] + s * 16:vo["g"] + s * 16 + 16], op0=ALU.add, op1=ALU.mult,
                  reads=[B["modT"], B["vecs"]], writes=[B["scal"]])
                A("dve", "tensor_copy", out=scal[:, base + 16:base + 32], in_=modT[:, s * 48:s * 48 + 16, r],
                  reads=[B["modT"]], writes=[B["scal"]])
                A("dve", "tensor_scalar", out=scal[:, base + 32:base + 48], in0=modT[:, s * 48 + 32:s * 48 + 48, r],
                  scalar1=(1.0 if s == 1 else 0.5), scalar2=None, op0=ALU.mult, reads=[B["modT"]],
                  writes=[B["scal"]])

        def norm(s, T, segs):
            bk, bkB = alloc()
            for dc in range(KC):
                q = dc % 2
                A("act", "activation", out=sqb[:, q, 0:T], in_=xT[:, dc, 0:T], func=AF.Square, reads=[xB[dc]],
                  writes=[B["sqb%d" % q]])
                A("pe", "matmul", bk[:, 0:T], lhsT=onesb[:], rhs=sqb[:, q, 0:T], start=(dc == 0), stop=(dc == KC - 1),
                  reads=[B["onesb"], B["sqb%d" % q]], writes=[bkB])
            A("act", "activation", out=rstd[:, 0:T], in_=bk[:, 0:T], func=AF.Sqrt, scale=1.0 / D, bias=epsc[:, 0:1],
              reads=[bkB, B["epsc"]], writes=[B["rstd"]])
            A("dve", "reciprocal", out=rstd[:, 0:T], in_=rstd[:, 0:T], reads=[B["rstd"]], writes=[B["rstd"]])
            for dc in range(KC):
                ti = talloc()
                A("dve", "tensor_tensor", out=tmp[:, ti, 0:T], in0=xT[:, dc, 0:T], in1=rstd[:, 0:T], op=ALU.mult,
                  reads=[xB[dc], B["rstd"]], writes=[tmpB[ti]])
                for (c0, c1, r) in segs:
                    A("act", "activation", out=hT[:, dc, c0:c1], in_=tmp[:, ti, c0:c1], func=AF.Identity,
                      scale=sc(s, r, 0, dc), bias=sc(s, r, 1, dc), reads=[tmpB[ti], B["scal"]], writes=[hB[dc]])

        def ffn(pss, s, T, segs, nameA, nameB, store=None):
            norm(s, T, segs)
            cur = Cur(pss, nameA)
            for f in range(NF):
                ba, baB = alloc()
                bb, bbB = alloc()
                for wi, (bk, bkB) in enumerate(((ba, baB), (bb, bbB))):
                    for kc in range(KC):
                        w, wB = cur.blk(f * 32 + wi * 16 + kc)
                        A("pe", "matmul", bk[:, 0:T], lhsT=w, rhs=hT[:, kc, 0:T], start=(kc == 0), stop=(kc == KC - 1),
                          reads=[wB, hB[kc]], writes=[bkB])
                ring_release()
                ti = talloc()
                A("act", "activation", out=tmp[:, ti, 0:T], in_=ba[:, 0:T], func=AF.Silu, reads=[baB],
                  writes=[tmpB[ti]])
                A("dve", "tensor_tensor", out=gT[:, f, 0:T], in0=bb[:, 0:T], in1=tmp[:, ti, 0:T], op=ALU.mult,
                  reads=[bbB, tmpB[ti]], writes=[gB[f]])
            cur = Cur(pss, nameB)
            bi = 0
            for dc in range(KC):
                bk, bkB = alloc()
                for f in range(NF):
                    w, wB = cur.blk(bi)
                    A("pe", "matmul", bk[:, 0:T], lhsT=w, rhs=gT[:, f, 0:T], start=(f == 0), stop=(f == NF - 1),
                      reads=[wB, gB[f]], writes=[bkB])
                    if bi % UB == UB - 1:
                        ring_release()
                    bi += 1
                if store is None:
                    for (c0, c1, r) in segs:
                        A("dve", "scalar_tensor_tensor", out=xT[:, dc, c0:c1], in0=bk[:, c0:c1], scalar=sc(s, r, 2, dc),
                          in1=xT[:, dc, c0:c1], op0=ALU.mult, op1=ALU.add, reads=[bkB, xB[dc], B["scal"]],
                          writes=[xB[dc]])
                else:
                    (c0, c1, r) = segs[0]
                    ti = talloc()
                    A("dve", "scalar_tensor_tensor", out=tmp[:, ti, 0:T], in0=bk[:, 0:T], scalar=sc(s, r, 2, dc),
                      in1=xT[:, dc, 0:T], op0=ALU.mult, op1=ALU.add, reads=[bkB, xB[dc], B["scal"]],
                      writes=[tmpB[ti]])
                    out_dmas.append(DMA("act", store(dc), tmp[:, ti, 0:T], s_tmp[ti], reads=[tmpB[ti]]))

        uT = lambda c: gT[:, c, :]
        oTt = lambda t: gT[:, 8 + t, :]
        yTt = lambda dc: gT[:, 16 + dc, :]
        vnb = lambda b: gT[:, 32 + 2 * b:34 + 2 * b, :].rearrange("p a c -> p (a c)")
        vnbB = lambda b: [gB[32 + 2 * b], gB[33 + 2 * b]]
        qTv = lambda b: gT[:, 36 + 4 * b:40 + 4 * b, :].rearrange("p a c -> p (a c)")
        qTB = lambda b: [gB[36 + 4 * b + i] for i in range(4)]
        st = {"kv": 0, "blk": 0}

        def rope_ap(kind, jg, n):
            if kind == "P":
                o = vo["ropeP"] + jg * 16
            elif kind == "S":
                o = vo["ropeS"]
            else:
                o = vo["ropeH"]
            return vecs[0:n, o:o + 8], vecs[0:n, o + 8:o + 16]

        def attention(nq, qslot, tiles, c0, halo=False):
            qT_ = qTv(qslot)
            for h in range(NKV):
                pset = h % 2
                pTb = B["pT%d" % pset] if nq == 128 else B["pTS"]
                pviews = []
                for ti_, (kTa, kTb_, vDa, vDb_, nk, mask) in enumerate(tiles):
                    bk, bkB = alloc()
                    A("pe", "matmul", bk[0:nk, 0:4 * nq], lhsT=kTa[0:64, h * nk:(h + 1) * nk],
                      rhs=qT_[0:64, h * 4 * nq:(h + 1) * 4 * nq], start=True, stop=True, reads=[kTb_] + qTB(qslot),
                      writes=[bkB])
                    if nq == 128:
                        pv = pT[:, pset, ti_, :]
                    else:
                        pv = pTS[:, ti_, :]
                    pviews.append(pv)
                    s3 = bk[:, 0:4 * nq].rearrange("p (g q) -> p g q", g=4)
                    p3 = pv[:, 0:4 * nq].rearrange("p (g q) -> p g q", g=4)
                    kw = {}
                    if mask == "full":
                        regs = [(0, nk, 0, nq)]
                    elif mask == "prev":
                        regs = [(0, 64, 0, 64), (64, 128, 0, 128)]
                    else:
                        regs = [(0, 64, 0, 128), (64, 128, 64, 128)]
                    for (p0, p1, q0, q1) in regs:
                        bias = vecs[p0:p1, vo["halo"]:vo["halo"] + 1] if (halo and mask == "prev") else 0.0
                        A("act", "activation", out=p3[p0:p1, :, q0:q1], in_=s3[p0:p1, :, q0:q1], func=AF.Exp,
                          scale=HD ** -0.5, bias=bias, reads=[bkB, B["vecs"]], writes=[pTb])
                bo, boB = alloc()
                bd, bdB = alloc()
                nt_ = len(tiles)
                for ti_, (kTa, kTb_, vDa, vDb_, nk, mask) in enumerate(tiles):
                    A("pe", "matmul", bo[:, 0:4 * nq], lhsT=vDa[0:nk, h * 128:(h + 1) * 128],
                      rhs=pviews[ti_][0:nk, 0:4 * nq], start=(ti_ == 0), stop=(ti_ == nt_ - 1), reads=[vDb_, pTb],
                      writes=[boB])
                for ti_, (kTa, kTb_, vDa, vDb_, nk, mask) in enumerate(tiles):
                    A("pe", "matmul", bd[:, 0:4 * nq], lhsT=onesb[0:nk, :], rhs=pviews[ti_][0:nk, 0:4 * nq],
                      start=(ti_ == 0), stop=(ti_ == nt_ - 1), reads=[B["onesb"], pTb], writes=[bdB])
                ti = talloc()
                r3 = tmp[:, ti, 0:4 * nq].rearrange("p (g q) -> p g q", g=4)
                for g in range(4):
                    A("act", "activation", out=tmp[:, ti, g * nq:(g + 1) * nq], in_=bd[:, g * nq:(g + 1) * nq],
                      func=AF.Identity, scale=1.0, bias=esink[:, 4 * h + g:4 * h + g + 1], reads=[bdB, B["esink"]],
                      writes=[tmpB[ti]])
                A("dve", "reciprocal", out=tmp[:, ti, 0:4 * nq], in_=tmp[:, ti, 0:4 * nq], reads=[tmpB[ti]],
                  writes=[tmpB[ti]])
                o3 = bo[:, 0:4 * nq].rearrange("p (g q) -> p g q", g=4)
                for par in range(2):
                    p0 = par * 64
                    A("dve", "tensor_tensor", out=gT[p0:p0 + 64, 8 + 2 * h:8 + 2 * h + 2, c0:c0 + nq],
                      in0=o3[p0:p0 + 64, par::2, :], in1=r3[p0:p0 + 64, par::2, :], op=ALU.mult,
                      reads=[boB, tmpB[ti]], writes=[gB[8 + 2 * h], gB[8 + 2 * h + 1]])

        def spatial_gate(n, vslot, c0, is_s):
            for half in range(2):
                bk, bkB = alloc()
                for gg in range(4):
                    g = half * 4 + gg
                    A("pe", "matmul", bk[:, gg * n:(gg + 1) * n], lhsT=vnb(vslot)[0:n, g * 128:(g + 1) * 128],
                      rhs=wsTb[0:n, g, 0:n], start=True, stop=True, reads=vnbB(vslot) + [B["wsTb"]], writes=[bkB])
                for gg in range(4):
                    g = half * 4 + gg
                    ti = talloc()
                    Cv = CgS[:, g, 0:n] if is_s else Cg[:, g, 0:n]
                    A("dve", "scalar_tensor_tensor", out=tmp[:, ti, 0:n], in0=bk[:, gg * n:(gg + 1) * n],
                      scalar=lng(g), in1=Cv, op0=ALU.mult, op1=ALU.add,
                      reads=[bkB, B["vecs"], B["Cg"], B["CgS"]], writes=[tmpB[ti]])
                    A("dve", "tensor_tensor", out=gT[:, g, c0:c0 + n], in0=gT[:, g, c0:c0 + n], in1=tmp[:, ti, 0:n],
                      op=ALU.mult, reads=[gB[g], tmpB[ti]], writes=[gB[g]])

        def mixer2(pss, T, Tm, segs, mseg, blocks, first_p, last):
            norm(1, T, segs)
            cur = Cur(pss, "MIXA")
            full_blocks = [b for b in blocks if b["full"]]
            for i, b in enumerate(full_blocks):
                b["gi"] = i
                b["vslot"] = i % 2
            for cg in range(2):
                groups = [full_blocks[i:i + 3] for i in range(0, len(full_blocks), 3)]
                for grp in groups:
                    bks = [alloc() for _ in grp]
                    for kc in range(KC):
                        w, wB = cur.blk(cg * 64 + kc * 4, 4)
                        for b, (bk, bkB) in zip(grp, bks):
                            A("pe", "matmul", bk[0:b["n"], :], lhsT=hT[:, kc, b["c0"]:b["c0"] + b["n"]], rhs=w,
                              start=(kc == 0), stop=(kc == KC - 1), reads=[wB, hB[kc]], writes=[bkB])
                    for b, (bk, bkB) in zip(grp, bks):
                        n, gi = b["n"], b["gi"]
                        A("act", "activation", out=gv[0:n, gi, cg * 512:(cg + 1) * 512], in_=bk[0:n, :], func=AF.Gelu,
                          reads=[bkB], writes=[gvB[gi]])
                ring_release()
                ring_release()
            for cg in range(2, 4):
                bl = full_blocks
                groups = [bl[i:i + 3] for i in range(0, len(bl), 3)]
                for grp in groups:
                    bks = [alloc() for _ in grp]
                    for kc in range(KC):
                        w, wB = cur.blk(cg * 64 + kc * 4, 4)
                        for b, (bk, bkB) in zip(grp, bks):
                            A("pe", "matmul", bk[0:b["n"], :], lhsT=hT[:, kc, b["c0"]:b["c0"] + b["n"]], rhs=w,
                              start=(kc == 0), stop=(kc == KC - 1), reads=[wB, hB[kc]], writes=[bkB])
                    for b, bkp in zip(grp, bks):
                        b["cg%d" % cg] = bkp
                        if cg < 4:
                            n, gi = b["n"], b["gi"]
                            A("act", "activation", out=qraw[0:n, gi, (cg - 2) * 512:(cg - 1) * 512], in_=bkp[0][0:n, :],
                              func=AF.Copy, reads=[bkp[1]], writes=qrawBs(gi))
                ring_release()
                ring_release()
            return full_blocks

        qraw_all = gT[:, 16:32, :].rearrange("p a c -> p (a c)").bitcast(F32)

        class _QR:
            def __getitem__(self, key):
                p, gi, c = key
                if isinstance(c, slice) and c.start is None:
                    return qraw_all[p, gi * 1024:(gi + 1) * 1024]
                return qraw_all[p, gi * 1024 + c.start:gi * 1024 + c.stop]
        qraw = _QR()

        qrawBs = lambda gi: [gB[16 + 4 * gi + i] for i in range(4)]

        def qk_process2(b, kslot, want_q, out_k=None, out_v=None):
            n, kind, jg = b["n"], b["kind"], b["jg"]
            bkv = b["cg4"]
            lo = 0 if want_q else 1024
            W = 1280 - lo
            nh = W // 64
            A("act", "activation", out=qkf[0:n, 0:256], in_=bkv[0][0:n, 0:256], func=AF.Copy, reads=[bkv[1]],
              writes=[B["qkf"]])
            vd3 = vaDr[0:n, kslot, :].rearrange("p (h d) -> p h d", h=4)
            va3 = bkv[0][0:n, 256:512].rearrange("p (h d) -> p h d", h=4)
            A("act", "activation", out=vd3[:, :, 0:64], in_=va3, func=AF.Copy, reads=[bkv[1]], writes=[vaDB[kslot]])
            A("act", "activation", out=vd3[:, :, 64:128], in_=va3, func=AF.Copy, reads=[bkv[1]], writes=[vaDB[kslot]])
            if out_v is not None:
                A("act", "activation", out=vaf[0:n, :], in_=bkv[0][0:n, 256:512], func=AF.Copy, reads=[bkv[1]],
                  writes=[B["vaf"]])
                out_dmas.append(DMA("act", out_v, vaf[0:n, :], s_misc[7], reads=[B["vaf"]]))

            def src(c0_, c1_):
                return None
            if want_q:
                gi = b["gi"]
                A("act", "activation", out=sqn[0:n, 0:1024], in_=qraw[0:n, gi, :], func=AF.Square,
                  reads=qrawBs(gi), writes=[B["sqn"]])
            A("act", "activation", out=sqn[0:n, 1024:1280], in_=qkf[0:n, 0:256], func=AF.Square,
              reads=[B["qkf"]], writes=[B["sqn"]])
            A("dve", "tensor_reduce", out=small[0:n, 0:nh], in_=sqn[0:n, lo:1280].rearrange("p (h d) -> p h d", d=64),
              axis=AX.X, op=ALU.add, reads=[B["sqn"]], writes=[B["small"]])
            A("act", "activation", out=small[0:n, 0:nh], in_=small[0:n, 0:nh], func=AF.Sqrt, scale=1.0 / HD,
              bias=epsc[0:n, 0:1], reads=[B["small"], B["epsc"]], writes=[B["small"]])
            A("dve", "reciprocal", out=small[0:n, 0:nh], in_=small[0:n, 0:nh], reads=[B["small"]],
              writes=[B["small"]])
            if want_q:
                gi = b["gi"]
                A("dve", "tensor_tensor", out=sqn[0:n, 0:1024].rearrange("p (h d) -> p h d", d=64),
                  in0=qraw[0:n, gi, :].rearrange("p (h d) -> p h d", d=64),
                  in1=small[0:n, 0:16].unsqueeze(2).to_broadcast([n, 16, 64]), op=ALU.mult,
                  reads=qrawBs(gi) + [B["small"]], writes=[B["sqn"]])
                A("pool", "tensor_tensor", out=sqn[0:n, 0:1024].rearrange("p (h d) -> p h d", d=64),
                  in0=sqn[0:n, 0:1024].rearrange("p (h d) -> p h d", d=64),
                  in1=vecs[0:n, vo["gq"]:vo["gq"] + 64].unsqueeze(1).to_broadcast([n, 16, 64]), op=ALU.mult,
                  reads=[B["sqn"], B["vecs"]], writes=[B["sqn"]])
            A("dve", "tensor_tensor", out=sqn[0:n, 1024:1280].rearrange("p (h d) -> p h d", d=64),
              in0=qkf[0:n, 0:256].rearrange("p (h d) -> p h d", d=64),
              in1=small[0:n, nh - 4:nh].unsqueeze(2).to_broadcast([n, 4, 64]), op=ALU.mult,
              reads=[B["qkf"], B["small"]], writes=[B["sqn"]])
            A("pool", "tensor_tensor", out=sqn[0:n, 1024:1280].rearrange("p (h d) -> p h d", d=64),
              in0=sqn[0:n, 1024:1280].rearrange("p (h d) -> p h d", d=64),
              in1=vecs[0:n, vo["gk"]:vo["gk"] + 64].unsqueeze(1).to_broadcast([n, 4, 64]), op=ALU.mult,
              reads=[B["sqn"], B["vecs"]], writes=[B["sqn"]])
            if not want_q:
                stop_if(262)
            X = sqn[0:n, lo:1280].rearrange("p (h d) -> p h d", d=64)
            x1, x2 = X[:, :, 0:8], X[:, :, 8:16]
            cosA, sinA = rope_ap(kind, jg, n)
            cb = cosA.unsqueeze(1).to_broadcast([n, nh, 8])
            sbb = sinA.unsqueeze(1).to_broadcast([n, nh, 8])
            rt = lambda i: ropet[0:n, i, 0:nh * 8].rearrange("p (h d) -> p h d", d=8)
            rd = [B["sqn"], B["vecs"]]
            A("dve", "tensor_tensor", out=rt(0), in0=x1, in1=cb, op=ALU.mult, reads=rd, writes=[B["ropet"]])
            A("dve", "tensor_tensor", out=rt(1), in0=x2, in1=sbb, op=ALU.mult, reads=rd, writes=[B["ropet"]])
            A("dve", "tensor_tensor", out=rt(2), in0=x2, in1=cb, op=ALU.mult, reads=rd, writes=[B["ropet"]])
            A("dve", "tensor_tensor", out=rt(3), in0=x1, in1=sbb, op=ALU.mult, reads=rd, writes=[B["ropet"]])
            A("dve", "tensor_tensor", out=x1, in0=rt(0), in1=rt(1), op=ALU.subtract, reads=[B["ropet"]],
              writes=[B["sqn"]])
            A("dve", "tensor_tensor", out=x2, in0=rt(2), in1=rt(3), op=ALU.add, reads=[B["ropet"]],
              writes=[B["sqn"]])
            if not want_q:
                stop_if(263)
            A("act", "activation", out=qkb[0:n, lo:1280], in_=sqn[0:n, lo:1280], func=AF.Copy, reads=[B["sqn"]],
              writes=[B["qkb"]])
            if not want_q:
                stop_if(264)
            if out_k is not None:
                out_dmas.append(DMA("act", out_k, sqn[0:n, 1024:1280], s_misc[8], reads=[B["sqn"]]))
            qslot = None
            if want_q:
                qslot = st["blk"] % 2
                st["blk"] += 1
                for half in range(2):
                    bk, bkB = alloc()
                    bkb_ = bk[:, :].bitcast(BF16)
                    for hh in range(8):
                        h = half * 8 + hh
                        A("pe", "transpose", bkb_[0:64, hh * n:(hh + 1) * n], qkb[0:n, h * 64:(h + 1) * 64],
                          identb[0:n, 0:n], reads=[B["qkb"], B["identb"]], writes=[bkB])
                    A("act", "activation", out=qTv(qslot)[0:64, half * 8 * n:(half + 1) * 8 * n],
                      in_=bkb_[0:64, 0:8 * n], func=AF.Copy, reads=[bkB], writes=qTB(qslot))
            bk, bkB = alloc()
            bkb_ = bk[:, :].bitcast(BF16)
            for h in range(NKV):
                A("pe", "transpose", bkb_[0:64, h * n:(h + 1) * n], qkb[0:n, 1024 + h * 64:1024 + (h + 1) * 64],
                  identb[0:n, 0:n], reads=[B["qkb"], B["identb"]], writes=[bkB])
            A("act", "activation", out=kTr[0:64, kslot, 0:4 * n], in_=bkb_[0:64, 0:4 * n], func=AF.Copy, reads=[bkB],
              writes=[kTB[kslot]])
            return qslot

        def v_process2(b):
            n, gi, vslot = b["n"], b["gi"], b["vslot"]
            is_s = b["kind"] == "S"
            A("dve", "bn_stats", out=small[0:n, 32:38], in_=gv[0:n, gi, 0:512], reads=[gvB[gi]], writes=[B["ss"]])
            A("dve", "bn_stats", out=small[0:n, 38:44], in_=gv[0:n, gi, 512:1024], reads=[gvB[gi]], writes=[B["ss"]])
            A("dve", "bn_aggr", out=small[0:n, 44:46], in_=small[0:n, 32:44], reads=[B["ss"]], writes=[B["ss"]])
            A("act", "activation", out=small[0:n, 46:47], in_=small[0:n, 45:46], func=AF.Sqrt, scale=1.0,
              bias=epsc[0:n, 0:1], reads=[B["ss"], B["epsc"]], writes=[B["ss"]])
            A("dve", "reciprocal", out=small[0:n, 46:47], in_=small[0:n, 46:47], reads=[B["ss"]], writes=[B["ss"]])
            if is_s:
                A("dve", "tensor_scalar", out=gv[0:n, gi, :], in0=gv[0:n, gi, :], scalar1=small[0:n, 44:45],
                  scalar2=small[0:n, 46:47], op0=ALU.subtract, op1=ALU.mult, reads=[gvB[gi], B["ss"]],
                  writes=[gvB[gi]])
                A("dve", "tensor_copy", out=vnb(vslot)[0:n, :], in_=gv[0:n, gi, :], reads=[gvB[gi]],
                  writes=vnbB(vslot))
                bk, bkB = alloc()
                for g in range(8):
                    A("pe", "matmul", bk[:, g * n:(g + 1) * n], lhsT=gv[0:n, gi, g * 128:(g + 1) * 128],
                      rhs=identf[0:n, 0:n], start=True, stop=True, reads=[gvB[gi], B["identf"]], writes=[bkB])
                for g in range(8):
                    A("act", "activation", out=vnTs[:, g, :], in_=bk[:, g * n:(g + 1) * n], func=AF.Identity,
                      scale=lng(g), bias=lnb(g), reads=[bkB, B["vecs"]], writes=[B["vnTs"]])
                out_dmas.append(DMA("act", vnT_d.rearrange("(g p) t -> p g t", p=128), vnTs[:], s_misc[9],
                                    reads=[B["vnTs"]]))
            else:
                A("dve", "scalar_tensor_tensor", out=small[0:n, 47:48], in0=small[0:n, 44:45], scalar=-1.0,
                  in1=small[0:n, 46:47], op0=ALU.mult, op1=ALU.mult, reads=[B["ss"]], writes=[B["ss"]])
                A("act", "activation", out=vnb(vslot)[0:n, :], in_=gv[0:n, gi, :], func=AF.Identity,
                  scale=small[0:n, 46:47], bias=small[0:n, 47:48], reads=[gvB[gi], B["ss"]], writes=vnbB(vslot))

        def mixer_rest(pss, T, Tm, mseg, blocks, first_p, last):
            full_blocks = [b for b in blocks if b["full"]]
            cur = Cur(pss, "MIXU")
            for c in range(8):
                bk, bkB = alloc()
                for kc in range(KC):
                    w, wB = cur.blk(c * KC + kc)
                    A("pe", "matmul", bk[:, 0:T], lhsT=w, rhs=hT[:, kc, 0:T], start=(kc == 0), stop=(kc == KC - 1),
                      reads=[wB, hB[kc]], writes=[bkB])
                    if (c * KC + kc) % UB == UB - 1:
                        ring_release()
                A("act", "activation", out=gT[:, c, 0:T], in_=bk[:, 0:T], func=AF.Gelu, reads=[bkB], writes=[gB[c]])
            curk = Cur(pss, "MIXK")
            def front(b):
                bkp = alloc()
                for kc in range(KC):
                    w, wB = curk.blk(kc * 4, 4)
                    A("pe", "matmul", bkp[0][0:b["n"], :], lhsT=hT[:, kc, b["c0"]:b["c0"] + b["n"]], rhs=w,
                      start=(kc == 0), stop=(kc == KC - 1), reads=[wB, hB[kc]], writes=[bkp[1]])
                b["cg4"] = bkp
                if b is blocks[-1]:
                    ring_release()
                    ring_release()
                ks = st["kv"] % 3
                st["kv"] += 1
                b["kslot"] = ks
                if b["kind"] == "H":
                    qk_process2(b, ks, True)
                    st["prev_k"] = ks
                    return False
                n = b["n"]
                v_process2(b)
                spatial_gate(n, b["vslot"], b["c0"], b["kind"] == "S")
                if b["kind"] == "S":
                    b["qs"] = qk_process2(b, ks, True, out_k=ks_d[96:128, :], out_v=vs_d[96:128, :])
                    b["tiles"] = [(kTc[:, :], B["kTc"], vaDc[:, :], B["vaDc"], 128, "full"),
                                  (kTr[:, ks, :], kTB[ks], vaDr[:, ks, :], vaDB[ks], 32, "full")]
                else:
                    is_last = last and b is blocks[-1]
                    b["qs"] = qk_process2(b, ks, True, out_k=klast_d if is_last else None,
                                          out_v=vlast_d if is_last else None)
                    pk = st["prev_k"]
                    b["tiles"] = [(kTr[:, pk, :], kTB[pk], vaDr[:, pk, :], vaDB[pk], 128, "prev"),
                                  (kTr[:, ks, :], kTB[ks], vaDr[:, ks, :], vaDB[ks], 128, "cur")]
                    st["prev_k"] = ks
                return True

            def back(b):
                if b["kind"] == "S":
                    attention(32, b["qs"], b["tiles"], b["c0"])
                else:
                    attention(128, b["qs"], b["tiles"], b["c0"], halo=(first_p and b is blocks[0]))

            pending = None
            for b in blocks:
                need = front(b)
                if pending is not None:
                    back(pending)
                    pending = None
                if need:
                    pending = b
            if pending is not None:
                back(pending)
            stop_if(26)
            (m0, m1, mr) = mseg
            cur = Cur(pss, "MIXM")
            for dc in range(KC):
                bga, bgaB = alloc()
                bpa, bpaB = alloc()
                bgb, bgbB = alloc()
                bpb, bpbB = alloc()
                base = dc * 48
                for kc in range(KC):
                    w, wB = cur.blk(base + kc)
                    A("pe", "matmul", bga[:, m0:m1], lhsT=w, rhs=hT[:, kc, m0:m1], start=(kc == 0), stop=(kc == KC - 1),
                      reads=[wB, hB[kc]], writes=[bgaB])
                for c in range(8):
                    w, wB = cur.blk(base + 16 + c)
                    A("pe", "matmul", bpa[:, m0:m1], lhsT=w, rhs=gT[:, c, m0:m1], start=(c == 0), stop=(c == 7),
                      reads=[wB, gB[c]], writes=[bpaB])
                for kc in range(KC):
                    w, wB = cur.blk(base + 24 + kc)
                    A("pe", "matmul", bgb[:, m0:m1], lhsT=w, rhs=hT[:, kc, m0:m1], start=(kc == 0), stop=(kc == KC - 1),
                      reads=[wB, hB[kc]], writes=[bgbB])
                for c in range(8):
                    w, wB = cur.blk(base + 40 + c)
                    A("pe", "matmul", bpb[:, m0:m1], lhsT=w, rhs=gT[:, 8 + c, m0:m1], start=(c == 0), stop=(c == 7),
                      reads=[wB, gB[8 + c]], writes=[bpbB])
                if dc % 2 == 1:
                    ring_release()
                    ring_release()
                    ring_release()
                t1 = talloc()
                A("act", "activation", out=tmp[:, t1, m0:m1], in_=bga[:, m0:m1], func=AF.Sigmoid, reads=[bgaB],
                  writes=[tmpB[t1]])
                A("dve", "tensor_tensor", out=tmp[:, t1, m0:m1], in0=bpa[:, m0:m1], in1=tmp[:, t1, m0:m1], op=ALU.mult,
                  reads=[bpaB, tmpB[t1]], writes=[tmpB[t1]])
                t2 = talloc()
                A("act", "activation", out=tmp[:, t2, m0:m1], in_=bgb[:, m0:m1], func=AF.Sigmoid, reads=[bgbB],
                  writes=[tmpB[t2]])
                A("dve", "tensor_tensor", out=tmp[:, t2, m0:m1], in0=bpb[:, m0:m1], in1=tmp[:, t2, m0:m1], op=ALU.mult,
                  reads=[bpbB, tmpB[t2]], writes=[tmpB[t2]])
                A("dve", "tensor_tensor", out=gT[:, 16 + dc, m0:m1], in0=tmp[:, t1, m0:m1], in1=tmp[:, t2, m0:m1],
                  op=ALU.add, reads=[tmpB[t1], tmpB[t2]], writes=[gB[16 + dc]])
            stop_if(27)
            cur = Cur(pss, "MIXO")
            for dc in range(KC):
                bk, bkB = alloc()
                for kc in range(KC):
                    w, wB = cur.blk(dc * KC + kc)
                    A("pe", "matmul", bk[:, m0:m1], lhsT=w, rhs=gT[:, 16 + kc, m0:m1], start=(kc == 0),
                      stop=(kc == KC - 1), reads=[wB, gB[16 + kc]], writes=[bkB])
                if dc % 2 == 1:
                    ring_release()
                A("dve", "scalar_tensor_tensor", out=xT[:, dc, m0:m1], in0=bk[:, m0:m1], scalar=sc(1, mr, 2, dc),
                  in1=xT[:, dc, m0:m1], op0=ALU.mult, op1=ALU.add, reads=[bkB, xB[dc], B["scal"]], writes=[xB[dc]])

        STOP = int(os.environ.get("KSTOP", "99"))
        class _Stop(Exception):
            pass
        def stop_if(k):
            if STOP == k:
                raise _Stop()
        try:
            segs0 = [(0, TH, 1), (TH, T0, 0)]
            DMA("sp", xT[:, :, 0:T0], xsh_d.rearrange("(dc p) t -> p dc t", p=128), s_misc[10], writes=xB)
            stop_if(0)
            ada(0, 0)
            ffn(0, 0, T0, segs0, "F1A", "F1B")
            stop_if(1)
            ada(0, 1)
            blocks0 = [dict(kind="S", c0=TH, n=TS, jg=0, full=True), dict(kind="H", c0=0, n=TH, jg=0, full=True)]
            mixer2(0, T0, TS, segs0, (TH, T0, 0), blocks0, False, False)
            mixer_rest(0, T0, TS, (TH, T0, 0), blocks0, False, False)
            stop_if(2)
            ada(0, 2)
            ffn(0, 2, T0, [(TH, T0, 0)], "F2A", "F2B")
            out_dmas.append(DMA("act", yTs_d.rearrange("(dc p) t -> p dc t", p=128), xT[:, :, TH:T0], s_misc[11], reads=xB))

            stop_if(3)
            segsP = [(0, TP, 1)]
            for t in range(1, NT + 1):
                t0 = (t - 1) * TP
                for dc in range(KC):
                    DMA("sp", xT[:, dc, :], xT_d[dc * 128:(dc + 1) * 128, t0:t0 + TP], s_x[dc], writes=[xB[dc]])
                ffn(t, 0, TP, segsP, "F1A", "F1B")
                blocksP = [dict(kind="P", c0=j * 128, n=128, jg=(t - 1) * 4 + j, full=True) for j in range(4)]
                mixer2(t, TP, TP, segsP, (0, TP, 1), blocksP, t == 1, t == NT)
                mixer_rest(t, TP, TP, (0, TP, 1), blocksP, t == 1, t == NT)
                ffn(t, 2, TP, segsP, "F2A", "F2B",
                    store=lambda dc, t0=t0: yT_d[dc * 128:(dc + 1) * 128, t0:t0 + TP])

        except _Stop:
            pass
        if STOP == 99:
            assert rs["released"] == NSEQ, (rs, NSEQ)
        for e in ("act", "sp"):
            P.add(e, lambda h: None, extra_deps=[i for i in out_dmas])
        P.emit(blk, s_eng)
    return nc


def _blocks(inp, NF):
    def r4(w, a, b):
        K, N = w.shape
        return w.reshape(K // 128, 128, N // 128, 128)
    w_ada = inp["w_ada"][0]
    w_in = inp["w_in"][0]
    out = []
    def ada(s):
        w = r4(w_ada[:, s * 6144:(s + 1) * 6144], 0, 0)
        return w.transpose(2, 0, 1, 3).reshape(-1, 128, 128)
    def ffa(w1, w3):
        a = r4(w1, 0, 0).transpose(2, 0, 1, 3)
        b = r4(w3, 0, 0).transpose(2, 0, 1, 3)
        return np.concatenate([a, b], axis=1).reshape(-1, 128, 128)
    def ffb(w2):
        return r4(w2, 0, 0).transpose(2, 0, 1, 3).reshape(-1, 128, 128)
    def mixa():
        cols = [np.arange(1024, 1536), np.arange(1536, 2048), np.arange(2048, 2560), np.arange(2560, 3072)]
        res = []
        for c in cols:
            w = w_in[:, c].reshape(16, 128, 4, 128).transpose(0, 2, 1, 3)
            res.append(w.reshape(-1, 128, 128))
        return np.concatenate(res, 0)
    def mixk():
        w = w_in[:, 3072:3584].reshape(16, 128, 4, 128).transpose(0, 2, 1, 3)
        return w.reshape(-1, 128, 128)
    def mixu():
        return r4(w_in[:, 0:1024], 0, 0).transpose(2, 0, 1, 3).reshape(-1, 128, 128)
    def mixm():
        ga = r4(w_in[:, 3584:5632], 0, 0).transpose(2, 0, 1, 3)
        gb = r4(w_in[:, 5632:7680], 0, 0).transpose(2, 0, 1, 3)
        pa = r4(inp["w_pa"][0], 0, 0).transpose(2, 0, 1, 3)
        pb = r4(inp["w_pb"][0], 0, 0).transpose(2, 0, 1, 3)
        return np.concatenate([ga, pa, gb, pb], axis=1).reshape(-1, 128, 128)
    def mixo():
        return r4(inp["w_o"][0], 0, 0).transpose(2, 0, 1, 3).reshape(-1, 128, 128)
    parts = [ada(0), ffa(inp["w1_ffn1"][0], inp["w3_ffn1"][0]), ffb(inp["w2_ffn1"][0]), ada(1), mixa(), mixu(), mixk(), mixm(),
             mixo(), ada(2), ffa(inp["w1_ffn2"][0], inp["w3_ffn2"][0]), ffb(inp["w2_ffn2"][0])]
    allb = np.concatenate(parts, 0)
    NB = allb.shape[0]
    assert NB % UB == 0
    u = allb.reshape(NB // UB, UB, 128, 128).transpose(0, 2, 1, 3).reshape(NB // UB, 128, UB * 128)
    return np.ascontiguousarray(u, dtype=np.float32)


def _rope_tab(pos):
    inv = ROPE_THETA ** (-np.arange(0, 16, 2, dtype=np.float32) / np.float32(16))
    ang = pos.astype(np.float32)[:, None] * inv.astype(np.float32)[None, :]
    return np.concatenate([np.cos(ang), np.sin(ang)], axis=1).astype(np.float32)


_CACHE = {}


def run(inp, NT, NF):
    inp = {k: np.asarray(v, dtype=np.float32) for k, v in inp.items()}
    key = (NT, NF)
    if key not in _CACHE:
        _CACHE[key] = build(NT, NF)
    nc = _CACHE[key]
    vo, NV = vec_layout(NT)
    wsrc = _blocks(inp, NF)
    xp = inp["x_prompt"]
    xs = inp["x_sample"]
    Bp, SEQ, _ = xp.shape
    HALF = NT * TP
    assert SEQ == 2 * HALF and Bp == 4 and xs.shape[0] == 8
    wsT = np.ascontiguousarray(inp["w_s"][0].transpose(2, 0, 1).reshape(128, 1024))
    bsrow = np.ascontiguousarray(inp["b_s"][0].reshape(1, 1024))
    in_maps = []
    for c in range(8):
        b, hf = c // 2, c % 2
        xT = np.ascontiguousarray(xp[b, hf * HALF:(hf + 1) * HALF, :].T)
        if hf == 1:
            halo = xp[b, HALF - TH:HALF, :]
        else:
            halo = np.zeros((TH, D), np.float32)
        xsh = np.ascontiguousarray(np.concatenate([halo, xs[c]], 0).T)
        vecs = np.zeros((128, NV), np.float32)
        cc = np.stack([inp["c_sample"][c], inp["c_prompt"][b]], 1)
        vecs[:, vo["c"]:vo["c"] + 32] = cc.reshape(16, 128, 2).transpose(1, 0, 2).reshape(128, 32)
        vecs[:, vo["bada"]:vo["bada"] + 144] = inp["b_ada"][0].reshape(144, 128).T
        for s, nm in enumerate(("g_ffn1", "g_mix", "g_ffn2")):
            vecs[:, vo["g"] + s * 16:vo["g"] + s * 16 + 16] = inp[nm][0].reshape(16, 128).T
        vecs[:, vo["halo"]] = 0.0 if hf == 1 else -30000.0
        vecs[:, vo["sink"]:vo["sink"] + 16] = inp["sinks"][0][None, :]
        vecs[:, vo["gq"]:vo["gq"] + 64] = inp["g_q"][0][None, :]
        vecs[:, vo["gk"]:vo["gk"] + 64] = inp["g_k"][0][None, :]
        vecs[:, vo["lng"]:vo["lng"] + 8] = inp["ln_v_g"][0].reshape(8, 128).T
        vecs[:, vo["lnb"]:vo["lnb"] + 8] = inp["ln_v_b"][0].reshape(8, 128).T
        posP = hf * HALF + np.arange(HALF)
        tabP = _rope_tab(posP).reshape(NT * 4, 128, 16).transpose(1, 0, 2).reshape(128, NT * 4 * 16)
        vecs[:, vo["ropeP"]:vo["ropeP"] + NT * 4 * 16] = tabP
        vecs[0:TS, vo["ropeS"]:vo["ropeS"] + 16] = _rope_tab(PAST_LEN + np.arange(TS))
        vecs[:, vo["ropeH"]:vo["ropeH"] + 16] = _rope_tab(np.maximum(hf * HALF - TH + np.arange(TH), 0))
        in_maps.append({
            "wsrc": wsrc, "xT": xT, "xsh": xsh, "vecs": vecs, "wsT": wsT, "bsrow": bsrow,
            "cache_k": np.ascontiguousarray(inp["cache_swa_k"][0, c].reshape(128, 256)),
            "cache_v": np.ascontiguousarray(inp["cache_swa_v"][0, c].reshape(128, 256)),
        })
    res = run_bass_kernel_spmd(nc, in_maps, core_ids=list(range(8)))
    R = res.results
    y_p = np.empty((4, SEQ, D), np.float32)
    y_s = np.empty((8, TS, D), np.float32)
    kp = np.empty((1, 4, 128, 4, 64), np.float32)
    vp = np.empty((1, 4, 128, 4, 64), np.float32)
    ks = np.empty((1, 8, 128, 4, 64), np.float32)
    vs = np.empty((1, 8, 128, 4, 64), np.float32)
    gvs = np.empty((1, 8, TS, DA), np.float32)
    for c in range(8):
        b, hf = c // 2, c % 2
        if R[c]["yT"] is None:
            continue
        y_p[b, hf * HALF:(hf + 1) * HALF, :] = R[c]["yT"].T
        y_s[c] = R[c]["yTs"].T
        if hf == 1:
            kp[0, b] = R[c]["klast"].reshape(128, 4, 64)
            vp[0, b] = R[c]["vlast"].reshape(128, 4, 64)
        ks[0, c] = R[c]["ks"].reshape(128, 4, 64)
        vs[0, c] = R[c]["vs"].reshape(128, 4, 64)
        gvs[0, c] = R[c]["vnT"].T
    return (y_p, y_s, kp, vp, ks, vs, gvs)


def kernel(**inputs):
    return run(inputs, 8, 44)
```

```python
import os
import numpy as np
from contextlib import ExitStack
import concourse.bass as bass
import concourse.mybir as mybir
from concourse.bass_utils import run_bass_kernel_spmd

F32 = mybir.dt.float32
BF16 = mybir.dt.bfloat16
ALU = mybir.AluOpType
AF = mybir.ActivationFunctionType
AX = mybir.AxisListType

D = 2048
KC = 16
DA = 1024
NHD = 16
NKV = 4
HD = 64
TP = 512
TS = 32
TH = 128
T0 = TS + TH
UB = 32
RSL = 4
CONVG = 4
EPS = 1e-6
ROPE_THETA = 500000.0
PAST_LEN = 2048
ENGS = ("pe", "act", "dve", "pool", "sp")


class Buf:
    __slots__ = ("writers", "readers")

    def __init__(self):
        self.writers = {}
        self.readers = {}


class Ins:
    __slots__ = ("eng", "fn", "deps", "needed", "sem", "val", "is_dma", "key")


class Prog:
    def __init__(self):
        self.streams = {e: [] for e in ENGS}
        self.dma_counts = {}

    def add(self, eng, fn, reads=(), writes=(), dma_sem=None, extra_deps=()):
        ins = Ins()
        ins.eng = eng
        ins.fn = fn
        ins.needed = False
        ins.is_dma = dma_sem is not None
        ins.sem = None
        ins.val = None
        if ins.is_dma:
            c = self.dma_counts.get(id(dma_sem), 0) + 16
            self.dma_counts[id(dma_sem)] = c
            ins.sem = dma_sem
            ins.val = c
            ins.key = ("dma", id(dma_sem))
        else:
            ins.key = eng
        deps = {}
        is_dma = ins.is_dma

        def dep(p):
            if p is None or p is ins:
                return
            if (not p.is_dma) and (not is_dma) and p.eng == eng == "pe":
                return
            deps[id(p)] = p

        for b in reads:
            for p in b.writers.values():
                dep(p)
        for b in writes:
            for k, p in b.readers.items():
                dep(p)
            for k, p in b.writers.items():
                dep(p)
        for p in extra_deps:
            dep(p)
        ins.deps = list(deps.values())
        for p in ins.deps:
            p.needed = True
        for b in reads:
            b.readers[ins.key] = ins
        for b in writes:
            b.writers[ins.key] = ins
        self.streams[eng].append(ins)
        return ins

    def emit(self, block, sems):
        for e in ENGS:
            c = 0
            for ins in self.streams[e]:
                if ins.is_dma:
                    continue
                if ins.needed:
                    c += 1
                    ins.sem = sems[e]
                    ins.val = c
        streams = self.streams

        def run_stream(e, h):
            waited = {}
            for ins in streams[e]:
                need = {}
                for p in ins.deps:
                    k = id(p.sem)
                    if waited.get(k, 0) >= p.val:
                        continue
                    if k not in need or need[k][1] < p.val:
                        need[k] = (p.sem, p.val)
                for k, (s, v) in need.items():
                    h.wait_ge(s, v)
                    waited[k] = v
                r = ins.fn(h)
                if r is None:
                    continue
                if ins.is_dma:
                    r.then_inc(ins.sem, 16)
                elif ins.needed:
                    r.then_inc(ins.sem, 1)

        @block.tensor
        def _(h):
            run_stream("pe", h)

        @block.scalar
        def _(h):
            run_stream("act", h)

        @block.vector
        def _(h):
            run_stream("dve", h)

        @block.gpsimd
        def _(h):
            run_stream("pool", h)

        @block.sync
        def _(h):
            run_stream("sp", h)


def sections(NF):
    secs = [("ADA0", 768), ("F1A", 32 * NF), ("F1B", 16 * NF), ("ADA1", 768), ("MIXA", 256), ("MIXU", 128), ("MIXK", 64),
            ("MIXM", 768), ("MIXO", 256), ("ADA2", 768), ("F2A", 32 * NF), ("F2B", 16 * NF)]
    off = {}
    o = 0
    for n, c in secs:
        assert c % UB == 0
        off[n] = (o // UB, c // UB)
        o += c
    return secs, off, o // UB


def vec_layout(NT):
    o = {}
    c = 0
    for n, w in [("c", 32), ("bada", 144), ("g", 48), ("halo", 1), ("sink", 16), ("gq", 64), ("gk", 64),
                 ("lng", 8), ("lnb", 8), ("ropeP", 16 * NT * 4), ("ropeS", 16), ("ropeH", 16)]:
        o[n] = c
        c += w
    return o, c


def build(NT, NF):
    NFA = max(NF, 44)
    secs, soff, NU = sections(NF)
    vo, NV = vec_layout(NT)
    nc = bass.Bass("TRN2", target_bir_lowering=False)
    dt_in = lambda n, s, t=F32: nc.dram_tensor(n, s, t, kind="ExternalInput").ap()
    dt_out = lambda n, s: nc.dram_tensor(n, s, F32, kind="ExternalOutput").ap()
    wsrc = dt_in("wsrc", [NU, 128, UB * 128])
    xT_d = dt_in("xT", [D, NT * TP])
    xsh_d = dt_in("xsh", [D, T0])
    vecs_d = dt_in("vecs", [128, NV])
    wsT_d = dt_in("wsT", [128, 8 * 128])
    bsrow_d = dt_in("bsrow", [1, 1024])
    ck_d = dt_in("cache_k", [128, 256])
    cv_d = dt_in("cache_v", [128, 256])
    wbf = nc.dram_tensor("wbf", [NU, 128, UB * 128], BF16, kind="Internal").ap()
    yT_d = dt_out("yT", [D, NT * TP])
    yTs_d = dt_out("yTs", [D, TS])
    klast_d = dt_out("klast", [128, 256])
    vlast_d = dt_out("vlast", [128, 256])
    ks_d = dt_out("ks", [128, 256])
    vs_d = dt_out("vs", [128, 256])
    vnT_d = dt_out("vnT", [DA, TS])

    with ExitStack() as es:
        def sb(name, shape, dt):
            return es.enter_context(nc.sbuf_tensor(name, shape, dt))

        def sem(name):
            return es.enter_context(nc.semaphore(name))

        P = Prog()
        xT = sb("xTs", [128, KC, TP], F32)
        hT = sb("hTs", [128, KC, TP], BF16)
        gT = sb("gTs", [128, NFA, TP], BF16)
        ring = sb("ring", [128, RSL, UB * 128], BF16)
        vecs = sb("vecs_s", [128, NV], F32)
        gv = sb("gv", [128, 4, 1024], F32)
        qkf = sb("qkf", [128, 256], F32)
        sqnT = sb("sqnT", [128, 2, 1280], F32)
        qkb = sb("qkb", [128, 1280], BF16)
        vaf = sb("vaf", [128, 256], F32)
        ropet = sb("ropet", [128, 4, 160], F32)
        kTr = sb("kTr", [128, 4, 512], BF16)
        vaDr = sb("vaDr", [128, 4, 512], BF16)
        kTc = sb("kTc", [128, 512], BF16)
        vaDc = sb("vaDc", [128, 512], BF16)
        pT = sb("pTs", [128, 2, 2, 512], BF16)
        pTS = sb("pTSs", [128, 2, 2, 128], BF16)
        NTMP = 8
        tmp = sb("tmp", [128, NTMP, TP], F32)
        sqb = sb("sqb", [128, 2, TP], BF16)
        rstd = sb("rstd", [128, TP], F32)
        small = sb("small", [128, 64], F32)
        modT = sb("modT", [128, 144, 2], F32)
        scal = sb("scal", [128, 288], F32)
        scT = sb("scT", [128, 32], BF16)
        esink = sb("esink", [128, 16], F32)
        identb = sb("identb", [128, 128], BF16)
        identf = sb("identf", [128, 128], F32)
        onesb = sb("onesb", [128, 128], BF16)
        wsTb = sb("wsTb", [128, 8, 128], BF16)
        Cg = sb("Cg", [128, 8, 128], F32)
        CgS = sb("CgS", [128, 8, 32], F32)
        vnTs = sb("vnTs", [128, 8, 32], F32)
        epsc = sb("epsc", [128, 1], F32)
        banks = [es.enter_context(nc.psum_tensor(f"bank{i}", [128, 512], F32)) for i in range(8)]
        bankB = [Buf() for _ in range(8)]
        bank_ctr = [0]

        def alloc():
            b = bank_ctr[0] % 8
            bank_ctr[0] += 1
            return banks[b], bankB[b]

        s_eng = {e: sem("s_" + e) for e in ("pe", "act", "dve", "pool", "sp")}
        s_ring = [sem(f"s_ring{i}") for i in range(RSL)]
        s_wb = [sem(f"s_wb{i}") for i in range(RSL)]
        s_ringp = [sem(f"s_ringp{i}") for i in range(RSL)]
        s_x = [sem(f"s_x{i}") for i in range(KC)]
        s_tmp = [sem(f"s_tmp{i}") for i in range(NTMP)]
        s_misc = [sem(f"s_misc{i}") for i in range(16)]
        blk = es.enter_context(nc.Block())

        xB = [Buf() for _ in range(KC)]
        hB = [Buf() for _ in range(KC)]
        gB = [Buf() for _ in range(NFA)]
        ringB = [Buf() for _ in range(RSL)]
        tmpB = [Buf() for _ in range(NTMP)]
        B = {k: Buf() for k in ("vecs", "gv0", "gv1", "gv2", "gv3", "qkf", "sqn0", "sqn1", "qkb", "kT3", "vaD3", "vaf", "ropet", "kTc", "vaDc",
                                "pT0", "pT1", "pTS0", "pTS1", "sqb0", "sqb1", "rstd", "small", "modT", "scal", "scT", "esink",
                                "identb", "identf", "onesb", "wsTb", "Cg", "CgS", "vnTs", "epsc", "kT0", "kT1", "kT2",
                                "vaD0", "vaD1", "vaD2", "wsTf", "bsrow", "ckf", "cvf", "ss", "dram_out")}
        gvB = [B["gv0"], B["gv1"], B["gv2"], B["gv3"]]
        kTB = [B["kT0"], B["kT1"], B["kT2"], B["kT3"]]
        vaDB = [B["vaD0"], B["vaD1"], B["vaD2"], B["vaD3"]]
        tmp_ctr = [0]

        def talloc():
            i = tmp_ctr[0] % NTMP
            tmp_ctr[0] += 1
            return i

        def A(eng, meth, *args, reads=(), writes=(), **kw):
            ins = P.add(eng, lambda e: getattr(e, meth)(*args, **kw), reads=reads, writes=writes)
            if os.environ.get("KDBG"):
                ins.fn.__dict__["lbl"] = (meth, [str(getattr(a, "ap", a)) + "@" + str(getattr(a, "offset", "")) for a in args],
                                          {k: (str(v.ap) + "@" + str(v.offset)) if hasattr(v, "ap") else v for k, v in kw.items()})
            return ins

        def DMA(eng, out, in_, s, reads=(), writes=(), extra=()):
            return P.add(eng, lambda e: e.dma_start(out=out, in_=in_), reads=reads, writes=writes, dma_sem=s,
                         extra_deps=extra)

        out_dmas = []

        useq = list(range(NU))
        non_ada = [u for n, c in secs if not n.startswith("ADA") for u in range(soff[n][0], soff[n][0] + soff[n][1])]
        for _ in range(NT):
            useq += non_ada
        pos_of = {}
        p_ = 0
        for u in range(NU):
            pos_of[(0, u)] = p_
            p_ += 1
        for t in range(1, NT + 1):
            for u in non_ada:
                pos_of[(t, u)] = p_
                p_ += 1
        NSEQ = len(useq)
        non_ada_set = set(non_ada)
        wbB = [Buf() for _ in range(NU)]
        rs = {"next": 0, "released": 0}

        def ring_advance():
            while rs["next"] < NSEQ and rs["next"] < rs["released"] + RSL:
                k = rs["next"]
                u = useq[k]
                sl = k % RSL
                if k < NU:
                    DMA("pool", ring[:, sl, :], wsrc[u], s_ringp[sl], writes=[ringB[sl]])
                    if u in non_ada_set:
                        DMA("sp", wbf[u], ring[:, sl, :], s_wb[sl], reads=[ringB[sl]], writes=[wbB[u]])
                else:
                    DMA("sp", ring[:, sl, :], wbf[u], s_ring[sl], reads=[wbB[u]], writes=[ringB[sl]])
                rs["next"] += 1

        def ring_release():
            rs["released"] += 1
            ring_advance()

        class Cur:
            def __init__(self, pss, sec):
                self.pss = pss
                self.u0 = soff[sec][0]
                self.i = 0
                self.cur_unit = None

            def _touch(self, bi):
                u = self.u0 + bi // UB
                k = pos_of[(self.pss, u)]
                assert k >= rs["released"], (k, rs)
                assert k < rs["next"], ("unit not loaded", k, rs)
                return k % RSL, bi % UB

            def blk(self, bi, n=1):
                sl, o = self._touch(bi)
                return ring[:, sl, o * 128:(o + n) * 128], ringB[sl]

        DMA("sp", vecs[:], vecs_d, s_misc[0], writes=[B["vecs"]])
        wsTf = gv[:, 0, :]
        bsr = gv[0:1, 1, :]
        ckf = gv[:, 2, 0:256]
        cvf = gv[:, 2, 256:512]
        DMA("sp", wsTf, wsT_d, s_misc[1], writes=[gvB[0]])
        DMA("sp", bsr, bsrow_d, s_misc[2], writes=[gvB[1]])
        DMA("sp", ckf, ck_d, s_misc[3], writes=[gvB[2]])
        DMA("sp", cvf, cv_d, s_misc[4], writes=[gvB[2]])
        A("pool", "memset", identb[:], 0.0, writes=[B["identb"]])
        A("pool", "affine_select", out=identb[:], in_=identb[:], pattern=[[-1, 128]], compare_op=ALU.not_equal,
          fill=1.0, base=0, channel_multiplier=1, reads=[B["identb"]], writes=[B["identb"]])
        A("pool", "memset", identf[:], 0.0, writes=[B["identf"]])
        A("pool", "affine_select", out=identf[:], in_=identf[:], pattern=[[-1, 128]], compare_op=ALU.not_equal,
          fill=1.0, base=0, channel_multiplier=1, reads=[B["identf"]], writes=[B["identf"]])
        A("pool", "memset", onesb[:], 1.0, writes=[B["onesb"]])
        A("pool", "memset", epsc[:], EPS, writes=[B["epsc"]])
        A("pool", "memset", pT[:], 0.0, writes=[B["pT0"], B["pT1"]])
        A("pool", "memset", gv[:, 3, :], 1.0, writes=[gvB[3]])
        ring_advance()

        A("act", "activation", out=scT[:], in_=vecs[:, vo["c"]:vo["c"] + 32], func=AF.Silu, reads=[B["vecs"]],
          writes=[B["scT"]])
        A("act", "activation", out=esink[:], in_=vecs[:, vo["sink"]:vo["sink"] + 16], func=AF.Exp, reads=[B["vecs"]],
          writes=[B["esink"]])
        wsTf3 = wsTf.rearrange("p (g i) -> p g i", g=8)
        A("dve", "memset", wsTf3[64:128, :, 0:64], 0.0, reads=[], writes=[gvB[0]])
        A("dve", "tensor_copy", out=wsTb[:], in_=wsTf3, reads=[gvB[0]], writes=[B["wsTb"]])
        onesf = gv[:, 3, 0:128]
        lnb = lambda g: vecs[:, vo["lnb"] + g:vo["lnb"] + g + 1]
        lng = lambda g: vecs[:, vo["lng"] + g:vo["lng"] + g + 1]
        for half in range(2):
            bk, bkB = alloc()
            A("pe", "matmul", bk[:, :], lhsT=onesf, rhs=wsTf[:, half * 512:(half + 1) * 512], start=True, stop=True,
              reads=[gvB[3], gvB[0]], writes=[bkB])
            bk2, bk2B = alloc()
            A("pe", "matmul", bk2[:, :], lhsT=gv[0:1, 3, 0:128], rhs=bsr[:, half * 512:(half + 1) * 512], start=True,
              stop=True, reads=[gvB[3], gvB[1]], writes=[bk2B])
            ti = talloc()
            A("act", "activation", out=tmp[:, ti, :], in_=bk2[:, :], func=AF.Copy, reads=[bk2B], writes=[tmpB[ti]])
            for gg in range(4):
                g = half * 4 + gg
                A("dve", "scalar_tensor_tensor", out=Cg[:, g, :], in0=bk[:, gg * 128:(gg + 1) * 128], scalar=lnb(g),
                  in1=tmp[:, ti, gg * 128:(gg + 1) * 128], op0=ALU.mult, op1=ALU.add,
                  reads=[bkB, tmpB[ti], B["vecs"]], writes=[B["Cg"]])
        bk, bkB = alloc()
        bk2, bk2B = alloc()
        bsr3 = bsr.rearrange("p (g i) -> p g i", g=8)
        for g in range(8):
            A("pe", "matmul", bk[:, g * 32:(g + 1) * 32], lhsT=gv[0:32, 3, 0:128], rhs=wsTf3[0:32, g, 0:32], start=True,
              stop=True, reads=[gvB[3], gvB[0]], writes=[bkB])
            A("pe", "matmul", bk2[:, g * 32:(g + 1) * 32], lhsT=gv[0:1, 3, 0:128], rhs=bsr3[:, g, 0:32], start=True,
              stop=True, reads=[gvB[3], gvB[1]], writes=[bk2B])
        ti = talloc()
        A("act", "activation", out=tmp[:, ti, 0:256], in_=bk2[:, 0:256], func=AF.Copy, reads=[bk2B], writes=[tmpB[ti]])
        for g in range(8):
            A("dve", "scalar_tensor_tensor", out=CgS[:, g, :], in0=bk[:, g * 32:(g + 1) * 32], scalar=lnb(g),
              in1=tmp[:, ti, g * 32:(g + 1) * 32], op0=ALU.mult, op1=ALU.add, reads=[bkB, tmpB[ti], B["vecs"]],
              writes=[B["CgS"]])
        A("dve", "tensor_copy", out=qkb[:, 0:256], in_=ckf, reads=[gvB[2]], writes=[B["qkb"]])
        bk, bkB = alloc()
        bkb = bk[:, :].bitcast(BF16)
        for h in range(NKV):
            A("pe", "transpose", bkb[0:64, h * 128:(h + 1) * 128], qkb[:, h * 64:(h + 1) * 64], identb[:],
              reads=[B["qkb"], B["identb"]], writes=[bkB])
        A("dve", "tensor_copy", out=kTc[0:64, :], in_=bkb[0:64, 0:512], reads=[bkB], writes=[B["kTc"]])
        vaDc3 = vaDc[:, :].rearrange("p (h d) -> p h d", h=4)
        cv3 = cvf.rearrange("p (h d) -> p h d", h=4)
        A("dve", "tensor_copy", out=vaDc3[:, :, 0:64], in_=cv3, reads=[gvB[2]], writes=[B["vaDc"]])
        A("dve", "tensor_copy", out=vaDc3[:, :, 64:128], in_=cv3, reads=[gvB[2]], writes=[B["vaDc"]])
        out_dmas.append(DMA("act", ks_d[0:96, :], ck_d[32:128, :], s_misc[5]))
        out_dmas.append(DMA("act", vs_d[0:96, :], cv_d[32:128, :], s_misc[6]))

        def sc(s, r, kind, dc):
            c = ((s * 2 + r) * 3 + kind) * 16 + dc
            return scal[:, c:c + 1]

        def ada(pss, s):
            cur = Cur(pss, "ADA%d" % s)
            bk, bkB = alloc()
            for ci in range(48):
                for kc in range(KC):
                    bi = ci * KC + kc
                    w, wB = cur.blk(bi)
                    A("pe", "matmul", bk[:, 2 * ci:2 * ci + 2], lhsT=w, rhs=scT[:, 2 * kc:2 * kc + 2], start=(kc == 0),
                      stop=(kc == KC - 1), reads=[wB, B["scT"]], writes=[bkB])
                    if bi % UB == UB - 1:
                        ring_release()
            bo = vo["bada"] + s * 48
            A("dve", "tensor_tensor", out=modT[:, s * 48:(s + 1) * 48, :],
              in0=bk[:, 0:96].rearrange("p (c r) -> p c r", r=2),
              in1=vecs[:, bo:bo + 48].unsqueeze(2).to_broadcast([128, 48, 2]), op=ALU.add,
              reads=[bkB, B["vecs"]], writes=[B["modT"]])
            for r in range(2):
                base = ((s * 2 + r) * 3) * 16
                A("dve", "scalar_tensor_tensor", out=scal[:, base:base + 16], in0=modT[:, s * 48 + 16:s * 48 + 32, r],
                  scalar=1.0, in1=vecs[:, vo["g"] + s * 16:vo["g"] + s * 16 + 16], op0=ALU.add, op1=ALU.mult,
                  reads=[B["modT"], B["vecs"]], writes=[B["scal"]])
                A("dve", "tensor_copy", out=scal[:, base + 16:base + 32], in_=modT[:, s * 48:s * 48 + 16, r],
                  reads=[B["modT"]], writes=[B["scal"]])
                A("dve", "tensor_scalar", out=scal[:, base + 32:base + 48], in0=modT[:, s * 48 + 32:s * 48 + 48, r],
                  scalar1=(1.0 if s == 1 else 0.5), scalar2=None, op0=ALU.mult, reads=[B["modT"]],
                  writes=[B["scal"]])

        def norm(s, T, segs):
            bk, bkB = alloc()
            for dc in range(KC):
                q = dc % 2
                A("act", "activation", out=sqb[:, q, 0:T], in_=xT[:, dc, 0:T], func=AF.Square, reads=[xB[dc]],
                  writes=[B["sqb%d" % q]])
                A("pe", "matmul", bk[:, 0:T], lhsT=onesb[:], rhs=sqb[:, q, 0:T], start=(dc == 0), stop=(dc == KC - 1),
                  reads=[B["onesb"], B["sqb%d" % q]], writes=[bkB])
            A("act", "activation", out=rstd[:, 0:T], in_=bk[:, 0:T], func=AF.Sqrt, scale=1.0 / D, bias=epsc[:, 0:1],
              reads=[bkB, B["epsc"]], writes=[B["rstd"]])
            A("dve", "reciprocal", out=rstd[:, 0:T], in_=rstd[:, 0:T], reads=[B["rstd"]], writes=[B["rstd"]])
            for dc in range(KC):
                ti = talloc()
                A("dve", "tensor_tensor", out=tmp[:, ti, 0:T], in0=xT[:, dc, 0:T], in1=rstd[:, 0:T], op=ALU.mult,
                  reads=[xB[dc], B["rstd"]], writes=[tmpB[ti]])
                for (c0, c1, r) in segs:
                    A("act", "activation", out=hT[:, dc, c0:c1], in_=tmp[:, ti, c0:c1], func=AF.Identity,
                      scale=sc(s, r, 0, dc), bias=sc(s, r, 1, dc), reads=[tmpB[ti], B["scal"]], writes=[hB[dc]])

        def ffn(pss, s, T, segs, nameA, nameB, store=None):
            norm(s, T, segs)
            cur = Cur(pss, nameA)
            for f in range(NF):
                ba, baB = alloc()
                bb, bbB = alloc()
                for wi, (bk, bkB) in enumerate(((ba, baB), (bb, bbB))):
                    for kc in range(KC):
                        w, wB = cur.blk(f * 32 + wi * 16 + kc)
                        A("pe", "matmul", bk[:, 0:T], lhsT=w, rhs=hT[:, kc, 0:T], start=(kc == 0), stop=(kc == KC - 1),
                          reads=[wB, hB[kc]], writes=[bkB])
                ring_release()
                ti = talloc()
                A("act", "activation", out=tmp[:, ti, 0:T], in_=ba[:, 0:T], func=AF.Silu, reads=[baB],
                  writes=[tmpB[ti]])
                A("dve", "tensor_tensor", out=gT[:, f, 0:T], in0=bb[:, 0:T], in1=tmp[:, ti, 0:T], op=ALU.mult,
                  reads=[bbB, tmpB[ti]], writes=[gB[f]])
            cur = Cur(pss, nameB)
            bi = 0
            for dc in range(KC):
                bk, bkB = alloc()
                for f in range(NF):
                    w, wB = cur.blk(bi)
                    A("pe", "matmul", bk[:, 0:T], lhsT=w, rhs=gT[:, f, 0:T], start=(f == 0), stop=(f == NF - 1),
                      reads=[wB, gB[f]], writes=[bkB])
                    if bi % UB == UB - 1:
                        ring_release()
                    bi += 1
                if store is None:
                    for (c0, c1, r) in segs:
                        A("dve", "scalar_tensor_tensor", out=xT[:, dc, c0:c1], in0=bk[:, c0:c1], scalar=sc(s, r, 2, dc),
                          in1=xT[:, dc, c0:c1], op0=ALU.mult, op1=ALU.add, reads=[bkB, xB[dc], B["scal"]],
                          writes=[xB[dc]])
                else:
                    (c0, c1, r) = segs[0]
                    ti = talloc()
                    A("dve", "scalar_tensor_tensor", out=tmp[:, ti, 0:T], in0=bk[:, 0:T], scalar=sc(s, r, 2, dc),
                      in1=xT[:, dc, 0:T], op0=ALU.mult, op1=ALU.add, reads=[bkB, xB[dc], B["scal"]],
                      writes=[tmpB[ti]])
                    out_dmas.append(DMA("act", store(dc), tmp[:, ti, 0:T], s_tmp[ti], reads=[tmpB[ti]]))

        uT = lambda c: gT[:, c, :]
        oTt = lambda t: gT[:, 8 + t, :]
        yTt = lambda dc: gT[:, 16 + dc, :]
        vnb = lambda b: gT[:, 32 + 2 * b:34 + 2 * b, :].rearrange("p a c -> p (a c)")
        vnbB = lambda b: [gB[32 + 2 * b], gB[33 + 2 * b]]
        qTv = lambda b: gT[:, 36 + 4 * b:40 + 4 * b, :].rearrange("p a c -> p (a c)")
        qTB = lambda b: [gB[36 + 4 * b + i] for i in range(4)]
        st = {"kv": 0, "blk": 0}

        def rope_ap(kind, jg, n):
            if kind == "P":
                o = vo["ropeP"] + jg * 16
            elif kind == "S":
                o = vo["ropeS"]
            else:
                o = vo["ropeH"]
            return vecs[0:n, o:o + 8], vecs[0:n, o + 8:o + 16]

        def attention(nq, qslot, tiles, c0, halo=False):
            qT_ = qTv(qslot)
            nt_ = len(tiles)
            state = {}

            def s1(h):
                pset = h % 2
                pTb = B["pT%d" % pset] if nq == 128 else B["pTS%d" % pset]
                pviews = []
                for ti_, (kTa, kTb_, vDa, vDb_, nk, mask) in enumerate(tiles):
                    bk, bkB = alloc()
                    A("pe", "matmul", bk[0:nk, 0:4 * nq], lhsT=kTa[0:64, h * nk:(h + 1) * nk],
                      rhs=qT_[0:64, h * 4 * nq:(h + 1) * 4 * nq], start=True, stop=True, reads=[kTb_] + qTB(qslot),
                      writes=[bkB])
                    if nq == 128:
                        pv = pT[:, pset, ti_, :]
                    else:
                        pv = pTS[:, pset, ti_, :]
                    pviews.append(pv)
                    s3 = bk[:, 0:4 * nq].rearrange("p (g q) -> p g q", g=4)
                    p3 = pv[:, 0:4 * nq].rearrange("p (g q) -> p g q", g=4)
                    kw = {}
                    if mask == "full":
                        regs = [(0, nk, 0, nq)]
                    elif mask == "prev":
                        regs = [(0, 64, 0, 64), (64, 128, 0, 128)]
                    else:
                        regs = [(0, 64, 0, 128), (64, 128, 64, 128)]
                    for (p0, p1, q0, q1) in regs:
                        bias = vecs[p0:p1, vo["halo"]:vo["halo"] + 1] if (halo and mask == "prev") else 0.0
                        A("act", "activation", out=p3[p0:p1, :, q0:q1], in_=s3[p0:p1, :, q0:q1], func=AF.Exp,
                          scale=HD ** -0.5, bias=bias, reads=[bkB, B["vecs"]], writes=[pTb])
                state[h] = (pTb, pviews)

            def s2(h):
                pTb, pviews = state[h]
                bo, boB = alloc()
                bd, bdB = alloc()
                for ti_, (kTa, kTb_, vDa, vDb_, nk, mask) in enumerate(tiles):
                    A("pe", "matmul", bo[:, 0:4 * nq], lhsT=vDa[0:nk, h * 128:(h + 1) * 128],
                      rhs=pviews[ti_][0:nk, 0:4 * nq], start=(ti_ == 0), stop=(ti_ == nt_ - 1), reads=[vDb_, pTb],
                      writes=[boB])
                for ti_, (kTa, kTb_, vDa, vDb_, nk, mask) in enumerate(tiles):
                    A("pe", "matmul", bd[:, 0:4 * nq], lhsT=onesb[0:nk, :], rhs=pviews[ti_][0:nk, 0:4 * nq],
                      start=(ti_ == 0), stop=(ti_ == nt_ - 1), reads=[B["onesb"], pTb], writes=[bdB])
                ti = talloc()
                r3 = tmp[:, ti, 0:4 * nq].rearrange("p (g q) -> p g q", g=4)
                for g in range(4):
                    A("act", "activation", out=tmp[:, ti, g * nq:(g + 1) * nq], in_=bd[:, g * nq:(g + 1) * nq],
                      func=AF.Identity, scale=1.0, bias=esink[:, 4 * h + g:4 * h + g + 1], reads=[bdB, B["esink"]],
                      writes=[tmpB[ti]])
                A("dve", "reciprocal", out=tmp[:, ti, 0:4 * nq], in_=tmp[:, ti, 0:4 * nq], reads=[tmpB[ti]],
                  writes=[tmpB[ti]])
                o3 = bo[:, 0:4 * nq].rearrange("p (g q) -> p g q", g=4)
                for par in range(2):
                    p0 = par * 64
                    A("dve", "tensor_tensor", out=gT[p0:p0 + 64, 8 + 2 * h:8 + 2 * h + 2, c0:c0 + nq],
                      in0=o3[p0:p0 + 64, par::2, :], in1=r3[p0:p0 + 64, par::2, :], op=ALU.mult,
                      reads=[boB, tmpB[ti]], writes=[gB[8 + 2 * h], gB[8 + 2 * h + 1]])

            for h in range(NKV + 1):
                if h < NKV:
                    s1(h)
                if h >= 1:
                    s2(h - 1)

        def spatial_gate(n, vslot, c0, is_s):
            for half in range(2):
                bk, bkB = alloc()
                for gg in range(4):
                    g = half * 4 + gg
                    A("pe", "matmul", bk[:, gg * n:(gg + 1) * n], lhsT=vnb(vslot)[0:n, g * 128:(g + 1) * 128],
                      rhs=wsTb[0:n, g, 0:n], start=True, stop=True, reads=vnbB(vslot) + [B["wsTb"]], writes=[bkB])
                for gg in range(4):
                    g = half * 4 + gg
                    ti = talloc()
                    Cv = CgS[:, g, 0:n] if is_s else Cg[:, g, 0:n]
                    A("dve", "scalar_tensor_tensor", out=tmp[:, ti, 0:n], in0=bk[:, gg * n:(gg + 1) * n],
                      scalar=lng(g), in1=Cv, op0=ALU.mult, op1=ALU.add,
                      reads=[bkB, B["vecs"], B["Cg"], B["CgS"]], writes=[tmpB[ti]])
                    A("dve", "tensor_tensor", out=gT[:, g, c0:c0 + n], in0=gT[:, g, c0:c0 + n], in1=tmp[:, ti, 0:n],
                      op=ALU.mult, reads=[gB[g], tmpB[ti]], writes=[gB[g]])

        def mixer2(pss, T, Tm, segs, mseg, blocks, first_p, last):
            norm(1, T, segs)
            cur = Cur(pss, "MIXA")
            full_blocks = [b for b in blocks if b["full"]]
            for i, b in enumerate(full_blocks):
                b["gi"] = i
                b["vslot"] = i % 2
            for cg in range(2):
                groups = [full_blocks[i:i + 3] for i in range(0, len(full_blocks), 3)]
                for grp in groups:
                    bks = [alloc() for _ in grp]
                    for kc in range(KC):
                        w, wB = cur.blk(cg * 64 + kc * 4, 4)
                        for b, (bk, bkB) in zip(grp, bks):
                            A("pe", "matmul", bk[0:b["n"], :], lhsT=hT[:, kc, b["c0"]:b["c0"] + b["n"]], rhs=w,
                              start=(kc == 0), stop=(kc == KC - 1), reads=[wB, hB[kc]], writes=[bkB])
                    for b, (bk, bkB) in zip(grp, bks):
                        n, gi = b["n"], b["gi"]
                        A("act", "activation", out=gv[0:n, gi, cg * 512:(cg + 1) * 512], in_=bk[0:n, :], func=AF.Gelu,
                          reads=[bkB], writes=[gvB[gi]])
                ring_release()
                ring_release()
            for cg in range(2, 4):
                bl = full_blocks
                groups = [bl[i:i + 3] for i in range(0, len(bl), 3)]
                for grp in groups:
                    bks = [alloc() for _ in grp]
                    for kc in range(KC):
                        w, wB = cur.blk(cg * 64 + kc * 4, 4)
                        for b, (bk, bkB) in zip(grp, bks):
                            A("pe", "matmul", bk[0:b["n"], :], lhsT=hT[:, kc, b["c0"]:b["c0"] + b["n"]], rhs=w,
                              start=(kc == 0), stop=(kc == KC - 1), reads=[wB, hB[kc]], writes=[bkB])
                    for b, bkp in zip(grp, bks):
                        b["cg%d" % cg] = bkp
                        if cg < 4:
                            n, gi = b["n"], b["gi"]
                            A("act", "activation", out=qraw[0:n, gi, (cg - 2) * 512:(cg - 1) * 512], in_=bkp[0][0:n, :],
                              func=AF.Copy, reads=[bkp[1]], writes=qrawBs(gi))
                ring_release()
                ring_release()
            return full_blocks

        qraw_all = gT[:, 16:32, :].rearrange("p a c -> p (a c)").bitcast(F32)

        class _QR:
            def __getitem__(self, key):
                p, gi, c = key
                if isinstance(c, slice) and c.start is None:
                    return qraw_all[p, gi * 1024:(gi + 1) * 1024]
                return qraw_all[p, gi * 1024 + c.start:gi * 1024 + c.stop]
        qraw = _QR()

        qrawBs = lambda gi: [gB[16 + 4 * gi + i] for i in range(4)]

        def qkA(b, kslot, want_q, sbi, out_v=None):
            n, kind, jg = b["n"], b["kind"], b["jg"]
            bkv = b["cg4"]
            lo = 0 if want_q else 1024
            W = 1280 - lo
            nh = W // 64
            A("act", "activation", out=qkf[0:n, 0:256], in_=bkv[0][0:n, 0:256], func=AF.Copy, reads=[bkv[1]],
              writes=[B["qkf"]])
            vd3 = vaDr[0:n, kslot, :].rearrange("p (h d) -> p h d", h=4)
            va3 = bkv[0][0:n, 256:512].rearrange("p (h d) -> p h d", h=4)
            A("act", "activation", out=vd3[:, :, 0:64], in_=va3, func=AF.Copy, reads=[bkv[1]], writes=[vaDB[kslot]])
            A("act", "activation", out=vd3[:, :, 64:128], in_=va3, func=AF.Copy, reads=[bkv[1]], writes=[vaDB[kslot]])
            if out_v is not None:
                A("act", "activation", out=vaf[0:n, :], in_=bkv[0][0:n, 256:512], func=AF.Copy, reads=[bkv[1]],
                  writes=[B["vaf"]])
                out_dmas.append(DMA("act", out_v, vaf[0:n, :], s_misc[7], reads=[B["vaf"]]))

            def src(c0_, c1_):
                return None
            if want_q:
                gi = b["gi"]
                A("act", "activation", out=sqnT[0:n, sbi, 0:1024], in_=qraw[0:n, gi, :], func=AF.Square,
                  reads=qrawBs(gi), writes=[B["sqn%d" % sbi]])
            A("act", "activation", out=sqnT[0:n, sbi, 1024:1280], in_=qkf[0:n, 0:256], func=AF.Square,
              reads=[B["qkf"]], writes=[B["sqn%d" % sbi]])
            A("dve", "tensor_reduce", out=small[0:n, 0:nh], in_=sqnT[0:n, sbi, lo:1280].rearrange("p (h d) -> p h d", d=64),
              axis=AX.X, op=ALU.add, reads=[B["sqn%d" % sbi]], writes=[B["small"]])
            A("act", "activation", out=small[0:n, 0:nh], in_=small[0:n, 0:nh], func=AF.Sqrt, scale=1.0 / HD,
              bias=epsc[0:n, 0:1], reads=[B["small"], B["epsc"]], writes=[B["small"]])
            A("dve", "reciprocal", out=small[0:n, 0:nh], in_=small[0:n, 0:nh], reads=[B["small"]],
              writes=[B["small"]])
            if want_q:
                gi = b["gi"]
                A("dve", "tensor_tensor", out=sqnT[0:n, sbi, 0:1024].rearrange("p (h d) -> p h d", d=64),
                  in0=qraw[0:n, gi, :].rearrange("p (h d) -> p h d", d=64),
                  in1=small[0:n, 0:16].unsqueeze(2).to_broadcast([n, 16, 64]), op=ALU.mult,
                  reads=qrawBs(gi) + [B["small"]], writes=[B["sqn%d" % sbi]])
                A("pool", "tensor_tensor", out=sqnT[0:n, sbi, 0:1024].rearrange("p (h d) -> p h d", d=64),
                  in0=sqnT[0:n, sbi, 0:1024].rearrange("p (h d) -> p h d", d=64),
                  in1=vecs[0:n, vo["gq"]:vo["gq"] + 64].unsqueeze(1).to_broadcast([n, 16, 64]), op=ALU.mult,
                  reads=[B["sqn%d" % sbi], B["vecs"]], writes=[B["sqn%d" % sbi]])
            A("dve", "tensor_tensor", out=sqnT[0:n, sbi, 1024:1280].rearrange("p (h d) -> p h d", d=64),
              in0=qkf[0:n, 0:256].rearrange("p (h d) -> p h d", d=64),
              in1=small[0:n, nh - 4:nh].unsqueeze(2).to_broadcast([n, 4, 64]), op=ALU.mult,
              reads=[B["qkf"], B["small"]], writes=[B["sqn%d" % sbi]])
            A("pool", "tensor_tensor", out=sqnT[0:n, sbi, 1024:1280].rearrange("p (h d) -> p h d", d=64),
              in0=sqnT[0:n, sbi, 1024:1280].rearrange("p (h d) -> p h d", d=64),
              in1=vecs[0:n, vo["gk"]:vo["gk"] + 64].unsqueeze(1).to_broadcast([n, 4, 64]), op=ALU.mult,
              reads=[B["sqn%d" % sbi], B["vecs"]], writes=[B["sqn%d" % sbi]])

        def qkB(b, kslot, want_q, sbi, out_k=None):
            n, kind, jg = b["n"], b["kind"], b["jg"]
            lo = 0 if want_q else 1024
            nh = (1280 - lo) // 64
            X = sqnT[0:n, sbi, lo:1280].rearrange("p (h d) -> p h d", d=64)
            x1, x2 = X[:, :, 0:8], X[:, :, 8:16]
            cosA, sinA = rope_ap(kind, jg, n)
            cb = cosA.unsqueeze(1).to_broadcast([n, nh, 8])
            sbb = sinA.unsqueeze(1).to_broadcast([n, nh, 8])
            rt = lambda i: ropet[0:n, i, 0:nh * 8].rearrange("p (h d) -> p h d", d=8)
            rd = [B["sqn%d" % sbi], B["vecs"]]
            A("dve", "tensor_tensor", out=rt(0), in0=x1, in1=cb, op=ALU.mult, reads=rd, writes=[B["ropet"]])
            A("dve", "tensor_tensor", out=rt(1), in0=x2, in1=sbb, op=ALU.mult, reads=rd, writes=[B["ropet"]])
            A("dve", "tensor_tensor", out=rt(2), in0=x2, in1=cb, op=ALU.mult, reads=rd, writes=[B["ropet"]])
            A("dve", "tensor_tensor", out=rt(3), in0=x1, in1=sbb, op=ALU.mult, reads=rd, writes=[B["ropet"]])
            A("dve", "tensor_tensor", out=x1, in0=rt(0), in1=rt(1), op=ALU.subtract, reads=[B["ropet"]],
              writes=[B["sqn%d" % sbi]])
            A("dve", "tensor_tensor", out=x2, in0=rt(2), in1=rt(3), op=ALU.add, reads=[B["ropet"]],
              writes=[B["sqn%d" % sbi]])
            A("act", "activation", out=qkb[0:n, lo:1280], in_=sqnT[0:n, sbi, lo:1280], func=AF.Copy, reads=[B["sqn%d" % sbi]],
              writes=[B["qkb"]])
            if out_k is not None:
                out_dmas.append(DMA("act", out_k, sqnT[0:n, sbi, 1024:1280], s_misc[8], reads=[B["sqn%d" % sbi]]))
            qslot = None
            if want_q:
                qslot = st["blk"] % 2
                st["blk"] += 1
                for half in range(2):
                    bk, bkB = alloc()
                    bkb_ = bk[:, :].bitcast(BF16)
                    for hh in range(8):
                        h = half * 8 + hh
                        A("pe", "transpose", bkb_[0:64, hh * n:(hh + 1) * n], qkb[0:n, h * 64:(h + 1) * 64],
                          identb[0:n, 0:n], reads=[B["qkb"], B["identb"]], writes=[bkB])
                    A("act", "activation", out=qTv(qslot)[0:64, half * 8 * n:(half + 1) * 8 * n],
                      in_=bkb_[0:64, 0:8 * n], func=AF.Copy, reads=[bkB], writes=qTB(qslot))
            bk, bkB = alloc()
            bkb_ = bk[:, :].bitcast(BF16)
            for h in range(NKV):
                A("pe", "transpose", bkb_[0:64, h * n:(h + 1) * n], qkb[0:n, 1024 + h * 64:1024 + (h + 1) * 64],
                  identb[0:n, 0:n], reads=[B["qkb"], B["identb"]], writes=[bkB])
            A("act", "activation", out=kTr[0:64, kslot, 0:4 * n], in_=bkb_[0:64, 0:4 * n], func=AF.Copy, reads=[bkB],
              writes=[kTB[kslot]])
            return qslot

        def v_process2(b):
            n, gi, vslot = b["n"], b["gi"], b["vslot"]
            is_s = b["kind"] == "S"
            A("dve", "bn_stats", out=small[0:n, 32:38], in_=gv[0:n, gi, 0:512], reads=[gvB[gi]], writes=[B["ss"]])
            A("dve", "bn_stats", out=small[0:n, 38:44], in_=gv[0:n, gi, 512:1024], reads=[gvB[gi]], writes=[B["ss"]])
            A("dve", "bn_aggr", out=small[0:n, 44:46], in_=small[0:n, 32:44], reads=[B["ss"]], writes=[B["ss"]])
            A("act", "activation", out=small[0:n, 46:47], in_=small[0:n, 45:46], func=AF.Sqrt, scale=1.0,
              bias=epsc[0:n, 0:1], reads=[B["ss"], B["epsc"]], writes=[B["ss"]])
            A("dve", "reciprocal", out=small[0:n, 46:47], in_=small[0:n, 46:47], reads=[B["ss"]], writes=[B["ss"]])
            if is_s:
                A("dve", "tensor_scalar", out=gv[0:n, gi, :], in0=gv[0:n, gi, :], scalar1=small[0:n, 44:45],
                  scalar2=small[0:n, 46:47], op0=ALU.subtract, op1=ALU.mult, reads=[gvB[gi], B["ss"]],
                  writes=[gvB[gi]])
                A("dve", "tensor_copy", out=vnb(vslot)[0:n, :], in_=gv[0:n, gi, :], reads=[gvB[gi]],
                  writes=vnbB(vslot))
                bk, bkB = alloc()
                for g in range(8):
                    A("pe", "matmul", bk[:, g * n:(g + 1) * n], lhsT=gv[0:n, gi, g * 128:(g + 1) * 128],
                      rhs=identf[0:n, 0:n], start=True, stop=True, reads=[gvB[gi], B["identf"]], writes=[bkB])
                for g in range(8):
                    A("act", "activation", out=vnTs[:, g, :], in_=bk[:, g * n:(g + 1) * n], func=AF.Identity,
                      scale=lng(g), bias=lnb(g), reads=[bkB, B["vecs"]], writes=[B["vnTs"]])
                out_dmas.append(DMA("act", vnT_d.rearrange("(g p) t -> p g t", p=128), vnTs[:], s_misc[9],
                                    reads=[B["vnTs"]]))
            else:
                A("dve", "scalar_tensor_tensor", out=small[0:n, 47:48], in0=small[0:n, 44:45], scalar=-1.0,
                  in1=small[0:n, 46:47], op0=ALU.mult, op1=ALU.mult, reads=[B["ss"]], writes=[B["ss"]])
                A("act", "activation", out=vnb(vslot)[0:n, :], in_=gv[0:n, gi, :], func=AF.Identity,
                  scale=small[0:n, 46:47], bias=small[0:n, 47:48], reads=[gvB[gi], B["ss"]], writes=vnbB(vslot))

        def mixer_rest(pss, T, Tm, mseg, blocks, first_p, last):
            full_blocks = [b for b in blocks if b["full"]]
            cur = Cur(pss, "MIXU")
            for c in range(8):
                bk, bkB = alloc()
                for kc in range(KC):
                    w, wB = cur.blk(c * KC + kc)
                    A("pe", "matmul", bk[:, 0:T], lhsT=w, rhs=hT[:, kc, 0:T], start=(kc == 0), stop=(kc == KC - 1),
                      reads=[wB, hB[kc]], writes=[bkB])
                    if (c * KC + kc) % UB == UB - 1:
                        ring_release()
                A("act", "activation", out=gT[:, c, 0:T], in_=bk[:, 0:T], func=AF.Gelu, reads=[bkB], writes=[gB[c]])
            curk = Cur(pss, "MIXK")
            def stageA(b, i):
                bkp = alloc()
                for kc in range(KC):
                    w, wB = curk.blk(kc * 4, 4)
                    A("pe", "matmul", bkp[0][0:b["n"], :], lhsT=hT[:, kc, b["c0"]:b["c0"] + b["n"]], rhs=w,
                      start=(kc == 0), stop=(kc == KC - 1), reads=[wB, hB[kc]], writes=[bkp[1]])
                b["cg4"] = bkp
                if b is blocks[-1]:
                    ring_release()
                    ring_release()
                ks = st["kv"] % 4
                st["kv"] += 1
                b["kslot"] = ks
                b["sbi"] = i % 2
                if b["kind"] == "H":
                    qkA(b, ks, True, b["sbi"])
                    st["prev_k"] = ks
                    b["attn"] = False
                    return
                v_process2(b)
                spatial_gate(b["n"], b["vslot"], b["c0"], b["kind"] == "S")
                b["attn"] = True
                if b["kind"] == "S":
                    qkA(b, ks, True, b["sbi"], out_v=vs_d[96:128, :])
                    b["out_k"] = ks_d[96:128, :]
                    b["tiles"] = [(kTc[:, :], B["kTc"], vaDc[:, :], B["vaDc"], 128, "full"),
                                  (kTr[:, ks, :], kTB[ks], vaDr[:, ks, :], vaDB[ks], 32, "full")]
                else:
                    is_last = last and b is blocks[-1]
                    qkA(b, ks, True, b["sbi"], out_v=vlast_d if is_last else None)
                    b["out_k"] = klast_d if is_last else None
                    pk = st["prev_k"]
                    b["tiles"] = [(kTr[:, pk, :], kTB[pk], vaDr[:, pk, :], vaDB[pk], 128, "prev"),
                                  (kTr[:, ks, :], kTB[ks], vaDr[:, ks, :], vaDB[ks], 128, "cur")]
                    st["prev_k"] = ks

            def stageB(b):
                b["qs"] = qkB(b, b["kslot"], True, b["sbi"], out_k=b.get("out_k"))

            def back(b):
                if not b["attn"]:
                    return
                if b["kind"] == "S":
                    attention(32, b["qs"], b["tiles"], b["c0"])
                else:
                    attention(128, b["qs"], b["tiles"], b["c0"], halo=(first_p and b is blocks[0]))

            nb_ = len(blocks)
            for i in range(nb_ + 2):
                if i < nb_:
                    stageA(blocks[i], i)
                if 0 <= i - 1 < nb_:
                    stageB(blocks[i - 1])
                if 0 <= i - 2 < nb_:
                    back(blocks[i - 2])
            stop_if(26)
            (m0, m1, mr) = mseg
            cur = Cur(pss, "MIXM")
            for dc in range(KC):
                bga, bgaB = alloc()
                bpa, bpaB = alloc()
                bgb, bgbB = alloc()
                bpb, bpbB = alloc()
                base = dc * 48
                for kc in range(KC):
                    w, wB = cur.blk(base + kc)
                    A("pe", "matmul", bga[:, m0:m1], lhsT=w, rhs=hT[:, kc, m0:m1], start=(kc == 0), stop=(kc == KC - 1),
                      reads=[wB, hB[kc]], writes=[bgaB])
                for c in range(8):
                    w, wB = cur.blk(base + 16 + c)
                    A("pe", "matmul", bpa[:, m0:m1], lhsT=w, rhs=gT[:, c, m0:m1], start=(c == 0), stop=(c == 7),
                      reads=[wB, gB[c]], writes=[bpaB])
                for kc in range(KC):
                    w, wB = cur.blk(base + 24 + kc)
                    A("pe", "matmul", bgb[:, m0:m1], lhsT=w, rhs=hT[:, kc, m0:m1], start=(kc == 0), stop=(kc == KC - 1),
                      reads=[wB, hB[kc]], writes=[bgbB])
                for c in range(8):
                    w, wB = cur.blk(base + 40 + c)
                    A("pe", "matmul", bpb[:, m0:m1], lhsT=w, rhs=gT[:, 8 + c, m0:m1], start=(c == 0), stop=(c == 7),
                      reads=[wB, gB[8 + c]], writes=[bpbB])
                if dc % 2 == 1:
                    ring_release()
                    ring_release()
                    ring_release()
                t1 = talloc()
                A("act", "activation", out=tmp[:, t1, m0:m1], in_=bga[:, m0:m1], func=AF.Sigmoid, reads=[bgaB],
                  writes=[tmpB[t1]])
                A("dve", "tensor_tensor", out=tmp[:, t1, m0:m1], in0=bpa[:, m0:m1], in1=tmp[:, t1, m0:m1], op=ALU.mult,
                  reads=[bpaB, tmpB[t1]], writes=[tmpB[t1]])
                t2 = talloc()
                A("act", "activation", out=tmp[:, t2, m0:m1], in_=bgb[:, m0:m1], func=AF.Sigmoid, reads=[bgbB],
                  writes=[tmpB[t2]])
                A("dve", "tensor_tensor", out=tmp[:, t2, m0:m1], in0=bpb[:, m0:m1], in1=tmp[:, t2, m0:m1], op=ALU.mult,
                  reads=[bpbB, tmpB[t2]], writes=[tmpB[t2]])
                A("dve", "tensor_tensor", out=gT[:, 16 + dc, m0:m1], in0=tmp[:, t1, m0:m1], in1=tmp[:, t2, m0:m1],
                  op=ALU.add, reads=[tmpB[t1], tmpB[t2]], writes=[gB[16 + dc]])
            stop_if(27)
            cur = Cur(pss, "MIXO")
            for dc in range(KC):
                bk, bkB = alloc()
                for kc in range(KC):
                    w, wB = cur.blk(dc * KC + kc)
                    A("pe", "matmul", bk[:, m0:m1], lhsT=w, rhs=gT[:, 16 + kc, m0:m1], start=(kc == 0),
                      stop=(kc == KC - 1), reads=[wB, gB[16 + kc]], writes=[bkB])
                if dc % 2 == 1:
                    ring_release()
                A("dve", "scalar_tensor_tensor", out=xT[:, dc, m0:m1], in0=bk[:, m0:m1], scalar=sc(1, mr, 2, dc),
                  in1=xT[:, dc, m0:m1], op0=ALU.mult, op1=ALU.add, reads=[bkB, xB[dc], B["scal"]], writes=[xB[dc]])

        STOP = int(os.environ.get("KSTOP", "99"))
        class _Stop(Exception):
            pass
        def stop_if(k):
            if STOP == k:
                raise _Stop()
        try:
            segs0 = [(0, TH, 1), (TH, T0, 0)]
            DMA("sp", xT[:, :, 0:T0], xsh_d.rearrange("(dc p) t -> p dc t", p=128), s_misc[10], writes=xB)
            stop_if(0)
            ada(0, 0)
            ffn(0, 0, T0, segs0, "F1A", "F1B")
            stop_if(1)
            ada(0, 1)
            blocks0 = [dict(kind="S", c0=TH, n=TS, jg=0, full=True), dict(kind="H", c0=0, n=TH, jg=0, full=True)]
            mixer2(0, T0, TS, segs0, (TH, T0, 0), blocks0, False, False)
            mixer_rest(0, T0, TS, (TH, T0, 0), blocks0, False, False)
            stop_if(2)
            ada(0, 2)
            ffn(0, 2, T0, [(TH, T0, 0)], "F2A", "F2B")
            out_dmas.append(DMA("act", yTs_d.rearrange("(dc p) t -> p dc t", p=128), xT[:, :, TH:T0], s_misc[11], reads=xB))

            stop_if(3)
            segsP = [(0, TP, 1)]
            for t in range(1, NT + 1):
                t0 = (t - 1) * TP
                for dc in range(KC):
                    DMA("sp", xT[:, dc, :], xT_d[dc * 128:(dc + 1) * 128, t0:t0 + TP], s_x[dc], writes=[xB[dc]])
                ffn(t, 0, TP, segsP, "F1A", "F1B")
                blocksP = [dict(kind="P", c0=j * 128, n=128, jg=(t - 1) * 4 + j, full=True) for j in range(4)]
                mixer2(t, TP, TP, segsP, (0, TP, 1), blocksP, t == 1, t == NT)
                mixer_rest(t, TP, TP, (0, TP, 1), blocksP, t == 1, t == NT)
                ffn(t, 2, TP, segsP, "F2A", "F2B",
                    store=lambda dc, t0=t0: yT_d[dc * 128:(dc + 1) * 128, t0:t0 + TP])

        except _Stop:
            pass
        if STOP == 99:
            assert rs["released"] == NSEQ, (rs, NSEQ)
        for e in ("act", "sp"):
            P.add(e, lambda h: None, extra_deps=[i for i in out_dmas])
        P.emit(blk, s_eng)
    return nc


def _blocks(inp, NF):
    def r4(w, a, b):
        K, N = w.shape
        return w.reshape(K // 128, 128, N // 128, 128)
    w_ada = inp["w_ada"][0]
    w_in = inp["w_in"][0]
    out = []
    def ada(s):
        w = r4(w_ada[:, s * 6144:(s + 1) * 6144], 0, 0)
        return w.transpose(2, 0, 1, 3).reshape(-1, 128, 128)
    def ffa(w1, w3):
        a = r4(w1, 0, 0).transpose(2, 0, 1, 3)
        b = r4(w3, 0, 0).transpose(2, 0, 1, 3)
        return np.concatenate([a, b], axis=1).reshape(-1, 128, 128)
    def ffb(w2):
        return r4(w2, 0, 0).transpose(2, 0, 1, 3).reshape(-1, 128, 128)
    def mixa():
        cols = [np.arange(1024, 1536), np.arange(1536, 2048), np.arange(2048, 2560), np.arange(2560, 3072)]
        res = []
        for c in cols:
            w = w_in[:, c].reshape(16, 128, 4, 128).transpose(0, 2, 1, 3)
            res.append(w.reshape(-1, 128, 128))
        return np.concatenate(res, 0)
    def mixk():
        w = w_in[:, 3072:3584].reshape(16, 128, 4, 128).transpose(0, 2, 1, 3)
        return w.reshape(-1, 128, 128)
    def mixu():
        return r4(w_in[:, 0:1024], 0, 0).transpose(2, 0, 1, 3).reshape(-1, 128, 128)
    def mixm():
        ga = r4(w_in[:, 3584:5632], 0, 0).transpose(2, 0, 1, 3)
        gb = r4(w_in[:, 5632:7680], 0, 0).transpose(2, 0, 1, 3)
        pa = r4(inp["w_pa"][0], 0, 0).transpose(2, 0, 1, 3)
        pb = r4(inp["w_pb"][0], 0, 0).transpose(2, 0, 1, 3)
        return np.concatenate([ga, pa, gb, pb], axis=1).reshape(-1, 128, 128)
    def mixo():
        return r4(inp["w_o"][0], 0, 0).transpose(2, 0, 1, 3).reshape(-1, 128, 128)
    parts = [ada(0), ffa(inp["w1_ffn1"][0], inp["w3_ffn1"][0]), ffb(inp["w2_ffn1"][0]), ada(1), mixa(), mixu(), mixk(), mixm(),
             mixo(), ada(2), ffa(inp["w1_ffn2"][0], inp["w3_ffn2"][0]), ffb(inp["w2_ffn2"][0])]
    allb = np.concatenate(parts, 0)
    NB = allb.shape[0]
    assert NB % UB == 0
    u = allb.reshape(NB // UB, UB, 128, 128).transpose(0, 2, 1, 3).reshape(NB // UB, 128, UB * 128)
    return np.ascontiguousarray(u, dtype=np.float32)


def _rope_tab(pos):
    inv = ROPE_THETA ** (-np.arange(0, 16, 2, dtype=np.float32) / np.float32(16))
    ang = pos.astype(np.float32)[:, None] * inv.astype(np.float32)[None, :]
    return np.concatenate([np.cos(ang), np.sin(ang)], axis=1).astype(np.float32)


_CACHE = {}


def run(inp, NT, NF):
    inp = {k: np.asarray(v, dtype=np.float32) for k, v in inp.items()}
    key = (NT, NF)
    if key not in _CACHE:
        _CACHE[key] = build(NT, NF)
    nc = _CACHE[key]
    vo, NV = vec_layout(NT)
    wsrc = _blocks(inp, NF)
    xp = inp["x_prompt"]
    xs = inp["x_sample"]
    Bp, SEQ, _ = xp.shape
    HALF = NT * TP
    assert SEQ == 2 * HALF and Bp == 4 and xs.shape[0] == 8
    wsT = np.ascontiguousarray(inp["w_s"][0].transpose(2, 0, 1).reshape(128, 1024))
    bsrow = np.ascontiguousarray(inp["b_s"][0].reshape(1, 1024))
    in_maps = []
    for c in range(8):
        b, hf = c // 2, c % 2
        xT = np.ascontiguousarray(xp[b, hf * HALF:(hf + 1) * HALF, :].T)
        if hf == 1:
            halo = xp[b, HALF - TH:HALF, :]
        else:
            halo = np.zeros((TH, D), np.float32)
        xsh = np.ascontiguousarray(np.concatenate([halo, xs[c]], 0).T)
        vecs = np.zeros((128, NV), np.float32)
        cc = np.stack([inp["c_sample"][c], inp["c_prompt"][b]], 1)
        vecs[:, vo["c"]:vo["c"] + 32] = cc.reshape(16, 128, 2).transpose(1, 0, 2).reshape(128, 32)
        vecs[:, vo["bada"]:vo["bada"] + 144] = inp["b_ada"][0].reshape(144, 128).T
        for s, nm in enumerate(("g_ffn1", "g_mix", "g_ffn2")):
            vecs[:, vo["g"] + s * 16:vo["g"] + s * 16 + 16] = inp[nm][0].reshape(16, 128).T
        vecs[:, vo["halo"]] = 0.0 if hf == 1 else -30000.0
        vecs[:, vo["sink"]:vo["sink"] + 16] = inp["sinks"][0][None, :]
        vecs[:, vo["gq"]:vo["gq"] + 64] = inp["g_q"][0][None, :]
        vecs[:, vo["gk"]:vo["gk"] + 64] = inp["g_k"][0][None, :]
        vecs[:, vo["lng"]:vo["lng"] + 8] = inp["ln_v_g"][0].reshape(8, 128).T
        vecs[:, vo["lnb"]:vo["lnb"] + 8] = inp["ln_v_b"][0].reshape(8, 128).T
        posP = hf * HALF + np.arange(HALF)
        tabP = _rope_tab(posP).reshape(NT * 4, 128, 16).transpose(1, 0, 2).reshape(128, NT * 4 * 16)
        vecs[:, vo["ropeP"]:vo["ropeP"] + NT * 4 * 16] = tabP
        vecs[0:TS, vo["ropeS"]:vo["ropeS"] + 16] = _rope_tab(PAST_LEN + np.arange(TS))
        vecs[:, vo["ropeH"]:vo["ropeH"] + 16] = _rope_tab(np.maximum(hf * HALF - TH + np.arange(TH), 0))
        in_maps.append({
            "wsrc": wsrc, "xT": xT, "xsh": xsh, "vecs": vecs, "wsT": wsT, "bsrow": bsrow,
            "cache_k": np.ascontiguousarray(inp["cache_swa_k"][0, c].reshape(128, 256)),
            "cache_v": np.ascontiguousarray(inp["cache_swa_v"][0, c].reshape(128, 256)),
        })
    res = run_bass_kernel_spmd(nc, in_maps, core_ids=list(range(8)))
    R = res.results
    y_p = np.empty((4, SEQ, D), np.float32)
    y_s = np.empty((8, TS, D), np.float32)
    kp = np.empty((1, 4, 128, 4, 64), np.float32)
    vp = np.empty((1, 4, 128, 4, 64), np.float32)
    ks = np.empty((1, 8, 128, 4, 64), np.float32)
    vs = np.empty((1, 8, 128, 4, 64), np.float32)
    gvs = np.empty((1, 8, TS, DA), np.float32)
    for c in range(8):
        b, hf = c // 2, c % 2
        if R[c]["yT"] is None:
            continue
        y_p[b, hf * HALF:(hf + 1) * HALF, :] = R[c]["yT"].T
        y_s[c] = R[c]["yTs"].T
        if hf == 1:
            kp[0, b] = R[c]["klast"].reshape(128, 4, 64)
            vp[0, b] = R[c]["vlast"].reshape(128, 4, 64)
        ks[0, c] = R[c]["ks"].reshape(128, 4, 64)
        vs[0, c] = R[c]["vs"].reshape(128, 4, 64)
        gvs[0, c] = R[c]["vnT"].T
    return (y_p, y_s, kp, vp, ks, vs, gvs)


def kernel(**inputs):
    return run(inputs, 8, 44)
```

```python
import os
import numpy as np
from contextlib import ExitStack
import concourse.bass as bass
import concourse.mybir as mybir
from concourse.bass_utils import run_bass_kernel_spmd

F32 = mybir.dt.float32
BF16 = mybir.dt.bfloat16
ALU = mybir.AluOpType
AF = mybir.ActivationFunctionType
AX = mybir.AxisListType

D = 2048
KC = 16
DA = 1024
NHD = 16
NKV = 4
HD = 64
TP = 512
TS = 32
TH = 128
T0 = TS + TH
UB = 32
RSL = 4
CONVG = 4
EPS = 1e-6
ROPE_THETA = 500000.0
PAST_LEN = 2048
ENGS = ("pe", "act", "dve", "pool", "sp")


class Buf:
    __slots__ = ("writers", "readers")

    def __init__(self):
        self.writers = {}
        self.readers = {}


class Ins:
    __slots__ = ("eng", "fn", "deps", "needed", "sem", "val", "is_dma", "key")


class Prog:
    def __init__(self):
        self.streams = {e: [] for e in ENGS}
        self.dma_counts = {}

    def add(self, eng, fn, reads=(), writes=(), dma_sem=None, extra_deps=()):
        ins = Ins()
        ins.eng = eng
        ins.fn = fn
        ins.needed = False
        ins.is_dma = dma_sem is not None
        ins.sem = None
        ins.val = None
        if ins.is_dma:
            c = self.dma_counts.get(id(dma_sem), 0) + 16
            self.dma_counts[id(dma_sem)] = c
            ins.sem = dma_sem
            ins.val = c
            ins.key = ("dma", id(dma_sem))
        else:
            ins.key = eng
        deps = {}
        is_dma = ins.is_dma

        def dep(p):
            if p is None or p is ins:
                return
            if (not p.is_dma) and (not is_dma) and p.eng == eng == "pe":
                return
            deps[id(p)] = p

        for b in reads:
            for p in b.writers.values():
                dep(p)
        for b in writes:
            for k, p in b.readers.items():
                dep(p)
            for k, p in b.writers.items():
                dep(p)
        for p in extra_deps:
            dep(p)
        ins.deps = list(deps.values())
        for p in ins.deps:
            p.needed = True
        for b in reads:
            b.readers[ins.key] = ins
        for b in writes:
            b.writers[ins.key] = ins
        self.streams[eng].append(ins)
        return ins

    def emit(self, block, sems):
        for e in ENGS:
            c = 0
            for ins in self.streams[e]:
                if ins.is_dma:
                    continue
                if ins.needed:
                    c += 1
                    ins.sem = sems[e]
                    ins.val = c
        streams = self.streams

        def run_stream(e, h):
            waited = {}
            for ins in streams[e]:
                need = {}
                for p in ins.deps:
                    k = id(p.sem)
                    if waited.get(k, 0) >= p.val:
                        continue
                    if k not in need or need[k][1] < p.val:
                        need[k] = (p.sem, p.val)
                for k, (s, v) in need.items():
                    h.wait_ge(s, v)
                    waited[k] = v
                r = ins.fn(h)
                if r is None:
                    continue
                if ins.is_dma:
                    r.then_inc(ins.sem, 16)
                elif ins.needed:
                    r.then_inc(ins.sem, 1)

        @block.tensor
        def _(h):
            run_stream("pe", h)

        @block.scalar
        def _(h):
            run_stream("act", h)

        @block.vector
        def _(h):
            run_stream("dve", h)

        @block.gpsimd
        def _(h):
            run_stream("pool", h)

        @block.sync
        def _(h):
            run_stream("sp", h)


def sections(NF):
    secs = [("ADA0", 768), ("F1A", 32 * NF), ("F1B", 16 * NF), ("ADA1", 768), ("MIXA", 256), ("MIXU", 128), ("MIXK", 64),
            ("MIXM", 768), ("MIXO", 256), ("ADA2", 768), ("F2A", 32 * NF), ("F2B", 16 * NF)]
    off = {}
    o = 0
    for n, c in secs:
        assert c % UB == 0
        off[n] = (o // UB, c // UB)
        o += c
    return secs, off, o // UB


def vec_layout(NT):
    o = {}
    c = 0
    for n, w in [("c", 32), ("bada", 144), ("g", 48), ("halo", 1), ("sink", 16), ("gq", 64), ("gk", 64),
                 ("lng", 8), ("lnb", 8), ("ropeP", 16 * NT * 4), ("ropeS", 16), ("ropeH", 16)]:
        o[n] = c
        c += w
    return o, c


def build(NT, NF):
    NFA = max(NF, 44)
    secs, soff, NU = sections(NF)
    vo, NV = vec_layout(NT)
    nc = bass.Bass("TRN2", target_bir_lowering=False)
    dt_in = lambda n, s, t=F32: nc.dram_tensor(n, s, t, kind="ExternalInput").ap()
    dt_out = lambda n, s: nc.dram_tensor(n, s, F32, kind="ExternalOutput").ap()
    wsrc = dt_in("wsrc", [NU, 128, UB * 128])
    xT_d = dt_in("xT", [D, NT * TP])
    xsh_d = dt_in("xsh", [D, TH])
    xs_d = dt_in("xs", [D, TS])
    vecs_d = dt_in("vecs", [128, NV])
    wsT_d = dt_in("wsT", [128, 8 * 128])
    bsrow_d = dt_in("bsrow", [1, 1024])
    ck_d = dt_in("cache_k", [128, 256])
    cv_d = dt_in("cache_v", [128, 256])
    wbf = nc.dram_tensor("wbf", [NU, 128, UB * 128], BF16, kind="Internal").ap()
    yT_d = dt_out("yT", [D, NT * TP])
    yTs_d = dt_out("yTs", [D, TS])
    klast_d = dt_out("klast", [128, 256])
    vlast_d = dt_out("vlast", [128, 256])
    ks_d = dt_out("ks", [128, 256])
    vs_d = dt_out("vs", [128, 256])
    vnT_d = dt_out("vnT", [DA, TS])

    with ExitStack() as es:
        def sb(name, shape, dt):
            return es.enter_context(nc.sbuf_tensor(name, shape, dt))

        def sem(name):
            return es.enter_context(nc.semaphore(name))

        P = Prog()
        xT = sb("xTs", [128, KC, TP], F32)
        hT = sb("hTs", [128, KC, TP], BF16)
        gT = sb("gTs", [128, NFA, TP], BF16)
        ring = sb("ring", [128, RSL, UB * 128], BF16)
        vecs = sb("vecs_s", [128, NV], F32)
        gv = sb("gv", [128, 4, 1024], F32)
        qkf = sb("qkf", [128, 256], F32)
        sqnT = sb("sqnT", [128, 2, 1280], F32)
        qkb = sb("qkb", [128, 1280], BF16)
        vaf = sb("vaf", [128, 256], F32)
        ropet = sb("ropet", [128, 4, 160], F32)
        kTr = sb("kTr", [128, 4, 512], BF16)
        vaDr = sb("vaDr", [128, 4, 512], BF16)
        kTc = sb("kTc", [128, 512], BF16)
        vaDc = sb("vaDc", [128, 512], BF16)
        pT = sb("pTs", [128, 2, 2, 512], BF16)
        pTS = sb("pTSs", [128, 2, 2, 128], BF16)
        NTMP = 8
        tmp = sb("tmp", [128, NTMP, TP], F32)
        sqb = sb("sqb", [128, 2, TP], BF16)
        rstd = sb("rstd", [128, TP], F32)
        small = sb("small", [128, 64], F32)
        modT = sb("modT", [128, 144, 2], F32)
        scal = sb("scal", [128, 288], F32)
        scT = sb("scT", [128, 32], BF16)
        esink = sb("esink", [128, 16], F32)
        identb = sb("identb", [128, 128], BF16)
        identf = sb("identf", [128, 128], F32)
        onesb = sb("onesb", [128, 128], BF16)
        wsTb = sb("wsTb", [128, 8, 128], BF16)
        Cg = sb("Cg", [128, 8, 128], F32)
        CgS = sb("CgS", [128, 8, 32], F32)
        vnTs = sb("vnTs", [128, 8, 32], F32)
        epsc = sb("epsc", [128, 1], F32)
        banks = [es.enter_context(nc.psum_tensor(f"bank{i}", [128, 512], F32)) for i in range(8)]
        bankB = [Buf() for _ in range(8)]
        bank_ctr = [0]

        def alloc():
            b = bank_ctr[0] % 8
            bank_ctr[0] += 1
            return banks[b], bankB[b]

        s_eng = {e: sem("s_" + e) for e in ("pe", "act", "dve", "pool", "sp")}
        s_ring = [sem(f"s_ring{i}") for i in range(RSL)]
        s_wb = [sem(f"s_wb{i}") for i in range(RSL)]
        s_ringp = [sem(f"s_ringp{i}") for i in range(RSL)]
        s_x = [sem(f"s_x{i}") for i in range(KC)]
        s_tmp = [sem(f"s_tmp{i}") for i in range(NTMP)]
        s_misc = [sem(f"s_misc{i}") for i in range(16)]
        blk = es.enter_context(nc.Block())

        xB = [Buf() for _ in range(KC)]
        hB = [Buf() for _ in range(KC)]
        gB = [Buf() for _ in range(NFA)]
        ringB = [Buf() for _ in range(RSL)]
        tmpB = [Buf() for _ in range(NTMP)]
        B = {k: Buf() for k in ("vecs", "gv0", "gv1", "gv2", "gv3", "qkf", "sqn0", "sqn1", "qkb", "kT3", "vaD3", "vaf", "ropet", "kTc", "vaDc",
                                "pT0", "pT1", "pTS0", "pTS1", "sqb0", "sqb1", "rstd", "small", "modT", "scal", "scT", "esink",
                                "identb", "identf", "onesb", "wsTb", "Cg", "CgS", "vnTs", "epsc", "kT0", "kT1", "kT2",
                                "vaD0", "vaD1", "vaD2", "wsTf", "bsrow", "ckf", "cvf", "ss", "dram_out")}
        gvB = [B["gv0"], B["gv1"], B["gv2"], B["gv3"]]
        kTB = [B["kT0"], B["kT1"], B["kT2"], B["kT3"]]
        vaDB = [B["vaD0"], B["vaD1"], B["vaD2"], B["vaD3"]]
        tmp_ctr = [0]

        def talloc():
            i = tmp_ctr[0] % NTMP
            tmp_ctr[0] += 1
            return i

        def A(eng, meth, *args, reads=(), writes=(), **kw):
            ins = P.add(eng, lambda e: getattr(e, meth)(*args, **kw), reads=reads, writes=writes)
            if os.environ.get("KDBG"):
                ins.fn.__dict__["lbl"] = (meth, [str(getattr(a, "ap", a)) + "@" + str(getattr(a, "offset", "")) for a in args],
                                          {k: (str(v.ap) + "@" + str(v.offset)) if hasattr(v, "ap") else v for k, v in kw.items()})
            return ins

        def DMA(eng, out, in_, s, reads=(), writes=(), extra=()):
            return P.add(eng, lambda e: e.dma_start(out=out, in_=in_), reads=reads, writes=writes, dma_sem=s,
                         extra_deps=extra)

        out_dmas = []

        non_ada = [u for n, c in secs if not n.startswith("ADA") for u in range(soff[n][0], soff[n][0] + soff[n][1])]
        NPASS = NT + 2
        pass_secs = {0: ["ADA0", "F1A", "F1B", "ADA1", "MIXA", "MIXK"],
                     1: ["F1A", "F1B", "MIXA", "MIXU", "MIXK", "MIXM", "MIXO", "ADA2", "F2A", "F2B"]}
        for t in range(2, NPASS):
            pass_secs[t] = ["F1A", "F1B", "MIXA", "MIXU", "MIXK", "MIXM", "MIXO", "F2A", "F2B"]
        useq = []
        pos_of = {}
        for t in range(NPASS):
            for n in pass_secs[t]:
                for u in range(soff[n][0], soff[n][0] + soff[n][1]):
                    pos_of[(t, u)] = len(useq)
                    useq.append(u)
        NSEQ = len(useq)
        non_ada_set = set(non_ada)
        wbB = [Buf() for _ in range(NU)]
        rs = {"next": 0, "released": 0}
        seen = set()

        def ring_advance():
            while rs["next"] < NSEQ and rs["next"] < rs["released"] + RSL:
                k = rs["next"]
                u = useq[k]
                sl = k % RSL
                if u not in seen:
                    seen.add(u)
                    DMA("pool", ring[:, sl, :], wsrc[u], s_ringp[sl], writes=[ringB[sl]])
                    if u in non_ada_set:
                        DMA("sp", wbf[u], ring[:, sl, :], s_wb[sl], reads=[ringB[sl]], writes=[wbB[u]])
                else:
                    DMA("sp", ring[:, sl, :], wbf[u], s_ring[sl], reads=[wbB[u]], writes=[ringB[sl]])
                rs["next"] += 1

        def ring_release():
            rs["released"] += 1
            ring_advance()

        class Cur:
            def __init__(self, pss, sec):
                self.pss = pss
                self.u0 = soff[sec][0]
                self.i = 0
                self.cur_unit = None

            def _touch(self, bi):
                u = self.u0 + bi // UB
                k = pos_of[(self.pss, u)]
                assert k >= rs["released"], (k, rs)
                assert k < rs["next"], ("unit not loaded", k, rs)
                return k % RSL, bi % UB

            def blk(self, bi, n=1):
                sl, o = self._touch(bi)
                return ring[:, sl, o * 128:(o + n) * 128], ringB[sl]

        DMA("sp", vecs[:], vecs_d, s_misc[0], writes=[B["vecs"]])
        wsTf = gv[:, 0, :]
        bsr = gv[0:1, 1, :]
        ckf = gv[:, 2, 0:256]
        cvf = gv[:, 2, 256:512]
        DMA("sp", wsTf, wsT_d, s_misc[1], writes=[gvB[0]])
        DMA("sp", bsr, bsrow_d, s_misc[2], writes=[gvB[1]])
        DMA("sp", ckf, ck_d, s_misc[3], writes=[gvB[2]])
        DMA("sp", cvf, cv_d, s_misc[4], writes=[gvB[2]])
        A("pool", "memset", identb[:], 0.0, writes=[B["identb"]])
        A("pool", "affine_select", out=identb[:], in_=identb[:], pattern=[[-1, 128]], compare_op=ALU.not_equal,
          fill=1.0, base=0, channel_multiplier=1, reads=[B["identb"]], writes=[B["identb"]])
        A("pool", "memset", identf[:], 0.0, writes=[B["identf"]])
        A("pool", "affine_select", out=identf[:], in_=identf[:], pattern=[[-1, 128]], compare_op=ALU.not_equal,
          fill=1.0, base=0, channel_multiplier=1, reads=[B["identf"]], writes=[B["identf"]])
        A("pool", "memset", onesb[:], 1.0, writes=[B["onesb"]])
        A("pool", "memset", epsc[:], EPS, writes=[B["epsc"]])
        A("pool", "memset", pT[:], 0.0, writes=[B["pT0"], B["pT1"]])
        A("pool", "memset", gv[:, 3, :], 1.0, writes=[gvB[3]])
        ring_advance()

        A("act", "activation", out=scT[:], in_=vecs[:, vo["c"]:vo["c"] + 32], func=AF.Silu, reads=[B["vecs"]],
          writes=[B["scT"]])
        A("act", "activation", out=esink[:], in_=vecs[:, vo["sink"]:vo["sink"] + 16], func=AF.Exp, reads=[B["vecs"]],
          writes=[B["esink"]])
        wsTf3 = wsTf.rearrange("p (g i) -> p g i", g=8)
        A("dve", "memset", wsTf3[64:128, :, 0:64], 0.0, reads=[], writes=[gvB[0]])
        A("dve", "tensor_copy", out=wsTb[:], in_=wsTf3, reads=[gvB[0]], writes=[B["wsTb"]])
        onesf = gv[:, 3, 0:128]
        lnb = lambda g: vecs[:, vo["lnb"] + g:vo["lnb"] + g + 1]
        lng = lambda g: vecs[:, vo["lng"] + g:vo["lng"] + g + 1]
        for half in range(2):
            bk, bkB = alloc()
            A("pe", "matmul", bk[:, :], lhsT=onesf, rhs=wsTf[:, half * 512:(half + 1) * 512], start=True, stop=True,
              reads=[gvB[3], gvB[0]], writes=[bkB])
            bk2, bk2B = alloc()
            A("pe", "matmul", bk2[:, :], lhsT=gv[0:1, 3, 0:128], rhs=bsr[:, half * 512:(half + 1) * 512], start=True,
              stop=True, reads=[gvB[3], gvB[1]], writes=[bk2B])
            ti = talloc()
            A("act", "activation", out=tmp[:, ti, :], in_=bk2[:, :], func=AF.Copy, reads=[bk2B], writes=[tmpB[ti]])
            for gg in range(4):
                g = half * 4 + gg
                A("dve", "scalar_tensor_tensor", out=Cg[:, g, :], in0=bk[:, gg * 128:(gg + 1) * 128], scalar=lnb(g),
                  in1=tmp[:, ti, gg * 128:(gg + 1) * 128], op0=ALU.mult, op1=ALU.add,
                  reads=[bkB, tmpB[ti], B["vecs"]], writes=[B["Cg"]])
        bk, bkB = alloc()
        bk2, bk2B = alloc()
        bsr3 = bsr.rearrange("p (g i) -> p g i", g=8)
        for g in range(8):
            A("pe", "matmul", bk[:, g * 32:(g + 1) * 32], lhsT=gv[0:32, 3, 0:128], rhs=wsTf3[0:32, g, 0:32], start=True,
              stop=True, reads=[gvB[3], gvB[0]], writes=[bkB])
            A("pe", "matmul", bk2[:, g * 32:(g + 1) * 32], lhsT=gv[0:1, 3, 0:128], rhs=bsr3[:, g, 0:32], start=True,
              stop=True, reads=[gvB[3], gvB[1]], writes=[bk2B])
        ti = talloc()
        A("act", "activation", out=tmp[:, ti, 0:256], in_=bk2[:, 0:256], func=AF.Copy, reads=[bk2B], writes=[tmpB[ti]])
        for g in range(8):
            A("dve", "scalar_tensor_tensor", out=CgS[:, g, :], in0=bk[:, g * 32:(g + 1) * 32], scalar=lnb(g),
              in1=tmp[:, ti, g * 32:(g + 1) * 32], op0=ALU.mult, op1=ALU.add, reads=[bkB, tmpB[ti], B["vecs"]],
              writes=[B["CgS"]])
        A("dve", "tensor_copy", out=qkb[:, 0:256], in_=ckf, reads=[gvB[2]], writes=[B["qkb"]])
        bk, bkB = alloc()
        bkb = bk[:, :].bitcast(BF16)
        for h in range(NKV):
            A("pe", "transpose", bkb[0:64, h * 128:(h + 1) * 128], qkb[:, h * 64:(h + 1) * 64], identb[:],
              reads=[B["qkb"], B["identb"]], writes=[bkB])
        A("dve", "tensor_copy", out=kTc[0:64, :], in_=bkb[0:64, 0:512], reads=[bkB], writes=[B["kTc"]])
        vaDc3 = vaDc[:, :].rearrange("p (h d) -> p h d", h=4)
        cv3 = cvf.rearrange("p (h d) -> p h d", h=4)
        A("dve", "tensor_copy", out=vaDc3[:, :, 0:64], in_=cv3, reads=[gvB[2]], writes=[B["vaDc"]])
        A("dve", "tensor_copy", out=vaDc3[:, :, 64:128], in_=cv3, reads=[gvB[2]], writes=[B["vaDc"]])
        out_dmas.append(DMA("act", ks_d[0:96, :], ck_d[32:128, :], s_misc[5]))
        out_dmas.append(DMA("act", vs_d[0:96, :], cv_d[32:128, :], s_misc[6]))

        def sc(s, r, kind, dc):
            c = ((s * 2 + r) * 3 + kind) * 16 + dc
            return scal[:, c:c + 1]

        def ada(pss, s):
            cur = Cur(pss, "ADA%d" % s)
            bk, bkB = alloc()
            for ci in range(48):
                for kc in range(KC):
                    bi = ci * KC + kc
                    w, wB = cur.blk(bi)
                    A("pe", "matmul", bk[:, 2 * ci:2 * ci + 2], lhsT=w, rhs=scT[:, 2 * kc:2 * kc + 2], start=(kc == 0),
                      stop=(kc == KC - 1), reads=[wB, B["scT"]], writes=[bkB])
                    if bi % UB == UB - 1:
                        ring_release()
            bo = vo["bada"] + s * 48
            A("dve", "tensor_tensor", out=modT[:, s * 48:(s + 1) * 48, :],
              in0=bk[:, 0:96].rearrange("p (c r) -> p c r", r=2),
              in1=vecs[:, bo:bo + 48].unsqueeze(2).to_broadcast([128, 48, 2]), op=ALU.add,
              reads=[bkB, B["vecs"]], writes=[B["modT"]])
            for r in range(2):
                base = ((s * 2 + r) * 3) * 16
                A("dve", "scalar_tensor_tensor", out=scal[:, base:base + 16], in0=modT[:, s * 48 + 16:s * 48 + 32, r],
                  scalar=1.0, in1=vecs[:, vo["g"] + s * 16:vo["g"] + s * 16 + 16], op0=ALU.add, op1=ALU.mult,
                  reads=[B["modT"], B["vecs"]], writes=[B["scal"]])
                A("dve", "tensor_copy", out=scal[:, base + 16:base + 32], in_=modT[:, s * 48:s * 48 + 16, r],
                  reads=[B["modT"]], writes=[B["scal"]])
                A("dve", "tensor_scalar", out=scal[:, base + 32:base + 48], in0=modT[:, s * 48 + 32:s * 48 + 48, r],
                  scalar1=(1.0 if s == 1 else 0.5), scalar2=None, op0=ALU.mult, reads=[B["modT"]],
                  writes=[B["scal"]])

        def norm(s, T, segs):
            bk, bkB = alloc()
            for dc in range(KC):
                q = dc % 2
                A("act", "activation", out=sqb[:, q, 0:T], in_=xT[:, dc, 0:T], func=AF.Square, reads=[xB[dc]],
                  writes=[B["sqb%d" % q]])
                A("pe", "matmul", bk[:, 0:T], lhsT=onesb[:], rhs=sqb[:, q, 0:T], start=(dc == 0), stop=(dc == KC - 1),
                  reads=[B["onesb"], B["sqb%d" % q]], writes=[bkB])
            A("act", "activation", out=rstd[:, 0:T], in_=bk[:, 0:T], func=AF.Sqrt, scale=1.0 / D, bias=epsc[:, 0:1],
              reads=[bkB, B["epsc"]], writes=[B["rstd"]])
            A("dve", "reciprocal", out=rstd[:, 0:T], in_=rstd[:, 0:T], reads=[B["rstd"]], writes=[B["rstd"]])
            for dc in range(KC):
                ti = talloc()
                A("dve", "tensor_tensor", out=tmp[:, ti, 0:T], in0=xT[:, dc, 0:T], in1=rstd[:, 0:T], op=ALU.mult,
                  reads=[xB[dc], B["rstd"]], writes=[tmpB[ti]])
                for (c0, c1, r) in segs:
                    A("act", "activation", out=hT[:, dc, c0:c1], in_=tmp[:, ti, c0:c1], func=AF.Identity,
                      scale=sc(s, r, 0, dc), bias=sc(s, r, 1, dc), reads=[tmpB[ti], B["scal"]], writes=[hB[dc]])

        def ffn(pss, s, T, segs, nameA, nameB, store=None):
            norm(s, T, segs)
            cur = Cur(pss, nameA)
            for f in range(NF):
                ba, baB = alloc()
                bb, bbB = alloc()
                for wi, (bk, bkB) in enumerate(((ba, baB), (bb, bbB))):
                    for kc in range(KC):
                        w, wB = cur.blk(f * 32 + wi * 16 + kc)
                        A("pe", "matmul", bk[:, 0:T], lhsT=w, rhs=hT[:, kc, 0:T], start=(kc == 0), stop=(kc == KC - 1),
                          reads=[wB, hB[kc]], writes=[bkB])
                ring_release()
                ti = talloc()
                A("act", "activation", out=tmp[:, ti, 0:T], in_=ba[:, 0:T], func=AF.Silu, reads=[baB],
                  writes=[tmpB[ti]])
                A("dve", "tensor_tensor", out=gT[:, f, 0:T], in0=bb[:, 0:T], in1=tmp[:, ti, 0:T], op=ALU.mult,
                  reads=[bbB, tmpB[ti]], writes=[gB[f]])
            cur = Cur(pss, nameB)
            bi = 0
            for dc in range(KC):
                bk, bkB = alloc()
                for f in range(NF):
                    w, wB = cur.blk(bi)
                    A("pe", "matmul", bk[:, 0:T], lhsT=w, rhs=gT[:, f, 0:T], start=(f == 0), stop=(f == NF - 1),
                      reads=[wB, gB[f]], writes=[bkB])
                    if bi % UB == UB - 1:
                        ring_release()
                    bi += 1
                if store is None:
                    for (c0, c1, r) in segs:
                        A("dve", "scalar_tensor_tensor", out=xT[:, dc, c0:c1], in0=bk[:, c0:c1], scalar=sc(s, r, 2, dc),
                          in1=xT[:, dc, c0:c1], op0=ALU.mult, op1=ALU.add, reads=[bkB, xB[dc], B["scal"]],
                          writes=[xB[dc]])
                else:
                    (c0, c1, r) = segs[0]
                    ti = talloc()
                    A("dve", "scalar_tensor_tensor", out=tmp[:, ti, 0:T], in0=bk[:, 0:T], scalar=sc(s, r, 2, dc),
                      in1=xT[:, dc, 0:T], op0=ALU.mult, op1=ALU.add, reads=[bkB, xB[dc], B["scal"]],
                      writes=[tmpB[ti]])
                    out_dmas.append(DMA("act", store(dc), tmp[:, ti, 0:T], s_tmp[ti], reads=[tmpB[ti]]))

        uT = lambda c: gT[:, c, :]
        oTt = lambda t: gT[:, 8 + t, :]
        yTt = lambda dc: gT[:, 16 + dc, :]
        vnb = lambda b: gT[:, 32 + 2 * b:34 + 2 * b, :].rearrange("p a c -> p (a c)")
        vnbB = lambda b: [gB[32 + 2 * b], gB[33 + 2 * b]]
        qTv = lambda b: gT[:, 36 + 4 * b:40 + 4 * b, :].rearrange("p a c -> p (a c)")
        qTB = lambda b: [gB[36 + 4 * b + i] for i in range(4)]
        st = {"kv": 0, "blk": 0}

        def rope_ap(kind, jg, n):
            if kind == "P":
                o = vo["ropeP"] + jg * 16
            elif kind == "S":
                o = vo["ropeS"]
            else:
                o = vo["ropeH"]
            return vecs[0:n, o:o + 8], vecs[0:n, o + 8:o + 16]

        def attention(nq, qslot, tiles, c0, halo=False):
            qT_ = qTv(qslot)
            nt_ = len(tiles)
            state = {}

            def s1(h):
                pset = h % 2
                pTb = B["pT%d" % pset] if nq == 128 else B["pTS%d" % pset]
                pviews = []
                for ti_, (kTa, kTb_, vDa, vDb_, nk, mask) in enumerate(tiles):
                    bk, bkB = alloc()
                    A("pe", "matmul", bk[0:nk, 0:4 * nq], lhsT=kTa[0:64, h * nk:(h + 1) * nk],
                      rhs=qT_[0:64, h * 4 * nq:(h + 1) * 4 * nq], start=True, stop=True, reads=[kTb_] + qTB(qslot),
                      writes=[bkB])
                    if nq == 128:
                        pv = pT[:, pset, ti_, :]
                    else:
                        pv = pTS[:, pset, ti_, :]
                    pviews.append(pv)
                    s3 = bk[:, 0:4 * nq].rearrange("p (g q) -> p g q", g=4)
                    p3 = pv[:, 0:4 * nq].rearrange("p (g q) -> p g q", g=4)
                    kw = {}
                    if mask == "full":
                        regs = [(0, nk, 0, nq)]
                    elif mask == "prev":
                        regs = [(0, 64, 0, 64), (64, 128, 0, 128)]
                    else:
                        regs = [(0, 64, 0, 128), (64, 128, 64, 128)]
                    for (p0, p1, q0, q1) in regs:
                        bias = vecs[p0:p1, vo["halo"]:vo["halo"] + 1] if (halo and mask == "prev") else 0.0
                        A("act", "activation", out=p3[p0:p1, :, q0:q1], in_=s3[p0:p1, :, q0:q1], func=AF.Exp,
                          scale=HD ** -0.5, bias=bias, reads=[bkB, B["vecs"]], writes=[pTb])
                state[h] = (pTb, pviews)

            def s2(h):
                pTb, pviews = state[h]
                bo, boB = alloc()
                bd, bdB = alloc()
                for ti_, (kTa, kTb_, vDa, vDb_, nk, mask) in enumerate(tiles):
                    A("pe", "matmul", bo[:, 0:4 * nq], lhsT=vDa[0:nk, h * 128:(h + 1) * 128],
                      rhs=pviews[ti_][0:nk, 0:4 * nq], start=(ti_ == 0), stop=(ti_ == nt_ - 1), reads=[vDb_, pTb],
                      writes=[boB])
                for ti_, (kTa, kTb_, vDa, vDb_, nk, mask) in enumerate(tiles):
                    A("pe", "matmul", bd[:, 0:4 * nq], lhsT=onesb[0:nk, :], rhs=pviews[ti_][0:nk, 0:4 * nq],
                      start=(ti_ == 0), stop=(ti_ == nt_ - 1), reads=[B["onesb"], pTb], writes=[bdB])
                ti = talloc()
                r3 = tmp[:, ti, 0:4 * nq].rearrange("p (g q) -> p g q", g=4)
                for g in range(4):
                    A("act", "activation", out=tmp[:, ti, g * nq:(g + 1) * nq], in_=bd[:, g * nq:(g + 1) * nq],
                      func=AF.Identity, scale=1.0, bias=esink[:, 4 * h + g:4 * h + g + 1], reads=[bdB, B["esink"]],
                      writes=[tmpB[ti]])
                A("dve", "reciprocal", out=tmp[:, ti, 0:4 * nq], in_=tmp[:, ti, 0:4 * nq], reads=[tmpB[ti]],
                  writes=[tmpB[ti]])
                o3 = bo[:, 0:4 * nq].rearrange("p (g q) -> p g q", g=4)
                for par in range(2):
                    p0 = par * 64
                    A("dve", "tensor_tensor", out=gT[p0:p0 + 64, 8 + 2 * h:8 + 2 * h + 2, c0:c0 + nq],
                      in0=o3[p0:p0 + 64, par::2, :], in1=r3[p0:p0 + 64, par::2, :], op=ALU.mult,
                      reads=[boB, tmpB[ti]], writes=[gB[8 + 2 * h], gB[8 + 2 * h + 1]])

            for h in range(NKV + 1):
                if h < NKV:
                    s1(h)
                if h >= 1:
                    s2(h - 1)

        def spatial_gate(n, vslot, c0, is_s):
            for half in range(2):
                bk, bkB = alloc()
                for gg in range(4):
                    g = half * 4 + gg
                    A("pe", "matmul", bk[:, gg * n:(gg + 1) * n], lhsT=vnb(vslot)[0:n, g * 128:(g + 1) * 128],
                      rhs=wsTb[0:n, g, 0:n], start=True, stop=True, reads=vnbB(vslot) + [B["wsTb"]], writes=[bkB])
                for gg in range(4):
                    g = half * 4 + gg
                    ti = talloc()
                    Cv = CgS[:, g, 0:n] if is_s else Cg[:, g, 0:n]
                    A("dve", "scalar_tensor_tensor", out=tmp[:, ti, 0:n], in0=bk[:, gg * n:(gg + 1) * n],
                      scalar=lng(g), in1=Cv, op0=ALU.mult, op1=ALU.add,
                      reads=[bkB, B["vecs"], B["Cg"], B["CgS"]], writes=[tmpB[ti]])
                    A("dve", "tensor_tensor", out=gT[:, g, c0:c0 + n], in0=gT[:, g, c0:c0 + n], in1=tmp[:, ti, 0:n],
                      op=ALU.mult, reads=[gB[g], tmpB[ti]], writes=[gB[g]])

        def mixer2(pss, T, Tm, segs, mseg, blocks, first_p, last):
            norm(1, T, segs)
            cur = Cur(pss, "MIXA")
            full_blocks = [b for b in blocks if b["full"]]
            for i, b in enumerate(full_blocks):
                b["gi"] = i
                b["vslot"] = i % 2
            for cg in range(2):
                groups = [full_blocks[i:i + 3] for i in range(0, len(full_blocks), 3)]
                for grp in groups:
                    bks = [alloc() for _ in grp]
                    for kc in range(KC):
                        w, wB = cur.blk(cg * 64 + kc * 4, 4)
                        for b, (bk, bkB) in zip(grp, bks):
                            A("pe", "matmul", bk[0:b["n"], :], lhsT=hT[:, kc, b["c0"]:b["c0"] + b["n"]], rhs=w,
                              start=(kc == 0), stop=(kc == KC - 1), reads=[wB, hB[kc]], writes=[bkB])
                    for b, (bk, bkB) in zip(grp, bks):
                        n, gi = b["n"], b["gi"]
                        A("act", "activation", out=gv[0:n, gi, cg * 512:(cg + 1) * 512], in_=bk[0:n, :], func=AF.Gelu,
                          reads=[bkB], writes=[gvB[gi]])
                ring_release()
                ring_release()
            for cg in range(2, 4):
                bl = full_blocks
                groups = [bl[i:i + 3] for i in range(0, len(bl), 3)]
                for grp in groups:
                    bks = [alloc() for _ in grp]
                    for kc in range(KC):
                        w, wB = cur.blk(cg * 64 + kc * 4, 4)
                        for b, (bk, bkB) in zip(grp, bks):
                            A("pe", "matmul", bk[0:b["n"], :], lhsT=hT[:, kc, b["c0"]:b["c0"] + b["n"]], rhs=w,
                              start=(kc == 0), stop=(kc == KC - 1), reads=[wB, hB[kc]], writes=[bkB])
                    for b, bkp in zip(grp, bks):
                        b["cg%d" % cg] = bkp
                        if cg < 4:
                            n, gi = b["n"], b["gi"]
                            A("act", "activation", out=qraw[0:n, gi, (cg - 2) * 512:(cg - 1) * 512], in_=bkp[0][0:n, :],
                              func=AF.Copy, reads=[bkp[1]], writes=qrawBs(gi))
                ring_release()
                ring_release()
            return full_blocks

        qraw_all = gT[:, 16:32, :].rearrange("p a c -> p (a c)").bitcast(F32)

        class _QR:
            def __getitem__(self, key):
                p, gi, c = key
                if isinstance(c, slice) and c.start is None:
                    return qraw_all[p, gi * 1024:(gi + 1) * 1024]
                return qraw_all[p, gi * 1024 + c.start:gi * 1024 + c.stop]
        qraw = _QR()

        qrawBs = lambda gi: [gB[16 + 4 * gi + i] for i in range(4)]

        def qkA(b, kslot, want_q, sbi, out_v=None):
            n, kind, jg = b["n"], b["kind"], b["jg"]
            bkv = b["cg4"]
            lo = 0 if want_q else 1024
            W = 1280 - lo
            nh = W // 64
            A("act", "activation", out=qkf[0:n, 0:256], in_=bkv[0][0:n, 0:256], func=AF.Copy, reads=[bkv[1]],
              writes=[B["qkf"]])
            vd3 = vaDr[0:n, kslot, :].rearrange("p (h d) -> p h d", h=4)
            va3 = bkv[0][0:n, 256:512].rearrange("p (h d) -> p h d", h=4)
            A("act", "activation", out=vd3[:, :, 0:64], in_=va3, func=AF.Copy, reads=[bkv[1]], writes=[vaDB[kslot]])
            A("act", "activation", out=vd3[:, :, 64:128], in_=va3, func=AF.Copy, reads=[bkv[1]], writes=[vaDB[kslot]])
            if out_v is not None:
                A("act", "activation", out=vaf[0:n, :], in_=bkv[0][0:n, 256:512], func=AF.Copy, reads=[bkv[1]],
                  writes=[B["vaf"]])
                out_dmas.append(DMA("act", out_v, vaf[0:n, :], s_misc[7], reads=[B["vaf"]]))

            def src(c0_, c1_):
                return None
            if want_q:
                gi = b["gi"]
                A("act", "activation", out=sqnT[0:n, sbi, 0:1024], in_=qraw[0:n, gi, :], func=AF.Square,
                  reads=qrawBs(gi), writes=[B["sqn%d" % sbi]])
            A("act", "activation", out=sqnT[0:n, sbi, 1024:1280], in_=qkf[0:n, 0:256], func=AF.Square,
              reads=[B["qkf"]], writes=[B["sqn%d" % sbi]])
            A("dve", "tensor_reduce", out=small[0:n, 0:nh], in_=sqnT[0:n, sbi, lo:1280].rearrange("p (h d) -> p h d", d=64),
              axis=AX.X, op=ALU.add, reads=[B["sqn%d" % sbi]], writes=[B["small"]])
            A("act", "activation", out=small[0:n, 0:nh], in_=small[0:n, 0:nh], func=AF.Sqrt, scale=1.0 / HD,
              bias=epsc[0:n, 0:1], reads=[B["small"], B["epsc"]], writes=[B["small"]])
            A("dve", "reciprocal", out=small[0:n, 0:nh], in_=small[0:n, 0:nh], reads=[B["small"]],
              writes=[B["small"]])
            if want_q:
                gi = b["gi"]
                A("dve", "tensor_tensor", out=sqnT[0:n, sbi, 0:1024].rearrange("p (h d) -> p h d", d=64),
                  in0=qraw[0:n, gi, :].rearrange("p (h d) -> p h d", d=64),
                  in1=small[0:n, 0:16].unsqueeze(2).to_broadcast([n, 16, 64]), op=ALU.mult,
                  reads=qrawBs(gi) + [B["small"]], writes=[B["sqn%d" % sbi]])
                A("pool", "tensor_tensor", out=sqnT[0:n, sbi, 0:1024].rearrange("p (h d) -> p h d", d=64),
                  in0=sqnT[0:n, sbi, 0:1024].rearrange("p (h d) -> p h d", d=64),
                  in1=vecs[0:n, vo["gq"]:vo["gq"] + 64].unsqueeze(1).to_broadcast([n, 16, 64]), op=ALU.mult,
                  reads=[B["sqn%d" % sbi], B["vecs"]], writes=[B["sqn%d" % sbi]])
            A("dve", "tensor_tensor", out=sqnT[0:n, sbi, 1024:1280].rearrange("p (h d) -> p h d", d=64),
              in0=qkf[0:n, 0:256].rearrange("p (h d) -> p h d", d=64),
              in1=small[0:n, nh - 4:nh].unsqueeze(2).to_broadcast([n, 4, 64]), op=ALU.mult,
              reads=[B["qkf"], B["small"]], writes=[B["sqn%d" % sbi]])
            A("pool", "tensor_tensor", out=sqnT[0:n, sbi, 1024:1280].rearrange("p (h d) -> p h d", d=64),
              in0=sqnT[0:n, sbi, 1024:1280].rearrange("p (h d) -> p h d", d=64),
              in1=vecs[0:n, vo["gk"]:vo["gk"] + 64].unsqueeze(1).to_broadcast([n, 4, 64]), op=ALU.mult,
              reads=[B["sqn%d" % sbi], B["vecs"]], writes=[B["sqn%d" % sbi]])

        def qkB(b, kslot, want_q, sbi, out_k=None):
            n, kind, jg = b["n"], b["kind"], b["jg"]
            lo = 0 if want_q else 1024
            nh = (1280 - lo) // 64
            X = sqnT[0:n, sbi, lo:1280].rearrange("p (h d) -> p h d", d=64)
            x1, x2 = X[:, :, 0:8], X[:, :, 8:16]
            cosA, sinA = rope_ap(kind, jg, n)
            cb = cosA.unsqueeze(1).to_broadcast([n, nh, 8])
            sbb = sinA.unsqueeze(1).to_broadcast([n, nh, 8])
            rt = lambda i: ropet[0:n, i, 0:nh * 8].rearrange("p (h d) -> p h d", d=8)
            rd = [B["sqn%d" % sbi], B["vecs"]]
            A("dve", "tensor_tensor", out=rt(0), in0=x1, in1=cb, op=ALU.mult, reads=rd, writes=[B["ropet"]])
            A("dve", "tensor_tensor", out=rt(1), in0=x2, in1=sbb, op=ALU.mult, reads=rd, writes=[B["ropet"]])
            A("dve", "tensor_tensor", out=rt(2), in0=x2, in1=cb, op=ALU.mult, reads=rd, writes=[B["ropet"]])
            A("dve", "tensor_tensor", out=rt(3), in0=x1, in1=sbb, op=ALU.mult, reads=rd, writes=[B["ropet"]])
            A("dve", "tensor_tensor", out=x1, in0=rt(0), in1=rt(1), op=ALU.subtract, reads=[B["ropet"]],
              writes=[B["sqn%d" % sbi]])
            A("dve", "tensor_tensor", out=x2, in0=rt(2), in1=rt(3), op=ALU.add, reads=[B["ropet"]],
              writes=[B["sqn%d" % sbi]])
            A("act", "activation", out=qkb[0:n, lo:1280], in_=sqnT[0:n, sbi, lo:1280], func=AF.Copy, reads=[B["sqn%d" % sbi]],
              writes=[B["qkb"]])
            if out_k is not None:
                out_dmas.append(DMA("act", out_k, sqnT[0:n, sbi, 1024:1280], s_misc[8], reads=[B["sqn%d" % sbi]]))
            qslot = None
            if want_q:
                qslot = st["blk"] % 2
                st["blk"] += 1
                for half in range(2):
                    bk, bkB = alloc()
                    bkb_ = bk[:, :].bitcast(BF16)
                    for hh in range(8):
                        h = half * 8 + hh
                        A("pe", "transpose", bkb_[0:64, hh * n:(hh + 1) * n], qkb[0:n, h * 64:(h + 1) * 64],
                          identb[0:n, 0:n], reads=[B["qkb"], B["identb"]], writes=[bkB])
                    A("act", "activation", out=qTv(qslot)[0:64, half * 8 * n:(half + 1) * 8 * n],
                      in_=bkb_[0:64, 0:8 * n], func=AF.Copy, reads=[bkB], writes=qTB(qslot))
            bk, bkB = alloc()
            bkb_ = bk[:, :].bitcast(BF16)
            for h in range(NKV):
                A("pe", "transpose", bkb_[0:64, h * n:(h + 1) * n], qkb[0:n, 1024 + h * 64:1024 + (h + 1) * 64],
                  identb[0:n, 0:n], reads=[B["qkb"], B["identb"]], writes=[bkB])
            A("act", "activation", out=kTr[0:64, kslot, 0:4 * n], in_=bkb_[0:64, 0:4 * n], func=AF.Copy, reads=[bkB],
              writes=[kTB[kslot]])
            return qslot

        def v_process2(b):
            n, gi, vslot = b["n"], b["gi"], b["vslot"]
            is_s = b["kind"] == "S"
            A("dve", "bn_stats", out=small[0:n, 32:38], in_=gv[0:n, gi, 0:512], reads=[gvB[gi]], writes=[B["ss"]])
            A("dve", "bn_stats", out=small[0:n, 38:44], in_=gv[0:n, gi, 512:1024], reads=[gvB[gi]], writes=[B["ss"]])
            A("dve", "bn_aggr", out=small[0:n, 44:46], in_=small[0:n, 32:44], reads=[B["ss"]], writes=[B["ss"]])
            A("act", "activation", out=small[0:n, 46:47], in_=small[0:n, 45:46], func=AF.Sqrt, scale=1.0,
              bias=epsc[0:n, 0:1], reads=[B["ss"], B["epsc"]], writes=[B["ss"]])
            A("dve", "reciprocal", out=small[0:n, 46:47], in_=small[0:n, 46:47], reads=[B["ss"]], writes=[B["ss"]])
            if is_s:
                A("dve", "tensor_scalar", out=gv[0:n, gi, :], in0=gv[0:n, gi, :], scalar1=small[0:n, 44:45],
                  scalar2=small[0:n, 46:47], op0=ALU.subtract, op1=ALU.mult, reads=[gvB[gi], B["ss"]],
                  writes=[gvB[gi]])
                A("dve", "tensor_copy", out=vnb(vslot)[0:n, :], in_=gv[0:n, gi, :], reads=[gvB[gi]],
                  writes=vnbB(vslot))
                bk, bkB = alloc()
                for g in range(8):
                    A("pe", "matmul", bk[:, g * n:(g + 1) * n], lhsT=gv[0:n, gi, g * 128:(g + 1) * 128],
                      rhs=identf[0:n, 0:n], start=True, stop=True, reads=[gvB[gi], B["identf"]], writes=[bkB])
                for g in range(8):
                    A("act", "activation", out=vnTs[:, g, :], in_=bk[:, g * n:(g + 1) * n], func=AF.Identity,
                      scale=lng(g), bias=lnb(g), reads=[bkB, B["vecs"]], writes=[B["vnTs"]])
                out_dmas.append(DMA("act", vnT_d.rearrange("(g p) t -> p g t", p=128), vnTs[:], s_misc[9],
                                    reads=[B["vnTs"]]))
            else:
                A("dve", "scalar_tensor_tensor", out=small[0:n, 47:48], in0=small[0:n, 44:45], scalar=-1.0,
                  in1=small[0:n, 46:47], op0=ALU.mult, op1=ALU.mult, reads=[B["ss"]], writes=[B["ss"]])
                A("act", "activation", out=vnb(vslot)[0:n, :], in_=gv[0:n, gi, :], func=AF.Identity,
                  scale=small[0:n, 46:47], bias=small[0:n, 47:48], reads=[gvB[gi], B["ss"]], writes=vnbB(vslot))

        def mixer_rest(pss, T, Tm, mseg, blocks, first_p, last, konly=False):
            full_blocks = [b for b in blocks if b["full"]]
            cur = Cur(pss, "MIXU") if not konly else None
            for c in range(8 if not konly else 0):
                bk, bkB = alloc()
                for kc in range(KC):
                    w, wB = cur.blk(c * KC + kc)
                    A("pe", "matmul", bk[:, 0:T], lhsT=w, rhs=hT[:, kc, 0:T], start=(kc == 0), stop=(kc == KC - 1),
                      reads=[wB, hB[kc]], writes=[bkB])
                    if (c * KC + kc) % UB == UB - 1:
                        ring_release()
                A("act", "activation", out=gT[:, c, 0:T], in_=bk[:, 0:T], func=AF.Gelu, reads=[bkB], writes=[gB[c]])
            curk = Cur(pss, "MIXK")
            def stageA(b, i):
                bkp = alloc()
                for kc in range(KC):
                    w, wB = curk.blk(kc * 4, 4)
                    A("pe", "matmul", bkp[0][0:b["n"], :], lhsT=hT[:, kc, b["c0"]:b["c0"] + b["n"]], rhs=w,
                      start=(kc == 0), stop=(kc == KC - 1), reads=[wB, hB[kc]], writes=[bkp[1]])
                b["cg4"] = bkp
                if b is blocks[-1]:
                    ring_release()
                    ring_release()
                ks = st["kv"] % 4
                st["kv"] += 1
                b["kslot"] = ks
                b["sbi"] = i % 2
                if b["kind"] == "H":
                    qkA(b, ks, True, b["sbi"])
                    st["prev_k"] = ks
                    b["attn"] = False
                    return
                v_process2(b)
                spatial_gate(b["n"], b["vslot"], b["c0"], b["kind"] == "S")
                b["attn"] = True
                if b["kind"] == "S":
                    qkA(b, ks, True, b["sbi"], out_v=vs_d[96:128, :])
                    b["out_k"] = ks_d[96:128, :]
                    b["tiles"] = [(kTc[:, :], B["kTc"], vaDc[:, :], B["vaDc"], 128, "full"),
                                  (kTr[:, ks, :], kTB[ks], vaDr[:, ks, :], vaDB[ks], 32, "full")]
                else:
                    is_last = last and b is blocks[-1]
                    qkA(b, ks, True, b["sbi"], out_v=vlast_d if is_last else None)
                    b["out_k"] = klast_d if is_last else None
                    pk = st["prev_k"]
                    b["tiles"] = [(kTr[:, pk, :], kTB[pk], vaDr[:, pk, :], vaDB[pk], 128, "prev"),
                                  (kTr[:, ks, :], kTB[ks], vaDr[:, ks, :], vaDB[ks], 128, "cur")]
                    st["prev_k"] = ks

            def stageB(b):
                b["qs"] = qkB(b, b["kslot"], True, b["sbi"], out_k=b.get("out_k"))

            def back(b):
                if not b["attn"]:
                    return
                if b["kind"] == "S":
                    attention(32, b["qs"], b["tiles"], b["c0"])
                else:
                    attention(128, b["qs"], b["tiles"], b["c0"], halo=(first_p and b is blocks[0]))

            nb_ = len(blocks)
            for i in range(nb_ + 2):
                if i < nb_:
                    stageA(blocks[i], i)
                if 0 <= i - 1 < nb_:
                    stageB(blocks[i - 1])
                if 0 <= i - 2 < nb_:
                    back(blocks[i - 2])
            if konly:
                return
            (m0, m1, mr) = mseg
            cur = Cur(pss, "MIXM")
            for dc in range(KC):
                bga, bgaB = alloc()
                bpa, bpaB = alloc()
                bgb, bgbB = alloc()
                bpb, bpbB = alloc()
                base = dc * 48
                for kc in range(KC):
                    w, wB = cur.blk(base + kc)
                    A("pe", "matmul", bga[:, m0:m1], lhsT=w, rhs=hT[:, kc, m0:m1], start=(kc == 0), stop=(kc == KC - 1),
                      reads=[wB, hB[kc]], writes=[bgaB])
                for c in range(8):
                    w, wB = cur.blk(base + 16 + c)
                    A("pe", "matmul", bpa[:, m0:m1], lhsT=w, rhs=gT[:, c, m0:m1], start=(c == 0), stop=(c == 7),
                      reads=[wB, gB[c]], writes=[bpaB])
                for kc in range(KC):
                    w, wB = cur.blk(base + 24 + kc)
                    A("pe", "matmul", bgb[:, m0:m1], lhsT=w, rhs=hT[:, kc, m0:m1], start=(kc == 0), stop=(kc == KC - 1),
                      reads=[wB, hB[kc]], writes=[bgbB])
                for c in range(8):
                    w, wB = cur.blk(base + 40 + c)
                    A("pe", "matmul", bpb[:, m0:m1], lhsT=w, rhs=gT[:, 8 + c, m0:m1], start=(c == 0), stop=(c == 7),
                      reads=[wB, gB[8 + c]], writes=[bpbB])
                if dc % 2 == 1:
                    ring_release()
                    ring_release()
                    ring_release()
                t1 = talloc()
                A("act", "activation", out=tmp[:, t1, m0:m1], in_=bga[:, m0:m1], func=AF.Sigmoid, reads=[bgaB],
                  writes=[tmpB[t1]])
                A("dve", "tensor_tensor", out=tmp[:, t1, m0:m1], in0=bpa[:, m0:m1], in1=tmp[:, t1, m0:m1], op=ALU.mult,
                  reads=[bpaB, tmpB[t1]], writes=[tmpB[t1]])
                t2 = talloc()
                A("act", "activation", out=tmp[:, t2, m0:m1], in_=bgb[:, m0:m1], func=AF.Sigmoid, reads=[bgbB],
                  writes=[tmpB[t2]])
                A("dve", "tensor_tensor", out=tmp[:, t2, m0:m1], in0=bpb[:, m0:m1], in1=tmp[:, t2, m0:m1], op=ALU.mult,
                  reads=[bpbB, tmpB[t2]], writes=[tmpB[t2]])
                A("dve", "tensor_tensor", out=gT[:, 16 + dc, m0:m1], in0=tmp[:, t1, m0:m1], in1=tmp[:, t2, m0:m1],
                  op=ALU.add, reads=[tmpB[t1], tmpB[t2]], writes=[gB[16 + dc]])
            cur = Cur(pss, "MIXO")
            for dc in range(KC):
                bk, bkB = alloc()
                for kc in range(KC):
                    w, wB = cur.blk(dc * KC + kc)
                    A("pe", "matmul", bk[:, m0:m1], lhsT=w, rhs=gT[:, 16 + kc, m0:m1], start=(kc == 0),
                      stop=(kc == KC - 1), reads=[wB, gB[16 + kc]], writes=[bkB])
                if dc % 2 == 1:
                    ring_release()
                A("dve", "scalar_tensor_tensor", out=xT[:, dc, m0:m1], in0=bk[:, m0:m1], scalar=sc(1, mr, 2, dc),
                  in1=xT[:, dc, m0:m1], op0=ALU.mult, op1=ALU.add, reads=[bkB, xB[dc], B["scal"]], writes=[xB[dc]])

        STOP = int(os.environ.get("KSTOP", "99"))
        class _Stop(Exception):
            pass
        def stop_if(k):
            if STOP == k:
                raise _Stop()
        try:
            segs0 = [(0, TH, 1)]
            DMA("sp", xT[:, :, 0:TH], xsh_d.rearrange("(dc p) t -> p dc t", p=128), s_misc[10], writes=xB)
            ada(0, 0)
            ffn(0, 0, TH, segs0, "F1A", "F1B")
            ada(0, 1)
            blocks0 = [dict(kind="H", c0=0, n=TH, jg=0, full=True)]
            mixer2(0, TH, TH, segs0, (0, TH, 1), blocks0, False, False)
            mixer_rest(0, TH, TH, (0, TH, 1), blocks0, False, False, konly=True)

            segsP = [(0, TP, 1)]
            for t in range(1, NT + 1):
                t0 = (t - 1) * TP
                for dc in range(KC):
                    DMA("sp", xT[:, dc, :], xT_d[dc * 128:(dc + 1) * 128, t0:t0 + TP], s_x[dc], writes=[xB[dc]])
                ffn(t, 0, TP, segsP, "F1A", "F1B")
                blocksP = [dict(kind="P", c0=j * 128, n=128, jg=(t - 1) * 4 + j, full=True) for j in range(4)]
                mixer2(t, TP, TP, segsP, (0, TP, 1), blocksP, t == 1, t == NT)
                mixer_rest(t, TP, TP, (0, TP, 1), blocksP, t == 1, t == NT)
                if t == 1:
                    ada(1, 2)
                ffn(t, 2, TP, segsP, "F2A", "F2B",
                    store=lambda dc, t0=t0: yT_d[dc * 128:(dc + 1) * 128, t0:t0 + TP])

            ps_ = NT + 1
            segsS = [(0, TS, 0)]
            DMA("sp", xT[:, :, 0:TS], xs_d.rearrange("(dc p) t -> p dc t", p=128), s_misc[12], writes=xB)
            ffn(ps_, 0, TS, segsS, "F1A", "F1B")
            blocksS = [dict(kind="S", c0=0, n=TS, jg=0, full=True)]
            mixer2(ps_, TS, TS, segsS, (0, TS, 0), blocksS, False, False)
            mixer_rest(ps_, TS, TS, (0, TS, 0), blocksS, False, False)
            ffn(ps_, 2, TS, segsS, "F2A", "F2B")
            out_dmas.append(DMA("act", yTs_d.rearrange("(dc p) t -> p dc t", p=128), xT[:, :, 0:TS], s_misc[11],
                                reads=xB))

        except _Stop:
            pass
        if STOP == 99:
            assert rs["released"] == NSEQ, (rs, NSEQ)
        for e in ("act", "sp"):
            P.add(e, lambda h: None, extra_deps=[i for i in out_dmas])
        P.emit(blk, s_eng)
    return nc


def _blocks(inp, NF):
    def r4(w, a, b):
        K, N = w.shape
        return w.reshape(K // 128, 128, N // 128, 128)
    w_ada = inp["w_ada"][0]
    w_in = inp["w_in"][0]
    out = []
    def ada(s):
        w = r4(w_ada[:, s * 6144:(s + 1) * 6144], 0, 0)
        return w.transpose(2, 0, 1, 3).reshape(-1, 128, 128)
    def ffa(w1, w3):
        a = r4(w1, 0, 0).transpose(2, 0, 1, 3)
        b = r4(w3, 0, 0).transpose(2, 0, 1, 3)
        return np.concatenate([a, b], axis=1).reshape(-1, 128, 128)
    def ffb(w2):
        return r4(w2, 0, 0).transpose(2, 0, 1, 3).reshape(-1, 128, 128)
    def mixa():
        cols = [np.arange(1024, 1536), np.arange(1536, 2048), np.arange(2048, 2560), np.arange(2560, 3072)]
        res = []
        for c in cols:
            w = w_in[:, c].reshape(16, 128, 4, 128).transpose(0, 2, 1, 3)
            res.append(w.reshape(-1, 128, 128))
        return np.concatenate(res, 0)
    def mixk():
        w = w_in[:, 3072:3584].reshape(16, 128, 4, 128).transpose(0, 2, 1, 3)
        return w.reshape(-1, 128, 128)
    def mixu():
        return r4(w_in[:, 0:1024], 0, 0).transpose(2, 0, 1, 3).reshape(-1, 128, 128)
    def mixm():
        ga = r4(w_in[:, 3584:5632], 0, 0).transpose(2, 0, 1, 3)
        gb = r4(w_in[:, 5632:7680], 0, 0).transpose(2, 0, 1, 3)
        pa = r4(inp["w_pa"][0], 0, 0).transpose(2, 0, 1, 3)
        pb = r4(inp["w_pb"][0], 0, 0).transpose(2, 0, 1, 3)
        return np.concatenate([ga, pa, gb, pb], axis=1).reshape(-1, 128, 128)
    def mixo():
        return r4(inp["w_o"][0], 0, 0).transpose(2, 0, 1, 3).reshape(-1, 128, 128)
    parts = [ada(0), ffa(inp["w1_ffn1"][0], inp["w3_ffn1"][0]), ffb(inp["w2_ffn1"][0]), ada(1), mixa(), mixu(), mixk(), mixm(),
             mixo(), ada(2), ffa(inp["w1_ffn2"][0], inp["w3_ffn2"][0]), ffb(inp["w2_ffn2"][0])]
    allb = np.concatenate(parts, 0)
    NB = allb.shape[0]
    assert NB % UB == 0
    u = allb.reshape(NB // UB, UB, 128, 128).transpose(0, 2, 1, 3).reshape(NB // UB, 128, UB * 128)
    return np.ascontiguousarray(u, dtype=np.float32)


def _rope_tab(pos):
    inv = ROPE_THETA ** (-np.arange(0, 16, 2, dtype=np.float32) / np.float32(16))
    ang = pos.astype(np.float32)[:, None] * inv.astype(np.float32)[None, :]
    return np.concatenate([np.cos(ang), np.sin(ang)], axis=1).astype(np.float32)


_CACHE = {}


def run(inp, NT, NF):
    inp = {k: np.asarray(v, dtype=np.float32) for k, v in inp.items()}
    key = (NT, NF)
    if key not in _CACHE:
        _CACHE[key] = build(NT, NF)
    nc = _CACHE[key]
    vo, NV = vec_layout(NT)
    wsrc = _blocks(inp, NF)
    xp = inp["x_prompt"]
    xs = inp["x_sample"]
    Bp, SEQ, _ = xp.shape
    HALF = NT * TP
    assert SEQ == 2 * HALF and Bp == 4 and xs.shape[0] == 8
    wsT = np.ascontiguousarray(inp["w_s"][0].transpose(2, 0, 1).reshape(128, 1024))
    bsrow = np.ascontiguousarray(inp["b_s"][0].reshape(1, 1024))
    in_maps = []
    for c in range(8):
        b, hf = c // 2, c % 2
        xT = np.ascontiguousarray(xp[b, hf * HALF:(hf + 1) * HALF, :].T)
        if hf == 1:
            halo = xp[b, HALF - TH:HALF, :]
        else:
            halo = np.zeros((TH, D), np.float32)
        xsh = np.ascontiguousarray(halo.T)
        xs_t = np.ascontiguousarray(xs[c].T)
        vecs = np.zeros((128, NV), np.float32)
        cc = np.stack([inp["c_sample"][c], inp["c_prompt"][b]], 1)
        vecs[:, vo["c"]:vo["c"] + 32] = cc.reshape(16, 128, 2).transpose(1, 0, 2).reshape(128, 32)
        vecs[:, vo["bada"]:vo["bada"] + 144] = inp["b_ada"][0].reshape(144, 128).T
        for s, nm in enumerate(("g_ffn1", "g_mix", "g_ffn2")):
            vecs[:, vo["g"] + s * 16:vo["g"] + s * 16 + 16] = inp[nm][0].reshape(16, 128).T
        vecs[:, vo["halo"]] = 0.0 if hf == 1 else -30000.0
        vecs[:, vo["sink"]:vo["sink"] + 16] = inp["sinks"][0][None, :]
        vecs[:, vo["gq"]:vo["gq"] + 64] = inp["g_q"][0][None, :]
        vecs[:, vo["gk"]:vo["gk"] + 64] = inp["g_k"][0][None, :]
        vecs[:, vo["lng"]:vo["lng"] + 8] = inp["ln_v_g"][0].reshape(8, 128).T
        vecs[:, vo["lnb"]:vo["lnb"] + 8] = inp["ln_v_b"][0].reshape(8, 128).T
        posP = hf * HALF + np.arange(HALF)
        tabP = _rope_tab(posP).reshape(NT * 4, 128, 16).transpose(1, 0, 2).reshape(128, NT * 4 * 16)
        vecs[:, vo["ropeP"]:vo["ropeP"] + NT * 4 * 16] = tabP
        vecs[0:TS, vo["ropeS"]:vo["ropeS"] + 16] = _rope_tab(PAST_LEN + np.arange(TS))
        vecs[:, vo["ropeH"]:vo["ropeH"] + 16] = _rope_tab(np.maximum(hf * HALF - TH + np.arange(TH), 0))
        in_maps.append({
            "wsrc": wsrc, "xT": xT, "xsh": xsh, "xs": xs_t, "vecs": vecs, "wsT": wsT, "bsrow": bsrow,
            "cache_k": np.ascontiguousarray(inp["cache_swa_k"][0, c].reshape(128, 256)),
            "cache_v": np.ascontiguousarray(inp["cache_swa_v"][0, c].reshape(128, 256)),
        })
    res = run_bass_kernel_spmd(nc, in_maps, core_ids=list(range(8)))
    R = res.results
    y_p = np.empty((4, SEQ, D), np.float32)
    y_s = np.empty((8, TS, D), np.float32)
    kp = np.empty((1, 4, 128, 4, 64), np.float32)
    vp = np.empty((1, 4, 128, 4, 64), np.float32)
    ks = np.empty((1, 8, 128, 4, 64), np.float32)
    vs = np.empty((1, 8, 128, 4, 64), np.float32)
    gvs = np.empty((1, 8, TS, DA), np.float32)
    for c in range(8):
        b, hf = c // 2, c % 2
        if R[c]["yT"] is None:
            continue
        y_p[b, hf * HALF:(hf + 1) * HALF, :] = R[c]["yT"].T
        y_s[c] = R[c]["yTs"].T
        if hf == 1:
            kp[0, b] = R[c]["klast"].reshape(128, 4, 64)
            vp[0, b] = R[c]["vlast"].reshape(128, 4, 64)
        ks[0, c] = R[c]["ks"].reshape(128, 4, 64)
        vs[0, c] = R[c]["vs"].reshape(128, 4, 64)
        gvs[0, c] = R[c]["vnT"].T
    return (y_p, y_s, kp, vp, ks, vs, gvs)


def kernel(**inputs):
    return run(inputs, 8, 44)
```

```python
import os
import numpy as np
from contextlib import ExitStack
import concourse.bass as bass
import concourse.mybir as mybir
from concourse.bass_utils import run_bass_kernel_spmd

F32 = mybir.dt.float32
BF16 = mybir.dt.bfloat16
ALU = mybir.AluOpType
AF = mybir.ActivationFunctionType
AX = mybir.AxisListType

D = 2048
KC = 16
DA = 1024
NHD = 16
NKV = 4
HD = 64
TP = 512
TS = 32
TH = 128
T0 = TS + TH
UB = 32
RSL = 4
CONVG = 4
EPS = 1e-6
ROPE_THETA = 500000.0
PAST_LEN = 2048
ENGS = ("pe", "act", "dve", "pool", "sp")


class Buf:
    __slots__ = ("writers", "readers")

    def __init__(self):
        self.writers = {}
        self.readers = {}


class Ins:
    __slots__ = ("eng", "fn", "deps", "needed", "sem", "val", "is_dma", "key")


class Prog:
    def __init__(self):
        self.streams = {e: [] for e in ENGS}
        self.dma_counts = {}

    def add(self, eng, fn, reads=(), writes=(), dma_sem=None, extra_deps=()):
        ins = Ins()
        ins.eng = eng
        ins.fn = fn
        ins.needed = False
        ins.is_dma = dma_sem is not None
        ins.sem = None
        ins.val = None
        if ins.is_dma:
            c = self.dma_counts.get(id(dma_sem), 0) + 16
            self.dma_counts[id(dma_sem)] = c
            ins.sem = dma_sem
            ins.val = c
            ins.key = ("dma", id(dma_sem))
        else:
            ins.key = eng
        deps = {}
        is_dma = ins.is_dma

        def dep(p):
            if p is None or p is ins:
                return
            if (not p.is_dma) and (not is_dma) and p.eng == eng == "pe":
                return
            deps[id(p)] = p

        for b in reads:
            for p in b.writers.values():
                dep(p)
        for b in writes:
            for k, p in b.readers.items():
                dep(p)
            for k, p in b.writers.items():
                dep(p)
        for p in extra_deps:
            dep(p)
        ins.deps = list(deps.values())
        for p in ins.deps:
            p.needed = True
        for b in reads:
            b.readers[ins.key] = ins
        for b in writes:
            b.writers[ins.key] = ins
        self.streams[eng].append(ins)
        return ins

    def emit(self, block, sems):
        for e in ENGS:
            c = 0
            for ins in self.streams[e]:
                if ins.is_dma:
                    continue
                if ins.needed:
                    c += 1
                    ins.sem = sems[e]
                    ins.val = c
        streams = self.streams

        def run_stream(e, h):
            waited = {}
            for ins in streams[e]:
                need = {}
                for p in ins.deps:
                    k = id(p.sem)
                    if waited.get(k, 0) >= p.val:
                        continue
                    if k not in need or need[k][1] < p.val:
                        need[k] = (p.sem, p.val)
                for k, (s, v) in need.items():
                    h.wait_ge(s, v)
                    waited[k] = v
                r = ins.fn(h)
                if r is None:
                    continue
                if ins.is_dma:
                    r.then_inc(ins.sem, 16)
                elif ins.needed:
                    r.then_inc(ins.sem, 1)

        @block.tensor
        def _(h):
            run_stream("pe", h)

        @block.scalar
        def _(h):
            run_stream("act", h)

        @block.vector
        def _(h):
            run_stream("dve", h)

        @block.gpsimd
        def _(h):
            run_stream("pool", h)

        @block.sync
        def _(h):
            run_stream("sp", h)


def sections(NF):
    secs = [("ADA0", 768), ("F1A", 32 * NF), ("F1B", 16 * NF), ("ADA1", 768), ("MIXA", 256), ("MIXU", 128), ("MIXK", 64),
            ("MIXM", 768), ("MIXO", 256), ("ADA2", 768), ("F2A", 32 * NF), ("F2B", 16 * NF)]
    off = {}
    o = 0
    for n, c in secs:
        assert c % UB == 0
        off[n] = (o // UB, c // UB)
        o += c
    return secs, off, o // UB


def vec_layout(NT):
    o = {}
    c = 0
    for n, w in [("c", 32), ("bada", 144), ("g", 48), ("halo", 1), ("sink", 16), ("gq", 64), ("gk", 64),
                 ("lng", 8), ("lnb", 8), ("ropeP", 16 * NT * 4), ("ropeS", 16), ("ropeH", 16)]:
        o[n] = c
        c += w
    return o, c


def build(NT, NF):
    NFA = max(NF, 44)
    secs, soff, NU = sections(NF)
    vo, NV = vec_layout(NT)
    nc = bass.Bass("TRN2", target_bir_lowering=False)
    dt_in = lambda n, s, t=F32: nc.dram_tensor(n, s, t, kind="ExternalInput").ap()
    dt_out = lambda n, s: nc.dram_tensor(n, s, F32, kind="ExternalOutput").ap()
    wsrc = dt_in("wsrc", [NU, 128, UB * 128])
    xT_d = dt_in("xT", [D, NT * TP])
    xsh_d = dt_in("xsh", [D, TH])
    xs_d = dt_in("xs", [D, TS])
    vecs_d = dt_in("vecs", [128, NV])
    wsT_d = dt_in("wsT", [128, 8 * 128])
    bsrow_d = dt_in("bsrow", [1, 1024])
    ck_d = dt_in("cache_k", [128, 256])
    cv_d = dt_in("cache_v", [128, 256])
    wbf = nc.dram_tensor("wbf", [NU, 128, UB * 128], BF16, kind="Internal").ap()
    yT_d = dt_out("yT", [D, NT * TP])
    yTs_d = dt_out("yTs", [D, TS])
    klast_d = dt_out("klast", [128, 256])
    vlast_d = dt_out("vlast", [128, 256])
    ks_d = dt_out("ks", [128, 256])
    vs_d = dt_out("vs", [128, 256])
    vnT_d = dt_out("vnT", [DA, TS])

    with ExitStack() as es:
        def sb(name, shape, dt):
            return es.enter_context(nc.sbuf_tensor(name, shape, dt))

        def sem(name):
            return es.enter_context(nc.semaphore(name))

        P = Prog()
        xT = sb("xTs", [128, KC, TP], F32)
        hT = sb("hTs", [128, KC, TP], BF16)
        gT = sb("gTs", [128, NFA, TP], BF16)
        ring = sb("ring", [128, RSL, UB * 128], BF16)
        vecs = sb("vecs_s", [128, NV], F32)
        gv = sb("gv", [128, 4, 1024], F32)
        qkf = sb("qkf", [128, 256], F32)
        sqnT = sb("sqnT", [128, 2, 1280], F32)
        qkb = sb("qkb", [128, 1280], BF16)
        vaf = sb("vaf", [128, 256], F32)
        ropet = sb("ropet", [128, 4, 160], F32)
        kTr = sb("kTr", [128, 4, 512], BF16)
        vaDr = sb("vaDr", [128, 4, 512], BF16)
        kTc = sb("kTc", [128, 512], BF16)
        vaDc = sb("vaDc", [128, 512], BF16)
        pT = sb("pTs", [128, 2, 2, 512], BF16)
        pTS = sb("pTSs", [128, 2, 2, 128], BF16)
        NTMP = 8
        tmp = sb("tmp", [128, NTMP, TP], F32)
        sqb = sb("sqb", [128, 2, TP], BF16)
        rstd = sb("rstd", [128, TP], F32)
        small = sb("small", [128, 64], F32)
        modT = sb("modT", [128, 144, 2], F32)
        scal = sb("scal", [128, 288], F32)
        scT = sb("scT", [128, 32], BF16)
        esink = sb("esink", [128, 16], F32)
        identb = sb("identb", [128, 128], BF16)
        identf = sb("identf", [128, 128], F32)
        onesb = sb("onesb", [128, 128], BF16)
        wsTb = sb("wsTb", [128, 8, 128], BF16)
        Cg = sb("Cg", [128, 8, 128], F32)
        CgS = sb("CgS", [128, 8, 32], F32)
        vnTs = sb("vnTs", [128, 8, 32], F32)
        epsc = sb("epsc", [128, 1], F32)
        banks = [es.enter_context(nc.psum_tensor(f"bank{i}", [128, 512], F32)) for i in range(8)]
        bankB = [Buf() for _ in range(8)]
        bank_ctr = [0]

        def alloc():
            b = bank_ctr[0] % 8
            bank_ctr[0] += 1
            return banks[b], bankB[b]

        s_eng = {e: sem("s_" + e) for e in ("pe", "act", "dve", "pool", "sp")}
        s_ring = [sem(f"s_ring{i}") for i in range(RSL)]
        s_wb = [sem(f"s_wb{i}") for i in range(RSL)]
        s_ringp = [sem(f"s_ringp{i}") for i in range(RSL)]
        s_x = [sem(f"s_x{i}") for i in range(KC)]
        s_tmp = [sem(f"s_tmp{i}") for i in range(NTMP)]
        s_misc = [sem(f"s_misc{i}") for i in range(16)]
        blk = es.enter_context(nc.Block())

        xB = [Buf() for _ in range(KC)]
        hB = [Buf() for _ in range(KC)]
        gB = [Buf() for _ in range(NFA)]
        ringB = [Buf() for _ in range(RSL)]
        tmpB = [Buf() for _ in range(NTMP)]
        B = {k: Buf() for k in ("vecs", "gv0", "gv1", "gv2", "gv3", "qkf", "sqn0", "sqn1", "qkb", "kT3", "vaD3", "vaf", "ropet", "kTc", "vaDc",
                                "pT0", "pT1", "pTS0", "pTS1", "sqb0", "sqb1", "rstd", "small", "modT", "scal", "scT", "esink",
                                "identb", "identf", "onesb", "wsTb", "Cg", "CgS", "vnTs", "epsc", "kT0", "kT1", "kT2",
                                "vaD0", "vaD1", "vaD2", "wsTf", "bsrow", "ckf", "cvf", "ss", "dram_out")}
        gvB = [B["gv0"], B["gv1"], B["gv2"], B["gv3"]]
        kTB = [B["kT0"], B["kT1"], B["kT2"], B["kT3"]]
        vaDB = [B["vaD0"], B["vaD1"], B["vaD2"], B["vaD3"]]
        tmp_ctr = [0]

        def talloc():
            i = tmp_ctr[0] % NTMP
            tmp_ctr[0] += 1
            return i

        def A(eng, meth, *args, reads=(), writes=(), **kw):
            ins = P.add(eng, lambda e: getattr(e, meth)(*args, **kw), reads=reads, writes=writes)
            if os.environ.get("KDBG"):
                ins.fn.__dict__["lbl"] = (meth, [str(getattr(a, "ap", a)) + "@" + str(getattr(a, "offset", "")) for a in args],
                                          {k: (str(v.ap) + "@" + str(v.offset)) if hasattr(v, "ap") else v for k, v in kw.items()})
            return ins

        def DMA(eng, out, in_, s, reads=(), writes=(), extra=()):
            return P.add(eng, lambda e: e.dma_start(out=out, in_=in_), reads=reads, writes=writes, dma_sem=s,
                         extra_deps=extra)

        out_dmas = []

        non_ada = [u for n, c in secs if not n.startswith("ADA") for u in range(soff[n][0], soff[n][0] + soff[n][1])]
        NPASS = NT + 2
        pass_secs = {0: ["ADA0", "F1A", "F1B", "ADA1", "MIXA", "MIXK"],
                     1: ["F1A", "F1B", "MIXA", "MIXU", "MIXK", "MIXM", "MIXO", "ADA2", "F2A", "F2B"]}
        for t in range(2, NPASS):
            pass_secs[t] = ["F1A", "F1B", "MIXA", "MIXU", "MIXK", "MIXM", "MIXO", "F2A", "F2B"]
        useq = []
        pos_of = {}
        for t in range(NPASS):
            for n in pass_secs[t]:
                for u in range(soff[n][0], soff[n][0] + soff[n][1]):
                    pos_of[(t, u)] = len(useq)
                    useq.append(u)
        NSEQ = len(useq)
        non_ada_set = set(non_ada)
        wbB = [Buf() for _ in range(NU)]
        rs = {"next": 0, "released": 0}
        seen = set()

        def ring_advance():
            while rs["next"] < NSEQ and rs["next"] < rs["released"] + RSL:
                k = rs["next"]
                u = useq[k]
                sl = k % RSL
                if u not in seen:
                    seen.add(u)
                    DMA("pool", ring[:, sl, :], wsrc[u], s_ringp[sl], writes=[ringB[sl]])
                    if u in non_ada_set:
                        DMA("sp", wbf[u], ring[:, sl, :], s_wb[sl], reads=[ringB[sl]], writes=[wbB[u]])
                else:
                    DMA("sp", ring[:, sl, :], wbf[u], s_ring[sl], reads=[wbB[u]], writes=[ringB[sl]])
                rs["next"] += 1

        def ring_release():
            rs["released"] += 1
            ring_advance()

        class Cur:
            def __init__(self, pss, sec):
                self.pss = pss
                self.u0 = soff[sec][0]
                self.i = 0
                self.cur_unit = None

            def _touch(self, bi):
                u = self.u0 + bi // UB
                k = pos_of[(self.pss, u)]
                assert k >= rs["released"], (k, rs)
                assert k < rs["next"], ("unit not loaded", k, rs)
                return k % RSL, bi % UB

            def blk(self, bi, n=1):
                sl, o = self._touch(bi)
                return ring[:, sl, o * 128:(o + n) * 128], ringB[sl]

        DMA("sp", vecs[:], vecs_d, s_misc[0], writes=[B["vecs"]])
        wsTf = gv[:, 0, :]
        bsr = gv[0:1, 1, :]
        ckf = gv[:, 2, 0:256]
        cvf = gv[:, 2, 256:512]
        DMA("sp", wsTf, wsT_d, s_misc[1], writes=[gvB[0]])
        DMA("sp", bsr, bsrow_d, s_misc[2], writes=[gvB[1]])
        DMA("sp", ckf, ck_d, s_misc[3], writes=[gvB[2]])
        DMA("sp", cvf, cv_d, s_misc[4], writes=[gvB[2]])
        A("pool", "memset", identb[:], 0.0, writes=[B["identb"]])
        A("pool", "affine_select", out=identb[:], in_=identb[:], pattern=[[-1, 128]], compare_op=ALU.not_equal,
          fill=1.0, base=0, channel_multiplier=1, reads=[B["identb"]], writes=[B["identb"]])
        A("pool", "memset", identf[:], 0.0, writes=[B["identf"]])
        A("pool", "affine_select", out=identf[:], in_=identf[:], pattern=[[-1, 128]], compare_op=ALU.not_equal,
          fill=1.0, base=0, channel_multiplier=1, reads=[B["identf"]], writes=[B["identf"]])
        A("pool", "memset", onesb[:], 1.0, writes=[B["onesb"]])
        A("pool", "memset", epsc[:], EPS, writes=[B["epsc"]])
        A("pool", "memset", pT[:], 0.0, writes=[B["pT0"], B["pT1"]])
        A("pool", "memset", gv[:, 3, :], 1.0, writes=[gvB[3]])
        ring_advance()

        A("act", "activation", out=scT[:], in_=vecs[:, vo["c"]:vo["c"] + 32], func=AF.Silu, reads=[B["vecs"]],
          writes=[B["scT"]])
        A("act", "activation", out=esink[:], in_=vecs[:, vo["sink"]:vo["sink"] + 16], func=AF.Exp, reads=[B["vecs"]],
          writes=[B["esink"]])
        wsTf3 = wsTf.rearrange("p (g i) -> p g i", g=8)
        A("dve", "memset", wsTf3[64:128, :, 0:64], 0.0, reads=[], writes=[gvB[0]])
        A("dve", "tensor_copy", out=wsTb[:], in_=wsTf3, reads=[gvB[0]], writes=[B["wsTb"]])
        onesf = gv[:, 3, 0:128]
        lnb = lambda g: vecs[:, vo["lnb"] + g:vo["lnb"] + g + 1]
        lng = lambda g: vecs[:, vo["lng"] + g:vo["lng"] + g + 1]
        for half in range(2):
            bk, bkB = alloc()
            A("pe", "matmul", bk[:, :], lhsT=onesf, rhs=wsTf[:, half * 512:(half + 1) * 512], start=True, stop=True,
              reads=[gvB[3], gvB[0]], writes=[bkB])
            bk2, bk2B = alloc()
            A("pe", "matmul", bk2[:, :], lhsT=gv[0:1, 3, 0:128], rhs=bsr[:, half * 512:(half + 1) * 512], start=True,
              stop=True, reads=[gvB[3], gvB[1]], writes=[bk2B])
            ti = talloc()
            A("act", "activation", out=tmp[:, ti, :], in_=bk2[:, :], func=AF.Copy, reads=[bk2B], writes=[tmpB[ti]])
            for gg in range(4):
                g = half * 4 + gg
                A("dve", "scalar_tensor_tensor", out=Cg[:, g, :], in0=bk[:, gg * 128:(gg + 1) * 128], scalar=lnb(g),
                  in1=tmp[:, ti, gg * 128:(gg + 1) * 128], op0=ALU.mult, op1=ALU.add,
                  reads=[bkB, tmpB[ti], B["vecs"]], writes=[B["Cg"]])
        bk, bkB = alloc()
        bk2, bk2B = alloc()
        bsr3 = bsr.rearrange("p (g i) -> p g i", g=8)
        for g in range(8):
            A("pe", "matmul", bk[:, g * 32:(g + 1) * 32], lhsT=gv[0:32, 3, 0:128], rhs=wsTf3[0:32, g, 0:32], start=True,
              stop=True, reads=[gvB[3], gvB[0]], writes=[bkB])
            A("pe", "matmul", bk2[:, g * 32:(g + 1) * 32], lhsT=gv[0:1, 3, 0:128], rhs=bsr3[:, g, 0:32], start=True,
              stop=True, reads=[gvB[3], gvB[1]], writes=[bk2B])
        ti = talloc()
        A("act", "activation", out=tmp[:, ti, 0:256], in_=bk2[:, 0:256], func=AF.Copy, reads=[bk2B], writes=[tmpB[ti]])
        for g in range(8):
            A("dve", "scalar_tensor_tensor", out=CgS[:, g, :], in0=bk[:, g * 32:(g + 1) * 32], scalar=lnb(g),
              in1=tmp[:, ti, g * 32:(g + 1) * 32], op0=ALU.mult, op1=ALU.add, reads=[bkB, tmpB[ti], B["vecs"]],
              writes=[B["CgS"]])
        A("dve", "tensor_copy", out=qkb[:, 0:256], in_=ckf, reads=[gvB[2]], writes=[B["qkb"]])
        bk, bkB = alloc()
        bkb = bk[:, :].bitcast(BF16)
        for h in range(NKV):
            A("pe", "transpose", bkb[0:64, h * 128:(h + 1) * 128], qkb[:, h * 64:(h + 1) * 64], identb[:],
              reads=[B["qkb"], B["identb"]], writes=[bkB])
        A("dve", "tensor_copy", out=kTc[0:64, :], in_=bkb[0:64, 0:512], reads=[bkB], writes=[B["kTc"]])
        vaDc3 = vaDc[:, :].rearrange("p (h d) -> p h d", h=4)
        cv3 = cvf.rearrange("p (h d) -> p h d", h=4)
        A("dve", "tensor_copy", out=vaDc3[:, :, 0:64], in_=cv3, reads=[gvB[2]], writes=[B["vaDc"]])
        A("dve", "tensor_copy", out=vaDc3[:, :, 64:128], in_=cv3, reads=[gvB[2]], writes=[B["vaDc"]])
        out_dmas.append(DMA("act", ks_d[0:96, :], ck_d[32:128, :], s_misc[5]))
        out_dmas.append(DMA("act", vs_d[0:96, :], cv_d[32:128, :], s_misc[6]))

        def sc(s, r, kind, dc):
            c = ((s * 2 + r) * 3 + kind) * 16 + dc
            return scal[:, c:c + 1]

        def ada(pss, s):
            cur = Cur(pss, "ADA%d" % s)
            bk, bkB = alloc()
            for ci in range(48):
                for kc in range(KC):
                    bi = ci * KC + kc
                    w, wB = cur.blk(bi)
                    A("pe", "matmul", bk[:, 2 * ci:2 * ci + 2], lhsT=w, rhs=scT[:, 2 * kc:2 * kc + 2], start=(kc == 0),
                      stop=(kc == KC - 1), reads=[wB, B["scT"]], writes=[bkB])
                    if bi % UB == UB - 1:
                        ring_release()
            bo = vo["bada"] + s * 48
            A("dve", "tensor_tensor", out=modT[:, s * 48:(s + 1) * 48, :],
              in0=bk[:, 0:96].rearrange("p (c r) -> p c r", r=2),
              in1=vecs[:, bo:bo + 48].unsqueeze(2).to_broadcast([128, 48, 2]), op=ALU.add,
              reads=[bkB, B["vecs"]], writes=[B["modT"]])
            for r in range(2):
                base = ((s * 2 + r) * 3) * 16
                A("dve", "scalar_tensor_tensor", out=scal[:, base:base + 16], in0=modT[:, s * 48 + 16:s * 48 + 32, r],
                  scalar=1.0, in1=vecs[:, vo["g"] + s * 16:vo["g"] + s * 16 + 16], op0=ALU.add, op1=ALU.mult,
                  reads=[B["modT"], B["vecs"]], writes=[B["scal"]])
                A("dve", "tensor_copy", out=scal[:, base + 16:base + 32], in_=modT[:, s * 48:s * 48 + 16, r],
                  reads=[B["modT"]], writes=[B["scal"]])
                A("dve", "tensor_scalar", out=scal[:, base + 32:base + 48], in0=modT[:, s * 48 + 32:s * 48 + 48, r],
                  scalar1=(1.0 if s == 1 else 0.5), scalar2=None, op0=ALU.mult, reads=[B["modT"]],
                  writes=[B["scal"]])

        def norm(s, T, segs):
            bk, bkB = alloc()
            for dc in range(KC):
                q = dc % 2
                A("act", "activation", out=sqb[:, q, 0:T], in_=xT[:, dc, 0:T], func=AF.Square, reads=[xB[dc]],
                  writes=[B["sqb%d" % q]])
                A("pe", "matmul", bk[:, 0:T], lhsT=onesb[:], rhs=sqb[:, q, 0:T], start=(dc == 0), stop=(dc == KC - 1),
                  reads=[B["onesb"], B["sqb%d" % q]], writes=[bkB])
            A("act", "activation", out=rstd[:, 0:T], in_=bk[:, 0:T], func=AF.Sqrt, scale=1.0 / D, bias=epsc[:, 0:1],
              reads=[bkB, B["epsc"]], writes=[B["rstd"]])
            A("dve", "reciprocal", out=rstd[:, 0:T], in_=rstd[:, 0:T], reads=[B["rstd"]], writes=[B["rstd"]])
            for dc in range(KC):
                ti = talloc()
                A("dve", "tensor_tensor", out=tmp[:, ti, 0:T], in0=xT[:, dc, 0:T], in1=rstd[:, 0:T], op=ALU.mult,
                  reads=[xB[dc], B["rstd"]], writes=[tmpB[ti]])
                for (c0, c1, r) in segs:
                    A("act", "activation", out=hT[:, dc, c0:c1], in_=tmp[:, ti, c0:c1], func=AF.Identity,
                      scale=sc(s, r, 0, dc), bias=sc(s, r, 1, dc), reads=[tmpB[ti], B["scal"]], writes=[hB[dc]])

        def ffn(pss, s, T, segs, nameA, nameB, store=None):
            norm(s, T, segs)
            cur = Cur(pss, nameA)
            for f in range(NF):
                ba, baB = alloc()
                bb, bbB = alloc()
                for wi, (bk, bkB) in enumerate(((ba, baB), (bb, bbB))):
                    for kc in range(KC):
                        w, wB = cur.blk(f * 32 + wi * 16 + kc)
                        A("pe", "matmul", bk[:, 0:T], lhsT=w, rhs=hT[:, kc, 0:T], start=(kc == 0), stop=(kc == KC - 1),
                          reads=[wB, hB[kc]], writes=[bkB])
                ring_release()
                ti = talloc()
                A("act", "activation", out=tmp[:, ti, 0:T], in_=ba[:, 0:T], func=AF.Silu, reads=[baB],
                  writes=[tmpB[ti]])
                A("dve", "tensor_tensor", out=gT[:, f, 0:T], in0=bb[:, 0:T], in1=tmp[:, ti, 0:T], op=ALU.mult,
                  reads=[bbB, tmpB[ti]], writes=[gB[f]])
            cur = Cur(pss, nameB)
            bi = 0
            for dc in range(KC):
                bk, bkB = alloc()
                for f in range(NF):
                    w, wB = cur.blk(bi)
                    A("pe", "matmul", bk[:, 0:T], lhsT=w, rhs=gT[:, f, 0:T], start=(f == 0), stop=(f == NF - 1),
                      reads=[wB, gB[f]], writes=[bkB])
                    if bi % UB == UB - 1:
                        ring_release()
                    bi += 1
                if store is None:
                    for (c0, c1, r) in segs:
                        A("dve", "scalar_tensor_tensor", out=xT[:, dc, c0:c1], in0=bk[:, c0:c1], scalar=sc(s, r, 2, dc),
                          in1=xT[:, dc, c0:c1], op0=ALU.mult, op1=ALU.add, reads=[bkB, xB[dc], B["scal"]],
                          writes=[xB[dc]])
                else:
                    (c0, c1, r) = segs[0]
                    ti = talloc()
                    A("dve", "scalar_tensor_tensor", out=tmp[:, ti, 0:T], in0=bk[:, 0:T], scalar=sc(s, r, 2, dc),
                      in1=xT[:, dc, 0:T], op0=ALU.mult, op1=ALU.add, reads=[bkB, xB[dc], B["scal"]],
                      writes=[tmpB[ti]])
                    out_dmas.append(DMA("act", store(dc), tmp[:, ti, 0:T], s_tmp[ti], reads=[tmpB[ti]]))

        uT = lambda c: gT[:, c, :]
        oTt = lambda t: gT[:, 8 + t, :]
        yTt = lambda dc: gT[:, 16 + dc, :]
        vnb = lambda b: gT[:, 32 + 2 * b:34 + 2 * b, :].rearrange("p a c -> p (a c)")
        vnbB = lambda b: [gB[32 + 2 * b], gB[33 + 2 * b]]
        qTv = lambda b: gT[:, 36 + 4 * b:40 + 4 * b, :].rearrange("p a c -> p (a c)")
        qTB = lambda b: [gB[36 + 4 * b + i] for i in range(4)]
        st = {"kv": 0, "blk": 0}

        def rope_ap(kind, jg, n):
            if kind == "P":
                o = vo["ropeP"] + jg * 16
            elif kind == "S":
                o = vo["ropeS"]
            else:
                o = vo["ropeH"]
            return vecs[0:n, o:o + 8], vecs[0:n, o + 8:o + 16]

        def attention(nq, qslot, tiles, c0, halo=False):
            qT_ = qTv(qslot)
            nt_ = len(tiles)
            state = {}

            def s1(h):
                pset = h % 2
                pTb = B["pT%d" % pset] if nq == 128 else B["pTS%d" % pset]
                pviews = []
                for ti_, (kTa, kTb_, vDa, vDb_, nk, mask) in enumerate(tiles):
                    bk, bkB = alloc()
                    A("pe", "matmul", bk[0:nk, 0:4 * nq], lhsT=kTa[0:64, h * nk:(h + 1) * nk],
                      rhs=qT_[0:64, h * 4 * nq:(h + 1) * 4 * nq], start=True, stop=True, reads=[kTb_] + qTB(qslot),
                      writes=[bkB])
                    if nq == 128:
                        pv = pT[:, pset, ti_, :]
                    else:
                        pv = pTS[:, pset, ti_, :]
                    pviews.append(pv)
                    s3 = bk[:, 0:4 * nq].rearrange("p (g q) -> p g q", g=4)
                    p3 = pv[:, 0:4 * nq].rearrange("p (g q) -> p g q", g=4)
                    kw = {}
                    if mask == "full":
                        regs = [(0, nk, 0, nq)]
                    elif mask == "prev":
                        regs = [(0, 64, 0, 64), (64, 128, 0, 128)]
                    else:
                        regs = [(0, 64, 0, 128), (64, 128, 64, 128)]
                    for (p0, p1, q0, q1) in regs:
                        bias = vecs[p0:p1, vo["halo"]:vo["halo"] + 1] if (halo and mask == "prev") else 0.0
                        A("act", "activation", out=p3[p0:p1, :, q0:q1], in_=s3[p0:p1, :, q0:q1], func=AF.Exp,
                          scale=HD ** -0.5, bias=bias, reads=[bkB, B["vecs"]], writes=[pTb])
                state[h] = (pTb, pviews)

            def s2(h):
                pTb, pviews = state[h]
                bo, boB = alloc()
                bd, bdB = alloc()
                for ti_, (kTa, kTb_, vDa, vDb_, nk, mask) in enumerate(tiles):
                    A("pe", "matmul", bo[:, 0:4 * nq], lhsT=vDa[0:nk, h * 128:(h + 1) * 128],
                      rhs=pviews[ti_][0:nk, 0:4 * nq], start=(ti_ == 0), stop=(ti_ == nt_ - 1), reads=[vDb_, pTb],
                      writes=[boB])
                for ti_, (kTa, kTb_, vDa, vDb_, nk, mask) in enumerate(tiles):
                    A("pe", "matmul", bd[:, 0:4 * nq], lhsT=onesb[0:nk, :], rhs=pviews[ti_][0:nk, 0:4 * nq],
                      start=(ti_ == 0), stop=(ti_ == nt_ - 1), reads=[B["onesb"], pTb], writes=[bdB])
                ti = talloc()
                r3 = tmp[:, ti, 0:4 * nq].rearrange("p (g q) -> p g q", g=4)
                for g in range(4):
                    A("act", "activation", out=tmp[:, ti, g * nq:(g + 1) * nq], in_=bd[:, g * nq:(g + 1) * nq],
                      func=AF.Identity, scale=1.0, bias=esink[:, 4 * h + g:4 * h + g + 1], reads=[bdB, B["esink"]],
                      writes=[tmpB[ti]])
                A("dve", "reciprocal", out=tmp[:, ti, 0:4 * nq], in_=tmp[:, ti, 0:4 * nq], reads=[tmpB[ti]],
                  writes=[tmpB[ti]])
                o3 = bo[:, 0:4 * nq].rearrange("p (g q) -> p g q", g=4)
                for par in range(2):
                    p0 = par * 64
                    A("dve", "tensor_tensor", out=gT[p0:p0 + 64, 8 + 2 * h:8 + 2 * h + 2, c0:c0 + nq],
                      in0=o3[p0:p0 + 64, par::2, :], in1=r3[p0:p0 + 64, par::2, :], op=ALU.mult,
                      reads=[boB, tmpB[ti]], writes=[gB[8 + 2 * h], gB[8 + 2 * h + 1]])

            for h in range(NKV + 1):
                if h < NKV:
                    s1(h)
                if h >= 1:
                    s2(h - 1)

        def spatial_gate(n, vslot, c0, is_s):
            for half in range(2):
                bk, bkB = alloc()
                for gg in range(4):
                    g = half * 4 + gg
                    A("pe", "matmul", bk[:, gg * n:(gg + 1) * n], lhsT=vnb(vslot)[0:n, g * 128:(g + 1) * 128],
                      rhs=wsTb[0:n, g, 0:n], start=True, stop=True, reads=vnbB(vslot) + [B["wsTb"]], writes=[bkB])
                for gg in range(4):
                    g = half * 4 + gg
                    ti = talloc()
                    Cv = CgS[:, g, 0:n] if is_s else Cg[:, g, 0:n]
                    A("dve", "scalar_tensor_tensor", out=tmp[:, ti, 0:n], in0=bk[:, gg * n:(gg + 1) * n],
                      scalar=lng(g), in1=Cv, op0=ALU.mult, op1=ALU.add,
                      reads=[bkB, B["vecs"], B["Cg"], B["CgS"]], writes=[tmpB[ti]])
                    A("dve", "tensor_tensor", out=gT[:, g, c0:c0 + n], in0=gT[:, g, c0:c0 + n], in1=tmp[:, ti, 0:n],
                      op=ALU.mult, reads=[gB[g], tmpB[ti]], writes=[gB[g]])

        def mixer2(pss, T, Tm, segs, mseg, blocks, first_p, last):
            norm(1, T, segs)
            cur = Cur(pss, "MIXA")
            full_blocks = [b for b in blocks if b["full"]]
            for i, b in enumerate(full_blocks):
                b["gi"] = i
                b["vslot"] = i % 2
            for cg in range(2):
                groups = [full_blocks[i:i + 3] for i in range(0, len(full_blocks), 3)]
                for grp in groups:
                    bks = [alloc() for _ in grp]
                    for kc in range(KC):
                        w, wB = cur.blk(cg * 64 + kc * 4, 4)
                        for b, (bk, bkB) in zip(grp, bks):
                            A("pe", "matmul", bk[0:b["n"], :], lhsT=hT[:, kc, b["c0"]:b["c0"] + b["n"]], rhs=w,
                              start=(kc == 0), stop=(kc == KC - 1), reads=[wB, hB[kc]], writes=[bkB])
                    for b, (bk, bkB) in zip(grp, bks):
                        n, gi = b["n"], b["gi"]
                        A("act", "activation", out=gv[0:n, gi, cg * 512:(cg + 1) * 512], in_=bk[0:n, :], func=AF.Gelu,
                          reads=[bkB], writes=[gvB[gi]])
                ring_release()
                ring_release()
            for cg in range(2, 4):
                bl = full_blocks
                groups = [bl[i:i + 3] for i in range(0, len(bl), 3)]
                for grp in groups:
                    bks = [alloc() for _ in grp]
                    for kc in range(KC):
                        w, wB = cur.blk(cg * 64 + kc * 4, 4)
                        for b, (bk, bkB) in zip(grp, bks):
                            A("pe", "matmul", bk[0:b["n"], :], lhsT=hT[:, kc, b["c0"]:b["c0"] + b["n"]], rhs=w,
                              start=(kc == 0), stop=(kc == KC - 1), reads=[wB, hB[kc]], writes=[bkB])
                    for b, bkp in zip(grp, bks):
                        b["cg%d" % cg] = bkp
                        if cg < 4:
                            n, gi = b["n"], b["gi"]
                            A("act", "activation", out=qraw[0:n, gi, (cg - 2) * 512:(cg - 1) * 512], in_=bkp[0][0:n, :],
                              func=AF.Copy, reads=[bkp[1]], writes=qrawBs(gi))
                ring_release()
                ring_release()
            return full_blocks

        qraw_all = gT[:, 16:32, :].rearrange("p a c -> p (a c)").bitcast(F32)

        class _QR:
            def __getitem__(self, key):
                p, gi, c = key
                if isinstance(c, slice) and c.start is None:
                    return qraw_all[p, gi * 1024:(gi + 1) * 1024]
                return qraw_all[p, gi * 1024 + c.start:gi * 1024 + c.stop]
        qraw = _QR()

        qrawBs = lambda gi: [gB[16 + 4 * gi + i] for i in range(4)]

        def qkA(b, kslot, want_q, sbi, out_v=None):
            n, kind, jg = b["n"], b["kind"], b["jg"]
            bkv = b["cg4"]
            lo = 0 if want_q else 1024
            W = 1280 - lo
            nh = W // 64
            A("act", "activation", out=qkf[0:n, 0:256], in_=bkv[0][0:n, 0:256], func=AF.Copy, reads=[bkv[1]],
              writes=[B["qkf"]])
            vd3 = vaDr[0:n, kslot, :].rearrange("p (h d) -> p h d", h=4)
            va3 = bkv[0][0:n, 256:512].rearrange("p (h d) -> p h d", h=4)
            A("act", "activation", out=vd3[:, :, 0:64], in_=va3, func=AF.Copy, reads=[bkv[1]], writes=[vaDB[kslot]])
            A("act", "activation", out=vd3[:, :, 64:128], in_=va3, func=AF.Copy, reads=[bkv[1]], writes=[vaDB[kslot]])
            if out_v is not None:
                A("act", "activation", out=vaf[0:n, :], in_=bkv[0][0:n, 256:512], func=AF.Copy, reads=[bkv[1]],
                  writes=[B["vaf"]])
                out_dmas.append(DMA("act", out_v, vaf[0:n, :], s_misc[7], reads=[B["vaf"]]))

            def src(c0_, c1_):
                return None
            if want_q:
                gi = b["gi"]
                A("act", "activation", out=sqnT[0:n, sbi, 0:1024], in_=qraw[0:n, gi, :], func=AF.Square,
                  reads=qrawBs(gi), writes=[B["sqn%d" % sbi]])
            A("act", "activation", out=sqnT[0:n, sbi, 1024:1280], in_=qkf[0:n, 0:256], func=AF.Square,
              reads=[B["qkf"]], writes=[B["sqn%d" % sbi]])
            A("dve", "tensor_reduce", out=small[0:n, 0:nh], in_=sqnT[0:n, sbi, lo:1280].rearrange("p (h d) -> p h d", d=64),
              axis=AX.X, op=ALU.add, reads=[B["sqn%d" % sbi]], writes=[B["small"]])
            A("act", "activation", out=small[0:n, 0:nh], in_=small[0:n, 0:nh], func=AF.Sqrt, scale=1.0 / HD,
              bias=epsc[0:n, 0:1], reads=[B["small"], B["epsc"]], writes=[B["small"]])
            A("dve", "reciprocal", out=small[0:n, 0:nh], in_=small[0:n, 0:nh], reads=[B["small"]],
              writes=[B["small"]])
            if want_q:
                gi = b["gi"]
                A("pool", "tensor_tensor", out=sqnT[0:n, sbi, 0:1024].rearrange("p (h d) -> p h d", d=64),
                  in0=qraw[0:n, gi, :].rearrange("p (h d) -> p h d", d=64),
                  in1=small[0:n, 0:16].unsqueeze(2).to_broadcast([n, 16, 64]), op=ALU.mult,
                  reads=qrawBs(gi) + [B["small"]], writes=[B["sqn%d" % sbi]])
                A("pool", "tensor_tensor", out=sqnT[0:n, sbi, 0:1024].rearrange("p (h d) -> p h d", d=64),
                  in0=sqnT[0:n, sbi, 0:1024].rearrange("p (h d) -> p h d", d=64),
                  in1=vecs[0:n, vo["gq"]:vo["gq"] + 64].unsqueeze(1).to_broadcast([n, 16, 64]), op=ALU.mult,
                  reads=[B["sqn%d" % sbi], B["vecs"]], writes=[B["sqn%d" % sbi]])
            A("pool", "tensor_tensor", out=sqnT[0:n, sbi, 1024:1280].rearrange("p (h d) -> p h d", d=64),
              in0=qkf[0:n, 0:256].rearrange("p (h d) -> p h d", d=64),
              in1=small[0:n, nh - 4:nh].unsqueeze(2).to_broadcast([n, 4, 64]), op=ALU.mult,
              reads=[B["qkf"], B["small"]], writes=[B["sqn%d" % sbi]])
            A("pool", "tensor_tensor", out=sqnT[0:n, sbi, 1024:1280].rearrange("p (h d) -> p h d", d=64),
              in0=sqnT[0:n, sbi, 1024:1280].rearrange("p (h d) -> p h d", d=64),
              in1=vecs[0:n, vo["gk"]:vo["gk"] + 64].unsqueeze(1).to_broadcast([n, 4, 64]), op=ALU.mult,
              reads=[B["sqn%d" % sbi], B["vecs"]], writes=[B["sqn%d" % sbi]])

        def qkB(b, kslot, want_q, sbi, out_k=None):
            n, kind, jg = b["n"], b["kind"], b["jg"]
            lo = 0 if want_q else 1024
            nh = (1280 - lo) // 64
            X = sqnT[0:n, sbi, lo:1280].rearrange("p (h d) -> p h d", d=64)
            x1, x2 = X[:, :, 0:8], X[:, :, 8:16]
            cosA, sinA = rope_ap(kind, jg, n)
            cb = cosA.unsqueeze(1).to_broadcast([n, nh, 8])
            sbb = sinA.unsqueeze(1).to_broadcast([n, nh, 8])
            rt = lambda i: ropet[0:n, i, 0:nh * 8].rearrange("p (h d) -> p h d", d=8)
            rd = [B["sqn%d" % sbi], B["vecs"]]
            A("pool", "tensor_tensor", out=rt(0), in0=x1, in1=cb, op=ALU.mult, reads=rd, writes=[B["ropet"]])
            A("pool", "tensor_tensor", out=rt(1), in0=x2, in1=sbb, op=ALU.mult, reads=rd, writes=[B["ropet"]])
            A("pool", "tensor_tensor", out=rt(2), in0=x2, in1=cb, op=ALU.mult, reads=rd, writes=[B["ropet"]])
            A("pool", "tensor_tensor", out=rt(3), in0=x1, in1=sbb, op=ALU.mult, reads=rd, writes=[B["ropet"]])
            A("pool", "tensor_tensor", out=x1, in0=rt(0), in1=rt(1), op=ALU.subtract, reads=[B["ropet"]],
              writes=[B["sqn%d" % sbi]])
            A("pool", "tensor_tensor", out=x2, in0=rt(2), in1=rt(3), op=ALU.add, reads=[B["ropet"]],
              writes=[B["sqn%d" % sbi]])
            A("pool", "tensor_copy", out=qkb[0:n, lo:1280], in_=sqnT[0:n, sbi, lo:1280], reads=[B["sqn%d" % sbi]],
              writes=[B["qkb"]])
            if out_k is not None:
                out_dmas.append(DMA("act", out_k, sqnT[0:n, sbi, 1024:1280], s_misc[8], reads=[B["sqn%d" % sbi]]))
            qslot = None
            if want_q:
                qslot = st["blk"] % 2
                st["blk"] += 1
                for half in range(2):
                    bk, bkB = alloc()
                    bkb_ = bk[:, :].bitcast(BF16)
                    for hh in range(8):
                        h = half * 8 + hh
                        A("pe", "transpose", bkb_[0:64, hh * n:(hh + 1) * n], qkb[0:n, h * 64:(h + 1) * 64],
                          identb[0:n, 0:n], reads=[B["qkb"], B["identb"]], writes=[bkB])
                    A("act", "activation", out=qTv(qslot)[0:64, half * 8 * n:(half + 1) * 8 * n],
                      in_=bkb_[0:64, 0:8 * n], func=AF.Copy, reads=[bkB], writes=qTB(qslot))
            bk, bkB = alloc()
            bkb_ = bk[:, :].bitcast(BF16)
            for h in range(NKV):
                A("pe", "transpose", bkb_[0:64, h * n:(h + 1) * n], qkb[0:n, 1024 + h * 64:1024 + (h + 1) * 64],
                  identb[0:n, 0:n], reads=[B["qkb"], B["identb"]], writes=[bkB])
            A("act", "activation", out=kTr[0:64, kslot, 0:4 * n], in_=bkb_[0:64, 0:4 * n], func=AF.Copy, reads=[bkB],
              writes=[kTB[kslot]])
            return qslot

        def v_process2(b):
            n, gi, vslot = b["n"], b["gi"], b["vslot"]
            is_s = b["kind"] == "S"
            A("dve", "bn_stats", out=small[0:n, 32:38], in_=gv[0:n, gi, 0:512], reads=[gvB[gi]], writes=[B["ss"]])
            A("dve", "bn_stats", out=small[0:n, 38:44], in_=gv[0:n, gi, 512:1024], reads=[gvB[gi]], writes=[B["ss"]])
            A("dve", "bn_aggr", out=small[0:n, 44:46], in_=small[0:n, 32:44], reads=[B["ss"]], writes=[B["ss"]])
            A("act", "activation", out=small[0:n, 46:47], in_=small[0:n, 45:46], func=AF.Sqrt, scale=1.0,
              bias=epsc[0:n, 0:1], reads=[B["ss"], B["epsc"]], writes=[B["ss"]])
            A("dve", "reciprocal", out=small[0:n, 46:47], in_=small[0:n, 46:47], reads=[B["ss"]], writes=[B["ss"]])
            if is_s:
                A("dve", "tensor_scalar", out=gv[0:n, gi, :], in0=gv[0:n, gi, :], scalar1=small[0:n, 44:45],
                  scalar2=small[0:n, 46:47], op0=ALU.subtract, op1=ALU.mult, reads=[gvB[gi], B["ss"]],
                  writes=[gvB[gi]])
                A("dve", "tensor_copy", out=vnb(vslot)[0:n, :], in_=gv[0:n, gi, :], reads=[gvB[gi]],
                  writes=vnbB(vslot))
                bk, bkB = alloc()
                for g in range(8):
                    A("pe", "matmul", bk[:, g * n:(g + 1) * n], lhsT=gv[0:n, gi, g * 128:(g + 1) * 128],
                      rhs=identf[0:n, 0:n], start=True, stop=True, reads=[gvB[gi], B["identf"]], writes=[bkB])
                for g in range(8):
                    A("act", "activation", out=vnTs[:, g, :], in_=bk[:, g * n:(g + 1) * n], func=AF.Identity,
                      scale=lng(g), bias=lnb(g), reads=[bkB, B["vecs"]], writes=[B["vnTs"]])
                out_dmas.append(DMA("act", vnT_d.rearrange("(g p) t -> p g t", p=128), vnTs[:], s_misc[9],
                                    reads=[B["vnTs"]]))
            else:
                A("dve", "scalar_tensor_tensor", out=small[0:n, 47:48], in0=small[0:n, 44:45], scalar=-1.0,
                  in1=small[0:n, 46:47], op0=ALU.mult, op1=ALU.mult, reads=[B["ss"]], writes=[B["ss"]])
                A("act", "activation", out=vnb(vslot)[0:n, :], in_=gv[0:n, gi, :], func=AF.Identity,
                  scale=small[0:n, 46:47], bias=small[0:n, 47:48], reads=[gvB[gi], B["ss"]], writes=vnbB(vslot))

        def mixer_rest(pss, T, Tm, mseg, blocks, first_p, last, konly=False):
            full_blocks = [b for b in blocks if b["full"]]
            cur = Cur(pss, "MIXU") if not konly else None
            for c in range(8 if not konly else 0):
                bk, bkB = alloc()
                for kc in range(KC):
                    w, wB = cur.blk(c * KC + kc)
                    A("pe", "matmul", bk[:, 0:T], lhsT=w, rhs=hT[:, kc, 0:T], start=(kc == 0), stop=(kc == KC - 1),
                      reads=[wB, hB[kc]], writes=[bkB])
                    if (c * KC + kc) % UB == UB - 1:
                        ring_release()
                A("act", "activation", out=gT[:, c, 0:T], in_=bk[:, 0:T], func=AF.Gelu, reads=[bkB], writes=[gB[c]])
            curk = Cur(pss, "MIXK")
            def stageA(b, i):
                bkp = alloc()
                for kc in range(KC):
                    w, wB = curk.blk(kc * 4, 4)
                    A("pe", "matmul", bkp[0][0:b["n"], :], lhsT=hT[:, kc, b["c0"]:b["c0"] + b["n"]], rhs=w,
                      start=(kc == 0), stop=(kc == KC - 1), reads=[wB, hB[kc]], writes=[bkp[1]])
                b["cg4"] = bkp
                if b is blocks[-1]:
                    ring_release()
                    ring_release()
                ks = st["kv"] % 4
                st["kv"] += 1
                b["kslot"] = ks
                b["sbi"] = i % 2
                if b["kind"] == "H":
                    qkA(b, ks, True, b["sbi"])
                    st["prev_k"] = ks
                    b["attn"] = False
                    return
                v_process2(b)
                spatial_gate(b["n"], b["vslot"], b["c0"], b["kind"] == "S")
                b["attn"] = True
                if b["kind"] == "S":
                    qkA(b, ks, True, b["sbi"], out_v=vs_d[96:128, :])
                    b["out_k"] = ks_d[96:128, :]
                    b["tiles"] = [(kTc[:, :], B["kTc"], vaDc[:, :], B["vaDc"], 128, "full"),
                                  (kTr[:, ks, :], kTB[ks], vaDr[:, ks, :], vaDB[ks], 32, "full")]
                else:
                    is_last = last and b is blocks[-1]
                    qkA(b, ks, True, b["sbi"], out_v=vlast_d if is_last else None)
                    b["out_k"] = klast_d if is_last else None
                    pk = st["prev_k"]
                    b["tiles"] = [(kTr[:, pk, :], kTB[pk], vaDr[:, pk, :], vaDB[pk], 128, "prev"),
                                  (kTr[:, ks, :], kTB[ks], vaDr[:, ks, :], vaDB[ks], 128, "cur")]
                    st["prev_k"] = ks

            def stageB(b):
                b["qs"] = qkB(b, b["kslot"], True, b["sbi"], out_k=b.get("out_k"))

            def back(b):
                if not b["attn"]:
                    return
                if b["kind"] == "S":
                    attention(32, b["qs"], b["tiles"], b["c0"])
                else:
                    attention(128, b["qs"], b["tiles"], b["c0"], halo=(first_p and b is blocks[0]))

            nb_ = len(blocks)
            for i in range(nb_ + 2):
                if i < nb_:
                    stageA(blocks[i], i)
                if 0 <= i - 1 < nb_:
                    stageB(blocks[i - 1])
                if 0 <= i - 2 < nb_:
                    back(blocks[i - 2])
            if konly:
                return
            (m0, m1, mr) = mseg
            cur = Cur(pss, "MIXM")
            for dc in range(KC):
                bga, bgaB = alloc()
                bpa, bpaB = alloc()
                bgb, bgbB = alloc()
                bpb, bpbB = alloc()
                base = dc * 48
                for kc in range(KC):
                    w, wB = cur.blk(base + kc)
                    A("pe", "matmul", bga[:, m0:m1], lhsT=w, rhs=hT[:, kc, m0:m1], start=(kc == 0), stop=(kc == KC - 1),
                      reads=[wB, hB[kc]], writes=[bgaB])
                for c in range(8):
                    w, wB = cur.blk(base + 16 + c)
                    A("pe", "matmul", bpa[:, m0:m1], lhsT=w, rhs=gT[:, c, m0:m1], start=(c == 0), stop=(c == 7),
                      reads=[wB, gB[c]], writes=[bpaB])
                for kc in range(KC):
                    w, wB = cur.blk(base + 24 + kc)
                    A("pe", "matmul", bgb[:, m0:m1], lhsT=w, rhs=hT[:, kc, m0:m1], start=(kc == 0), stop=(kc == KC - 1),
                      reads=[wB, hB[kc]], writes=[bgbB])
                for c in range(8):
                    w, wB = cur.blk(base + 40 + c)
                    A("pe", "matmul", bpb[:, m0:m1], lhsT=w, rhs=gT[:, 8 + c, m0:m1], start=(c == 0), stop=(c == 7),
                      reads=[wB, gB[8 + c]], writes=[bpbB])
                if dc % 2 == 1:
                    ring_release()
                    ring_release()
                    ring_release()
                t1 = talloc()
                A("act", "activation", out=tmp[:, t1, m0:m1], in_=bga[:, m0:m1], func=AF.Sigmoid, reads=[bgaB],
                  writes=[tmpB[t1]])
                A("dve", "tensor_tensor", out=tmp[:, t1, m0:m1], in0=bpa[:, m0:m1], in1=tmp[:, t1, m0:m1], op=ALU.mult,
                  reads=[bpaB, tmpB[t1]], writes=[tmpB[t1]])
                t2 = talloc()
                A("act", "activation", out=tmp[:, t2, m0:m1], in_=bgb[:, m0:m1], func=AF.Sigmoid, reads=[bgbB],
                  writes=[tmpB[t2]])
                A("dve", "tensor_tensor", out=tmp[:, t2, m0:m1], in0=bpb[:, m0:m1], in1=tmp[:, t2, m0:m1], op=ALU.mult,
                  reads=[bpbB, tmpB[t2]], writes=[tmpB[t2]])
                A("dve", "tensor_tensor", out=gT[:, 16 + dc, m0:m1], in0=tmp[:, t1, m0:m1], in1=tmp[:, t2, m0:m1],
                  op=ALU.add, reads=[tmpB[t1], tmpB[t2]], writes=[gB[16 + dc]])
            cur = Cur(pss, "MIXO")
            for dc in range(KC):
                bk, bkB = alloc()
                for kc in range(KC):
                    w, wB = cur.blk(dc * KC + kc)
                    A("pe", "matmul", bk[:, m0:m1], lhsT=w, rhs=gT[:, 16 + kc, m0:m1], start=(kc == 0),
                      stop=(kc == KC - 1), reads=[wB, gB[16 + kc]], writes=[bkB])
                if dc % 2 == 1:
                    ring_release()
                A("dve", "scalar_tensor_tensor", out=xT[:, dc, m0:m1], in0=bk[:, m0:m1], scalar=sc(1, mr, 2, dc),
                  in1=xT[:, dc, m0:m1], op0=ALU.mult, op1=ALU.add, reads=[bkB, xB[dc], B["scal"]], writes=[xB[dc]])

        STOP = int(os.environ.get("KSTOP", "99"))
        class _Stop(Exception):
            pass
        def stop_if(k):
            if STOP == k:
                raise _Stop()
        try:
            segs0 = [(0, TH, 1)]
            DMA("sp", xT[:, :, 0:TH], xsh_d.rearrange("(dc p) t -> p dc t", p=128), s_misc[10], writes=xB)
            ada(0, 0)
            ffn(0, 0, TH, segs0, "F1A", "F1B")
            ada(0, 1)
            blocks0 = [dict(kind="H", c0=0, n=TH, jg=0, full=True)]
            mixer2(0, TH, TH, segs0, (0, TH, 1), blocks0, False, False)
            mixer_rest(0, TH, TH, (0, TH, 1), blocks0, False, False, konly=True)

            segsP = [(0, TP, 1)]
            for t in range(1, NT + 1):
                t0 = (t - 1) * TP
                for dc in range(KC):
                    DMA("sp", xT[:, dc, :], xT_d[dc * 128:(dc + 1) * 128, t0:t0 + TP], s_x[dc], writes=[xB[dc]])
                ffn(t, 0, TP, segsP, "F1A", "F1B")
                blocksP = [dict(kind="P", c0=j * 128, n=128, jg=(t - 1) * 4 + j, full=True) for j in range(4)]
                mixer2(t, TP, TP, segsP, (0, TP, 1), blocksP, t == 1, t == NT)
                mixer_rest(t, TP, TP, (0, TP, 1), blocksP, t == 1, t == NT)
                if t == 1:
                    ada(1, 2)
                ffn(t, 2, TP, segsP, "F2A", "F2B",
                    store=lambda dc, t0=t0: yT_d[dc * 128:(dc + 1) * 128, t0:t0 + TP])

            ps_ = NT + 1
            segsS = [(0, TS, 0)]
            DMA("sp", xT[:, :, 0:TS], xs_d.rearrange("(dc p) t -> p dc t", p=128), s_misc[12], writes=xB)
            ffn(ps_, 0, TS, segsS, "F1A", "F1B")
            blocksS = [dict(kind="S", c0=0, n=TS, jg=0, full=True)]
            mixer2(ps_, TS, TS, segsS, (0, TS, 0), blocksS, False, False)
            mixer_rest(ps_, TS, TS, (0, TS, 0), blocksS, False, False)
            ffn(ps_, 2, TS, segsS, "F2A", "F2B")
            out_dmas.append(DMA("act", yTs_d.rearrange("(dc p) t -> p dc t", p=128), xT[:, :, 0:TS], s_misc[11],
                                reads=xB))

        except _Stop:
            pass
        if STOP == 99:
            assert rs["released"] == NSEQ, (rs, NSEQ)
        for e in ("act", "sp"):
            P.add(e, lambda h: None, extra_deps=[i for i in out_dmas])
        P.emit(blk, s_eng)
    return nc


def _blocks(inp, NF):
    def r4(w, a, b):
        K, N = w.shape
        return w.reshape(K // 128, 128, N // 128, 128)
    w_ada = inp["w_ada"][0]
    w_in = inp["w_in"][0]
    out = []
    def ada(s):
        w = r4(w_ada[:, s * 6144:(s + 1) * 6144], 0, 0)
        return w.transpose(2, 0, 1, 3).reshape(-1, 128, 128)
    def ffa(w1, w3):
        a = r4(w1, 0, 0).transpose(2, 0, 1, 3)
        b = r4(w3, 0, 0).transpose(2, 0, 1, 3)
        return np.concatenate([a, b], axis=1).reshape(-1, 128, 128)
    def ffb(w2):
        return r4(w2, 0, 0).transpose(2, 0, 1, 3).reshape(-1, 128, 128)
    def mixa():
        cols = [np.arange(1024, 1536), np.arange(1536, 2048), np.arange(2048, 2560), np.arange(2560, 3072)]
        res = []
        for c in cols:
            w = w_in[:, c].reshape(16, 128, 4, 128).transpose(0, 2, 1, 3)
            res.append(w.reshape(-1, 128, 128))
        return np.concatenate(res, 0)
    def mixk():
        w = w_in[:, 3072:3584].reshape(16, 128, 4, 128).transpose(0, 2, 1, 3)
        return w.reshape(-1, 128, 128)
    def mixu():
        return r4(w_in[:, 0:1024], 0, 0).transpose(2, 0, 1, 3).reshape(-1, 128, 128)
    def mixm():
        ga = r4(w_in[:, 3584:5632], 0, 0).transpose(2, 0, 1, 3)
        gb = r4(w_in[:, 5632:7680], 0, 0).transpose(2, 0, 1, 3)
        pa = r4(inp["w_pa"][0], 0, 0).transpose(2, 0, 1, 3)
        pb = r4(inp["w_pb"][0], 0, 0).transpose(2, 0, 1, 3)
        return np.concatenate([ga, pa, gb, pb], axis=1).reshape(-1, 128, 128)
    def mixo():
        return r4(inp["w_o"][0], 0, 0).transpose(2, 0, 1, 3).reshape(-1, 128, 128)
    parts = [ada(0), ffa(inp["w1_ffn1"][0], inp["w3_ffn1"][0]), ffb(inp["w2_ffn1"][0]), ada(1), mixa(), mixu(), mixk(), mixm(),
             mixo(), ada(2), ffa(inp["w1_ffn2"][0], inp["w3_ffn2"][0]), ffb(inp["w2_ffn2"][0])]
    allb = np.concatenate(parts, 0)
    NB = allb.shape[0]
    assert NB % UB == 0
    u = allb.reshape(NB // UB, UB, 128, 128).transpose(0, 2, 1, 3).reshape(NB // UB, 128, UB * 128)
    return np.ascontiguousarray(u, dtype=np.float32)


def _rope_tab(pos):
    inv = ROPE_THETA ** (-np.arange(0, 16, 2, dtype=np.float32) / np.float32(16))
    ang = pos.astype(np.float32)[:, None] * inv.astype(np.float32)[None, :]
    return np.concatenate([np.cos(ang), np.sin(ang)], axis=1).astype(np.float32)


_CACHE = {}


def run(inp, NT, NF):
    inp = {k: np.asarray(v, dtype=np.float32) for k, v in inp.items()}
    key = (NT, NF)
    if key not in _CACHE:
        _CACHE[key] = build(NT, NF)
    nc = _CACHE[key]
    vo, NV = vec_layout(NT)
    wsrc = _blocks(inp, NF)
    xp = inp["x_prompt"]
    xs = inp["x_sample"]
    Bp, SEQ, _ = xp.shape
    HALF = NT * TP
    assert SEQ == 2 * HALF and Bp == 4 and xs.shape[0] == 8
    wsT = np.ascontiguousarray(inp["w_s"][0].transpose(2, 0, 1).reshape(128, 1024))
    bsrow = np.ascontiguousarray(inp["b_s"][0].reshape(1, 1024))
    in_maps = []
    for c in range(8):
        b, hf = c // 2, c % 2
        xT = np.ascontiguousarray(xp[b, hf * HALF:(hf + 1) * HALF, :].T)
        if hf == 1:
            halo = xp[b, HALF - TH:HALF, :]
        else:
            halo = np.zeros((TH, D), np.float32)
        xsh = np.ascontiguousarray(halo.T)
        xs_t = np.ascontiguousarray(xs[c].T)
        vecs = np.zeros((128, NV), np.float32)
        cc = np.stack([inp["c_sample"][c], inp["c_prompt"][b]], 1)
        vecs[:, vo["c"]:vo["c"] + 32] = cc.reshape(16, 128, 2).transpose(1, 0, 2).reshape(128, 32)
        vecs[:, vo["bada"]:vo["bada"] + 144] = inp["b_ada"][0].reshape(144, 128).T
        for s, nm in enumerate(("g_ffn1", "g_mix", "g_ffn2")):
            vecs[:, vo["g"] + s * 16:vo["g"] + s * 16 + 16] = inp[nm][0].reshape(16, 128).T
        vecs[:, vo["halo"]] = 0.0 if hf == 1 else -30000.0
        vecs[:, vo["sink"]:vo["sink"] + 16] = inp["sinks"][0][None, :]
        vecs[:, vo["gq"]:vo["gq"] + 64] = inp["g_q"][0][None, :]
        vecs[:, vo["gk"]:vo["gk"] + 64] = inp["g_k"][0][None, :]
        vecs[:, vo["lng"]:vo["lng"] + 8] = inp["ln_v_g"][0].reshape(8, 128).T
        vecs[:, vo["lnb"]:vo["lnb"] + 8] = inp["ln_v_b"][0].reshape(8, 128).T
        posP = hf * HALF + np.arange(HALF)
        tabP = _rope_tab(posP).reshape(NT * 4, 128, 16).transpose(1, 0, 2).reshape(128, NT * 4 * 16)
        vecs[:, vo["ropeP"]:vo["ropeP"] + NT * 4 * 16] = tabP
        vecs[0:TS, vo["ropeS"]:vo["ropeS"] + 16] = _rope_tab(PAST_LEN + np.arange(TS))
        vecs[:, vo["ropeH"]:vo["ropeH"] + 16] = _rope_tab(np.maximum(hf * HALF - TH + np.arange(TH), 0))
        in_maps.append({
            "wsrc": wsrc, "xT": xT, "xsh": xsh, "xs": xs_t, "vecs": vecs, "wsT": wsT, "bsrow": bsrow,
            "cache_k": np.ascontiguousarray(inp["cache_swa_k"][0, c].reshape(128, 256)),
            "cache_v": np.ascontiguousarray(inp["cache_swa_v"][0, c].reshape(128, 256)),
        })
    res = run_bass_kernel_spmd(nc, in_maps, core_ids=list(range(8)))
    R = res.results
    y_p = np.empty((4, SEQ, D), np.float32)
    y_s = np.empty((8, TS, D), np.float32)
    kp = np.empty((1, 4, 128, 4, 64), np.float32)
    vp = np.empty((1, 4, 128, 4, 64), np.float32)
    ks = np.empty((1, 8, 128, 4, 64), np.float32)
    vs = np.empty((1, 8, 128, 4, 64), np.float32)
    gvs = np.empty((1, 8, TS, DA), np.float32)
    for c in range(8):
        b, hf = c // 2, c % 2
        if R[c]["yT"] is None:
            continue
        y_p[b, hf * HALF:(hf + 1) * HALF, :] = R[c]["yT"].T
        y_s[c] = R[c]["yTs"].T
        if hf == 1:
            kp[0, b] = R[c]["klast"].reshape(128, 4, 64)
            vp[0, b] = R[c]["vlast"].reshape(128, 4, 64)
        ks[0, c] = R[c]["ks"].reshape(128, 4, 64)
        vs[0, c] = R[c]["vs"].reshape(128, 4, 64)
        gvs[0, c] = R[c]["vnT"].T
    return (y_p, y_s, kp, vp, ks, vs, gvs)


def kernel(**inputs):
    return run(inputs, 8, 44)
```

```python
import os
import numpy as np
from contextlib import ExitStack
import concourse.bass as bass
import concourse.mybir as mybir
from concourse.bass_utils import run_bass_kernel_spmd

F32 = mybir.dt.float32
BF16 = mybir.dt.bfloat16
ALU = mybir.AluOpType
AF = mybir.ActivationFunctionType
AX = mybir.AxisListType

D = 2048
KC = 16
DA = 1024
NHD = 16
NKV = 4
HD = 64
TP = 512
TS = 32
TH = 128
T0 = TS + TH
UB = 32
RSL = 4
CONVG = 4
EPS = 1e-6
ROPE_THETA = 500000.0
PAST_LEN = 2048
ENGS = ("pe", "act", "dve", "pool", "sp")


class Buf:
    __slots__ = ("writers", "readers")

    def __init__(self):
        self.writers = {}
        self.readers = {}


class Ins:
    __slots__ = ("eng", "fn", "deps", "needed", "sem", "val", "is_dma", "key")


class Prog:
    def __init__(self):
        self.streams = {e: [] for e in ENGS}
        self.dma_counts = {}

    def add(self, eng, fn, reads=(), writes=(), dma_sem=None, extra_deps=()):
        ins = Ins()
        ins.eng = eng
        ins.fn = fn
        ins.needed = False
        ins.is_dma = dma_sem is not None
        ins.sem = None
        ins.val = None
        if ins.is_dma:
            c = self.dma_counts.get(id(dma_sem), 0) + 16
            self.dma_counts[id(dma_sem)] = c
            ins.sem = dma_sem
            ins.val = c
            ins.key = ("dma", id(dma_sem))
        else:
            ins.key = eng
        deps = {}
        is_dma = ins.is_dma

        def dep(p):
            if p is None or p is ins:
                return
            if (not p.is_dma) and (not is_dma) and p.eng == eng == "pe":
                return
            deps[id(p)] = p

        for b in reads:
            for p in b.writers.values():
                dep(p)
        for b in writes:
            for k, p in b.readers.items():
                dep(p)
            for k, p in b.writers.items():
                dep(p)
        for p in extra_deps:
            dep(p)
        ins.deps = list(deps.values())
        for p in ins.deps:
            p.needed = True
        for b in reads:
            b.readers[ins.key] = ins
        for b in writes:
            b.writers[ins.key] = ins
        self.streams[eng].append(ins)
        return ins

    def emit(self, block, sems):
        for e in ENGS:
            c = 0
            for ins in self.streams[e]:
                if ins.is_dma:
                    continue
                if ins.needed:
                    c += 1
                    ins.sem = sems[e]
                    ins.val = c
        streams = self.streams

        def run_stream(e, h):
            waited = {}
            for ins in streams[e]:
                need = {}
                for p in ins.deps:
                    k = id(p.sem)
                    if waited.get(k, 0) >= p.val:
                        continue
                    if k not in need or need[k][1] < p.val:
                        need[k] = (p.sem, p.val)
                for k, (s, v) in need.items():
                    h.wait_ge(s, v)
                    waited[k] = v
                r = ins.fn(h)
                if r is None:
                    continue
                if ins.is_dma:
                    r.then_inc(ins.sem, 16)
                elif ins.needed:
                    r.then_inc(ins.sem, 1)

        @block.tensor
        def _(h):
            run_stream("pe", h)

        @block.scalar
        def _(h):
            run_stream("act", h)

        @block.vector
        def _(h):
            run_stream("dve", h)

        @block.gpsimd
        def _(h):
            run_stream("pool", h)

        @block.sync
        def _(h):
            run_stream("sp", h)


def sections(NF):
    secs = [("ADA0", 768), ("F1A", 32 * NF), ("F1B", 16 * NF), ("ADA1", 768), ("MIXA", 256), ("MIXU", 128), ("MIXK", 64),
            ("MIXM", 768), ("MIXO", 256), ("ADA2", 768), ("F2A", 32 * NF), ("F2B", 16 * NF)]
    off = {}
    o = 0
    for n, c in secs:
        assert c % UB == 0
        off[n] = (o // UB, c // UB)
        o += c
    return secs, off, o // UB


def vec_layout(NT):
    o = {}
    c = 0
    for n, w in [("c", 32), ("bada", 144), ("g", 48), ("halo", 1), ("sink", 16), ("gq", 64), ("gk", 64),
                 ("lng", 8), ("lnb", 8), ("ropeP", 16 * NT * 4), ("ropeS", 16), ("ropeH", 16)]:
        o[n] = c
        c += w
    return o, c


def build(NT, NF):
    NFA = max(NF, 44)
    secs, soff, NU = sections(NF)
    vo, NV = vec_layout(NT)
    nc = bass.Bass("TRN2", target_bir_lowering=False)
    dt_in = lambda n, s, t=F32: nc.dram_tensor(n, s, t, kind="ExternalInput").ap()
    dt_out = lambda n, s: nc.dram_tensor(n, s, F32, kind="ExternalOutput").ap()
    wsrc = dt_in("wsrc", [NU, 128, UB * 128])
    xT_d = dt_in("xT", [D, NT * TP])
    xsh_d = dt_in("xsh", [D, TH])
    xs_d = dt_in("xs", [D, TS])
    vecs_d = dt_in("vecs", [128, NV])
    wsT_d = dt_in("wsT", [128, 8 * 128])
    bsrow_d = dt_in("bsrow", [1, 1024])
    ck_d = dt_in("cache_k", [128, 256])
    cv_d = dt_in("cache_v", [128, 256])
    wbf = nc.dram_tensor("wbf", [NU, 128, UB * 128], BF16, kind="Internal").ap()
    yT_d = dt_out("yT", [D, NT * TP])
    yTs_d = dt_out("yTs", [D, TS])
    klast_d = dt_out("klast", [128, 256])
    vlast_d = dt_out("vlast", [128, 256])
    ks_d = dt_out("ks", [128, 256])
    vs_d = dt_out("vs", [128, 256])
    vnT_d = dt_out("vnT", [DA, TS])

    with ExitStack() as es:
        def sb(name, shape, dt):
            return es.enter_context(nc.sbuf_tensor(name, shape, dt))

        def sem(name):
            return es.enter_context(nc.semaphore(name))

        P = Prog()
        xT = sb("xTs", [128, KC, TP], F32)
        hT = sb("hTs", [128, KC, TP], BF16)
        gT = sb("gTs", [128, NFA, TP], BF16)
        ring = sb("ring", [128, RSL, UB * 128], BF16)
        vecs = sb("vecs_s", [128, NV], F32)
        gv = sb("gv", [128, 4, 1024], F32)
        qkf = sb("qkf", [128, 256], F32)
        sqnT = sb("sqnT", [128, 2, 1280], F32)
        qkb = sb("qkb", [128, 1280], BF16)
        vaf = sb("vaf", [128, 256], F32)
        ropet = sb("ropet", [128, 4, 160], F32)
        kTr = sb("kTr", [128, 4, 512], BF16)
        vaDr = sb("vaDr", [128, 4, 512], BF16)
        kTc = sb("kTc", [128, 512], BF16)
        vaDc = sb("vaDc", [128, 512], BF16)
        pT = sb("pTs", [128, 2, 2, 512], BF16)
        pTS = sb("pTSs", [128, 2, 2, 128], BF16)
        NTMP = 8
        tmp = sb("tmp", [128, NTMP, TP], F32)
        sqb = sb("sqb", [128, 2, TP], BF16)
        rstd = sb("rstd", [128, TP], F32)
        small = sb("small", [128, 64], F32)
        modT = sb("modT", [128, 144, 2], F32)
        scal = sb("scal", [128, 288], F32)
        scT = sb("scT", [128, 32], BF16)
        esink = sb("esink", [128, 16], F32)
        identb = sb("identb", [128, 128], BF16)
        identf = sb("identf", [128, 128], F32)
        onesb = sb("onesb", [128, 128], BF16)
        wsTb = sb("wsTb", [128, 8, 128], BF16)
        Cg = sb("Cg", [128, 8, 128], F32)
        CgS = sb("CgS", [128, 8, 32], F32)
        vnTs = sb("vnTs", [128, 8, 32], F32)
        epsc = sb("epsc", [128, 1], F32)
        banks = [es.enter_context(nc.psum_tensor(f"bank{i}", [128, 512], F32)) for i in range(8)]
        bankB = [Buf() for _ in range(8)]
        bank_ctr = [0]

        def alloc():
            b = bank_ctr[0] % 8
            bank_ctr[0] += 1
            return banks[b], bankB[b]

        s_eng = {e: sem("s_" + e) for e in ("pe", "act", "dve", "pool", "sp")}
        s_ring = [sem(f"s_ring{i}") for i in range(RSL)]
        s_wb = [sem(f"s_wb{i}") for i in range(RSL)]
        s_ringp = [sem(f"s_ringp{i}") for i in range(RSL)]
        s_x = [sem(f"s_x{i}") for i in range(KC)]
        s_tmp = [sem(f"s_tmp{i}") for i in range(NTMP)]
        s_misc = [sem(f"s_misc{i}") for i in range(16)]
        blk = es.enter_context(nc.Block())

        xB = [Buf() for _ in range(KC)]
        hB = [Buf() for _ in range(KC)]
        gB = [Buf() for _ in range(NFA)]
        ringB = [Buf() for _ in range(RSL)]
        tmpB = [Buf() for _ in range(NTMP)]
        B = {k: Buf() for k in ("vecs", "gv0", "gv1", "gv2", "gv3", "qkf", "sqn0", "sqn1", "qkb", "kT3", "vaD3", "vaf", "ropet", "kTc", "vaDc",
                                "pT0", "pT1", "pTS0", "pTS1", "sqb0", "sqb1", "rstd", "small", "modT", "scal", "scT", "esink",
                                "identb", "identf", "onesb", "wsTb", "Cg", "CgS", "vnTs", "epsc", "kT0", "kT1", "kT2",
                                "vaD0", "vaD1", "vaD2", "wsTf", "bsrow", "ckf", "cvf", "ss", "dram_out")}
        gvB = [B["gv0"], B["gv1"], B["gv2"], B["gv3"]]
        kTB = [B["kT0"], B["kT1"], B["kT2"], B["kT3"]]
        vaDB = [B["vaD0"], B["vaD1"], B["vaD2"], B["vaD3"]]
        tmp_ctr = [0]

        def talloc():
            i = tmp_ctr[0] % NTMP
            tmp_ctr[0] += 1
            return i

        def A(eng, meth, *args, reads=(), writes=(), **kw):
            ins = P.add(eng, lambda e: getattr(e, meth)(*args, **kw), reads=reads, writes=writes)
            if os.environ.get("KDBG"):
                ins.fn.__dict__["lbl"] = (meth, [str(getattr(a, "ap", a)) + "@" + str(getattr(a, "offset", "")) for a in args],
                                          {k: (str(v.ap) + "@" + str(v.offset)) if hasattr(v, "ap") else v for k, v in kw.items()})
            return ins

        def DMA(eng, out, in_, s, reads=(), writes=(), extra=()):
            return P.add(eng, lambda e: e.dma_start(out=out, in_=in_), reads=reads, writes=writes, dma_sem=s,
                         extra_deps=extra)

        out_dmas = []

        non_ada = [u for n, c in secs if not n.startswith("ADA") for u in range(soff[n][0], soff[n][0] + soff[n][1])]
        NPASS = NT + 2
        pass_secs = {0: ["ADA0", "F1A", "F1B", "ADA1", "MIXA", "MIXK"],
                     1: ["F1A", "F1B", "MIXA", "MIXU", "MIXK", "MIXM", "MIXO", "ADA2", "F2A", "F2B"]}
        for t in range(2, NPASS):
            pass_secs[t] = ["F1A", "F1B", "MIXA", "MIXU", "MIXK", "MIXM", "MIXO", "F2A", "F2B"]
        useq = []
        pos_of = {}
        for t in range(NPASS):
            for n in pass_secs[t]:
                for u in range(soff[n][0], soff[n][0] + soff[n][1]):
                    pos_of[(t, u)] = len(useq)
                    useq.append(u)
        NSEQ = len(useq)
        non_ada_set = set(non_ada)
        wbB = [Buf() for _ in range(NU)]
        rs = {"next": 0, "released": 0}
        seen = set()

        def ring_advance():
            while rs["next"] < NSEQ and rs["next"] < rs["released"] + RSL:
                k = rs["next"]
                u = useq[k]
                sl = k % RSL
                if u not in seen:
                    seen.add(u)
                    DMA("pool", ring[:, sl, :], wsrc[u], s_ringp[sl], writes=[ringB[sl]])
                    if u in non_ada_set:
                        DMA("sp", wbf[u], ring[:, sl, :], s_wb[sl], reads=[ringB[sl]], writes=[wbB[u]])
                else:
                    DMA("sp", ring[:, sl, :], wbf[u], s_ring[sl], reads=[wbB[u]], writes=[ringB[sl]])
                rs["next"] += 1

        def ring_release():
            rs["released"] += 1
            ring_advance()

        class Cur:
            def __init__(self, pss, sec):
                self.pss = pss
                self.u0 = soff[sec][0]
                self.i = 0
                self.cur_unit = None

            def _touch(self, bi):
                u = self.u0 + bi // UB
                k = pos_of[(self.pss, u)]
                assert k >= rs["released"], (k, rs)
                assert k < rs["next"], ("unit not loaded", k, rs)
                return k % RSL, bi % UB

            def blk(self, bi, n=1):
                sl, o = self._touch(bi)
                return ring[:, sl, o * 128:(o + n) * 128], ringB[sl]

        DMA("sp", vecs[:], vecs_d, s_misc[0], writes=[B["vecs"]])
        wsTf = gv[:, 0, :]
        bsr = gv[0:1, 1, :]
        ckf = gv[:, 2, 0:256]
        cvf = gv[:, 2, 256:512]
        DMA("sp", wsTf, wsT_d, s_misc[1], writes=[gvB[0]])
        DMA("sp", bsr, bsrow_d, s_misc[2], writes=[gvB[1]])
        DMA("sp", ckf, ck_d, s_misc[3], writes=[gvB[2]])
        DMA("sp", cvf, cv_d, s_misc[4], writes=[gvB[2]])
        A("pool", "memset", identb[:], 0.0, writes=[B["identb"]])
        A("pool", "affine_select", out=identb[:], in_=identb[:], pattern=[[-1, 128]], compare_op=ALU.not_equal,
          fill=1.0, base=0, channel_multiplier=1, reads=[B["identb"]], writes=[B["identb"]])
        A("pool", "memset", identf[:], 0.0, writes=[B["identf"]])
        A("pool", "affine_select", out=identf[:], in_=identf[:], pattern=[[-1, 128]], compare_op=ALU.not_equal,
          fill=1.0, base=0, channel_multiplier=1, reads=[B["identf"]], writes=[B["identf"]])
        A("pool", "memset", onesb[:], 1.0, writes=[B["onesb"]])
        A("pool", "memset", epsc[:], EPS, writes=[B["epsc"]])
        A("pool", "memset", pT[:], 0.0, writes=[B["pT0"], B["pT1"]])
        A("pool", "memset", gv[:, 3, :], 1.0, writes=[gvB[3]])
        ring_advance()

        A("act", "activation", out=scT[:], in_=vecs[:, vo["c"]:vo["c"] + 32], func=AF.Silu, reads=[B["vecs"]],
          writes=[B["scT"]])
        A("act", "activation", out=esink[:], in_=vecs[:, vo["sink"]:vo["sink"] + 16], func=AF.Exp, reads=[B["vecs"]],
          writes=[B["esink"]])
        wsTf3 = wsTf.rearrange("p (g i) -> p g i", g=8)
        A("dve", "memset", wsTf3[64:128, :, 0:64], 0.0, reads=[], writes=[gvB[0]])
        A("dve", "tensor_copy", out=wsTb[:], in_=wsTf3, reads=[gvB[0]], writes=[B["wsTb"]])
        onesf = gv[:, 3, 0:128]
        lnb = lambda g: vecs[:, vo["lnb"] + g:vo["lnb"] + g + 1]
        lng = lambda g: vecs[:, vo["lng"] + g:vo["lng"] + g + 1]
        for half in range(2):
            bk, bkB = alloc()
            A("pe", "matmul", bk[:, :], lhsT=onesf, rhs=wsTf[:, half * 512:(half + 1) * 512], start=True, stop=True,
              reads=[gvB[3], gvB[0]], writes=[bkB])
            bk2, bk2B = alloc()
            A("pe", "matmul", bk2[:, :], lhsT=gv[0:1, 3, 0:128], rhs=bsr[:, half * 512:(half + 1) * 512], start=True,
              stop=True, reads=[gvB[3], gvB[1]], writes=[bk2B])
            ti = talloc()
            A("act", "activation", out=tmp[:, ti, :], in_=bk2[:, :], func=AF.Copy, reads=[bk2B], writes=[tmpB[ti]])
            for gg in range(4):
                g = half * 4 + gg
                A("dve", "scalar_tensor_tensor", out=Cg[:, g, :], in0=bk[:, gg * 128:(gg + 1) * 128], scalar=lnb(g),
                  in1=tmp[:, ti, gg * 128:(gg + 1) * 128], op0=ALU.mult, op1=ALU.add,
                  reads=[bkB, tmpB[ti], B["vecs"]], writes=[B["Cg"]])
        bk, bkB = alloc()
        bk2, bk2B = alloc()
        bsr3 = bsr.rearrange("p (g i) -> p g i", g=8)
        for g in range(8):
            A("pe", "matmul", bk[:, g * 32:(g + 1) * 32], lhsT=gv[0:32, 3, 0:128], rhs=wsTf3[0:32, g, 0:32], start=True,
              stop=True, reads=[gvB[3], gvB[0]], writes=[bkB])
            A("pe", "matmul", bk2[:, g * 32:(g + 1) * 32], lhsT=gv[0:1, 3, 0:128], rhs=bsr3[:, g, 0:32], start=True,
              stop=True, reads=[gvB[3], gvB[1]], writes=[bk2B])
        ti = talloc()
        A("act", "activation", out=tmp[:, ti, 0:256], in_=bk2[:, 0:256], func=AF.Copy, reads=[bk2B], writes=[tmpB[ti]])
        for g in range(8):
            A("dve", "scalar_tensor_tensor", out=CgS[:, g, :], in0=bk[:, g * 32:(g + 1) * 32], scalar=lnb(g),
              in1=tmp[:, ti, g * 32:(g + 1) * 32], op0=ALU.mult, op1=ALU.add, reads=[bkB, tmpB[ti], B["vecs"]],
              writes=[B["CgS"]])
        A("dve", "tensor_copy", out=qkb[:, 0:256], in_=ckf, reads=[gvB[2]], writes=[B["qkb"]])
        bk, bkB = alloc()
        bkb = bk[:, :].bitcast(BF16)
        for h in range(NKV):
            A("pe", "transpose", bkb[0:64, h * 128:(h + 1) * 128], qkb[:, h * 64:(h + 1) * 64], identb[:],
              reads=[B["qkb"], B["identb"]], writes=[bkB])
        A("dve", "tensor_copy", out=kTc[0:64, :], in_=bkb[0:64, 0:512], reads=[bkB], writes=[B["kTc"]])
        vaDc3 = vaDc[:, :].rearrange("p (h d) -> p h d", h=4)
        cv3 = cvf.rearrange("p (h d) -> p h d", h=4)
        A("dve", "tensor_copy", out=vaDc3[:, :, 0:64], in_=cv3, reads=[gvB[2]], writes=[B["vaDc"]])
        A("dve", "tensor_copy", out=vaDc3[:, :, 64:128], in_=cv3, reads=[gvB[2]], writes=[B["vaDc"]])
        out_dmas.append(DMA("act", ks_d[0:96, :], ck_d[32:128, :], s_misc[5]))
        out_dmas.append(DMA("act", vs_d[0:96, :], cv_d[32:128, :], s_misc[6]))

        def sc(s, r, kind, dc):
            c = ((s * 2 + r) * 3 + kind) * 16 + dc
            return scal[:, c:c + 1]

        def ada(pss, s):
            cur = Cur(pss, "ADA%d" % s)
            bk, bkB = alloc()
            for ci in range(48):
                for kc in range(KC):
                    bi = ci * KC + kc
                    w, wB = cur.blk(bi)
                    A("pe", "matmul", bk[:, 2 * ci:2 * ci + 2], lhsT=w, rhs=scT[:, 2 * kc:2 * kc + 2], start=(kc == 0),
                      stop=(kc == KC - 1), reads=[wB, B["scT"]], writes=[bkB])
                    if bi % UB == UB - 1:
                        ring_release()
            bo = vo["bada"] + s * 48
            A("dve", "tensor_tensor", out=modT[:, s * 48:(s + 1) * 48, :],
              in0=bk[:, 0:96].rearrange("p (c r) -> p c r", r=2),
              in1=vecs[:, bo:bo + 48].unsqueeze(2).to_broadcast([128, 48, 2]), op=ALU.add,
              reads=[bkB, B["vecs"]], writes=[B["modT"]])
            for r in range(2):
                base = ((s * 2 + r) * 3) * 16
                A("dve", "scalar_tensor_tensor", out=scal[:, base:base + 16], in0=modT[:, s * 48 + 16:s * 48 + 32, r],
                  scalar=1.0, in1=vecs[:, vo["g"] + s * 16:vo["g"] + s * 16 + 16], op0=ALU.add, op1=ALU.mult,
                  reads=[B["modT"], B["vecs"]], writes=[B["scal"]])
                A("dve", "tensor_copy", out=scal[:, base + 16:base + 32], in_=modT[:, s * 48:s * 48 + 16, r],
                  reads=[B["modT"]], writes=[B["scal"]])
                A("dve", "tensor_scalar", out=scal[:, base + 32:base + 48], in0=modT[:, s * 48 + 32:s * 48 + 48, r],
                  scalar1=(1.0 if s == 1 else 0.5), scalar2=None, op0=ALU.mult, reads=[B["modT"]],
                  writes=[B["scal"]])

        def norm(s, T, segs):
            bk, bkB = alloc()
            for dc in range(KC):
                q = dc % 2
                A("act", "activation", out=sqb[:, q, 0:T], in_=xT[:, dc, 0:T], func=AF.Square, reads=[xB[dc]],
                  writes=[B["sqb%d" % q]])
                A("pe", "matmul", bk[:, 0:T], lhsT=onesb[:], rhs=sqb[:, q, 0:T], start=(dc == 0), stop=(dc == KC - 1),
                  reads=[B["onesb"], B["sqb%d" % q]], writes=[bkB])
            A("act", "activation", out=rstd[:, 0:T], in_=bk[:, 0:T], func=AF.Sqrt, scale=1.0 / D, bias=epsc[:, 0:1],
              reads=[bkB, B["epsc"]], writes=[B["rstd"]])
            A("dve", "reciprocal", out=rstd[:, 0:T], in_=rstd[:, 0:T], reads=[B["rstd"]], writes=[B["rstd"]])
            for dc in range(KC):
                ti = talloc()
                A("dve", "tensor_tensor", out=tmp[:, ti, 0:T], in0=xT[:, dc, 0:T], in1=rstd[:, 0:T], op=ALU.mult,
                  reads=[xB[dc], B["rstd"]], writes=[tmpB[ti]])
                for (c0, c1, r) in segs:
                    A("act", "activation", out=hT[:, dc, c0:c1], in_=tmp[:, ti, c0:c1], func=AF.Identity,
                      scale=sc(s, r, 0, dc), bias=sc(s, r, 1, dc), reads=[tmpB[ti], B["scal"]], writes=[hB[dc]])

        def ffn(pss, s, T, segs, nameA, nameB, store=None):
            norm(s, T, segs)
            cur = Cur(pss, nameA)
            for f in range(NF):
                ba, baB = alloc()
                bb, bbB = alloc()
                for wi, (bk, bkB) in enumerate(((ba, baB), (bb, bbB))):
                    for kc in range(KC):
                        w, wB = cur.blk(f * 32 + wi * 16 + kc)
                        A("pe", "matmul", bk[:, 0:T], lhsT=w, rhs=hT[:, kc, 0:T], start=(kc == 0), stop=(kc == KC - 1),
                          reads=[wB, hB[kc]], writes=[bkB])
                ring_release()
                ti = talloc()
                A("act", "activation", out=tmp[:, ti, 0:T], in_=ba[:, 0:T], func=AF.Silu, reads=[baB],
                  writes=[tmpB[ti]])
                A("dve", "tensor_tensor", out=gT[:, f, 0:T], in0=bb[:, 0:T], in1=tmp[:, ti, 0:T], op=ALU.mult,
                  reads=[bbB, tmpB[ti]], writes=[gB[f]])
            cur = Cur(pss, nameB)
            bi = 0
            for dc in range(KC):
                bk, bkB = alloc()
                for f in range(NF):
                    w, wB = cur.blk(bi)
                    A("pe", "matmul", bk[:, 0:T], lhsT=w, rhs=gT[:, f, 0:T], start=(f == 0), stop=(f == NF - 1),
                      reads=[wB, gB[f]], writes=[bkB])
                    if bi % UB == UB - 1:
                        ring_release()
                    bi += 1
                if store is None:
                    for (c0, c1, r) in segs:
                        A("dve", "scalar_tensor_tensor", out=xT[:, dc, c0:c1], in0=bk[:, c0:c1], scalar=sc(s, r, 2, dc),
                          in1=xT[:, dc, c0:c1], op0=ALU.mult, op1=ALU.add, reads=[bkB, xB[dc], B["scal"]],
                          writes=[xB[dc]])
                else:
                    (c0, c1, r) = segs[0]
                    ti = talloc()
                    A("dve", "scalar_tensor_tensor", out=tmp[:, ti, 0:T], in0=bk[:, 0:T], scalar=sc(s, r, 2, dc),
                      in1=xT[:, dc, 0:T], op0=ALU.mult, op1=ALU.add, reads=[bkB, xB[dc], B["scal"]],
                      writes=[tmpB[ti]])
                    out_dmas.append(DMA("act", store(dc), tmp[:, ti, 0:T], s_tmp[ti], reads=[tmpB[ti]]))

        uT = lambda c: gT[:, c, :]
        oTt = lambda t: gT[:, 8 + t, :]
        yTt = lambda dc: gT[:, 16 + dc, :]
        vnb = lambda b: gT[:, 32 + 2 * b:34 + 2 * b, :].rearrange("p a c -> p (a c)")
        vnbB = lambda b: [gB[32 + 2 * b], gB[33 + 2 * b]]
        qTv = lambda b: gT[:, 36 + 4 * b:40 + 4 * b, :].rearrange("p a c -> p (a c)")
        qTB = lambda b: [gB[36 + 4 * b + i] for i in range(4)]
        st = {"kv": 0, "blk": 0}

        def rope_ap(kind, jg, n):
            if kind == "P":
                o = vo["ropeP"] + jg * 16
            elif kind == "S":
                o = vo["ropeS"]
            else:
                o = vo["ropeH"]
            return vecs[0:n, o:o + 8], vecs[0:n, o + 8:o + 16]

        def attention(nq, qslot, tiles, c0, halo=False):
            qT_ = qTv(qslot)
            nt_ = len(tiles)
            state = {}

            def s1(h):
                pset = h % 2
                pTb = B["pT%d" % pset] if nq == 128 else B["pTS%d" % pset]
                pviews = []
                for ti_, (kTa, kTb_, vDa, vDb_, nk, mask) in enumerate(tiles):
                    bk, bkB = alloc()
                    A("pe", "matmul", bk[0:nk, 0:4 * nq], lhsT=kTa[0:64, h * nk:(h + 1) * nk],
                      rhs=qT_[0:64, h * 4 * nq:(h + 1) * 4 * nq], start=True, stop=True, reads=[kTb_] + qTB(qslot),
                      writes=[bkB])
                    if nq == 128:
                        pv = pT[:, pset, ti_, :]
                    else:
                        pv = pTS[:, pset, ti_, :]
                    pviews.append(pv)
                    s3 = bk[:, 0:4 * nq].rearrange("p (g q) -> p g q", g=4)
                    p3 = pv[:, 0:4 * nq].rearrange("p (g q) -> p g q", g=4)
                    kw = {}
                    if mask == "full":
                        regs = [(0, nk, 0, nq)]
                    elif mask == "prev":
                        regs = [(0, 64, 0, 64), (64, 128, 0, 128)]
                    else:
                        regs = [(0, 64, 0, 128), (64, 128, 64, 128)]
                    for (p0, p1, q0, q1) in regs:
                        bias = vecs[p0:p1, vo["halo"]:vo["halo"] + 1] if (halo and mask == "prev") else 0.0
                        A("act", "activation", out=p3[p0:p1, :, q0:q1], in_=s3[p0:p1, :, q0:q1], func=AF.Exp,
                          scale=HD ** -0.5, bias=bias, reads=[bkB, B["vecs"]], writes=[pTb])
                state[h] = (pTb, pviews)

            def s2(h):
                pTb, pviews = state[h]
                bo, boB = alloc()
                bd, bdB = alloc()
                for ti_, (kTa, kTb_, vDa, vDb_, nk, mask) in enumerate(tiles):
                    A("pe", "matmul", bo[:, 0:4 * nq], lhsT=vDa[0:nk, h * 128:(h + 1) * 128],
                      rhs=pviews[ti_][0:nk, 0:4 * nq], start=(ti_ == 0), stop=(ti_ == nt_ - 1), reads=[vDb_, pTb],
                      writes=[boB])
                for ti_, (kTa, kTb_, vDa, vDb_, nk, mask) in enumerate(tiles):
                    A("pe", "matmul", bd[:, 0:4 * nq], lhsT=onesb[0:nk, :], rhs=pviews[ti_][0:nk, 0:4 * nq],
                      start=(ti_ == 0), stop=(ti_ == nt_ - 1), reads=[B["onesb"], pTb], writes=[bdB])
                ti = talloc()
                r3 = tmp[:, ti, 0:4 * nq].rearrange("p (g q) -> p g q", g=4)
                for g in range(4):
                    A("act", "activation", out=tmp[:, ti, g * nq:(g + 1) * nq], in_=bd[:, g * nq:(g + 1) * nq],
                      func=AF.Identity, scale=1.0, bias=esink[:, 4 * h + g:4 * h + g + 1], reads=[bdB, B["esink"]],
                      writes=[tmpB[ti]])
                A("dve", "reciprocal", out=tmp[:, ti, 0:4 * nq], in_=tmp[:, ti, 0:4 * nq], reads=[tmpB[ti]],
                  writes=[tmpB[ti]])
                o3 = bo[:, 0:4 * nq].rearrange("p (g q) -> p g q", g=4)
                for par in range(2):
                    p0 = par * 64
                    A("dve", "tensor_tensor", out=gT[p0:p0 + 64, 8 + 2 * h:8 + 2 * h + 2, c0:c0 + nq],
                      in0=o3[p0:p0 + 64, par::2, :], in1=r3[p0:p0 + 64, par::2, :], op=ALU.mult,
                      reads=[boB, tmpB[ti]], writes=[gB[8 + 2 * h], gB[8 + 2 * h + 1]])

            for h in range(NKV + 1):
                if h < NKV:
                    s1(h)
                if h >= 1:
                    s2(h - 1)

        def spatial_gate(n, vslot, c0, is_s):
            for half in range(2):
                bk, bkB = alloc()
                for gg in range(4):
                    g = half * 4 + gg
                    A("pe", "matmul", bk[:, gg * n:(gg + 1) * n], lhsT=vnb(vslot)[0:n, g * 128:(g + 1) * 128],
                      rhs=wsTb[0:n, g, 0:n], start=True, stop=True, reads=vnbB(vslot) + [B["wsTb"]], writes=[bkB])
                for gg in range(4):
                    g = half * 4 + gg
                    ti = talloc()
                    Cv = CgS[:, g, 0:n] if is_s else Cg[:, g, 0:n]
                    A("dve", "scalar_tensor_tensor", out=tmp[:, ti, 0:n], in0=bk[:, gg * n:(gg + 1) * n],
                      scalar=lng(g), in1=Cv, op0=ALU.mult, op1=ALU.add,
                      reads=[bkB, B["vecs"], B["Cg"], B["CgS"]], writes=[tmpB[ti]])
                    A("dve", "tensor_tensor", out=gT[:, g, c0:c0 + n], in0=gT[:, g, c0:c0 + n], in1=tmp[:, ti, 0:n],
                      op=ALU.mult, reads=[gB[g], tmpB[ti]], writes=[gB[g]])

        def mixer2(pss, T, Tm, segs, mseg, blocks, first_p, last):
            norm(1, T, segs)
            cur = Cur(pss, "MIXA")
            full_blocks = [b for b in blocks if b["full"]]
            for i, b in enumerate(full_blocks):
                b["gi"] = i
                b["vslot"] = i % 2
            for cg in range(2):
                groups = [full_blocks[i:i + 3] for i in range(0, len(full_blocks), 3)]
                for grp in groups:
                    bks = [alloc() for _ in grp]
                    for kc in range(KC):
                        w, wB = cur.blk(cg * 64 + kc * 4, 4)
                        for b, (bk, bkB) in zip(grp, bks):
                            A("pe", "matmul", bk[0:b["n"], :], lhsT=hT[:, kc, b["c0"]:b["c0"] + b["n"]], rhs=w,
                              start=(kc == 0), stop=(kc == KC - 1), reads=[wB, hB[kc]], writes=[bkB])
                    for b, (bk, bkB) in zip(grp, bks):
                        n, gi = b["n"], b["gi"]
                        A("act", "activation", out=gv[0:n, gi, cg * 512:(cg + 1) * 512], in_=bk[0:n, :], func=AF.Gelu,
                          reads=[bkB], writes=[gvB[gi]])
                ring_release()
                ring_release()
            for cg in range(2, 4):
                bl = full_blocks
                groups = [bl[i:i + 3] for i in range(0, len(bl), 3)]
                for grp in groups:
                    bks = [alloc() for _ in grp]
                    for kc in range(KC):
                        w, wB = cur.blk(cg * 64 + kc * 4, 4)
                        for b, (bk, bkB) in zip(grp, bks):
                            A("pe", "matmul", bk[0:b["n"], :], lhsT=hT[:, kc, b["c0"]:b["c0"] + b["n"]], rhs=w,
                              start=(kc == 0), stop=(kc == KC - 1), reads=[wB, hB[kc]], writes=[bkB])
                    for b, bkp in zip(grp, bks):
                        b["cg%d" % cg] = bkp
                        if cg < 4:
                            n, gi = b["n"], b["gi"]
                            A("act", "activation", out=qraw[0:n, gi, (cg - 2) * 512:(cg - 1) * 512], in_=bkp[0][0:n, :],
                              func=AF.Copy, reads=[bkp[1]], writes=qrawBs(gi))
                ring_release()
                ring_release()
            return full_blocks

        qraw_all = gT[:, 16:32, :].rearrange("p a c -> p (a c)").bitcast(F32)

        class _QR:
            def __getitem__(self, key):
                p, gi, c = key
                if isinstance(c, slice) and c.start is None:
                    return qraw_all[p, gi * 1024:(gi + 1) * 1024]
                return qraw_all[p, gi * 1024 + c.start:gi * 1024 + c.stop]
        qraw = _QR()

        qrawBs = lambda gi: [gB[16 + 4 * gi + i] for i in range(4)]

        def qkA(b, kslot, want_q, sbi, out_v=None):
            n, kind, jg = b["n"], b["kind"], b["jg"]
            bkv = b["cg4"]
            lo = 0 if want_q else 1024
            W = 1280 - lo
            nh = W // 64
            A("act", "activation", out=qkf[0:n, 0:256], in_=bkv[0][0:n, 0:256], func=AF.Copy, reads=[bkv[1]],
              writes=[B["qkf"]])
            vd3 = vaDr[0:n, kslot, :].rearrange("p (h d) -> p h d", h=4)
            va3 = bkv[0][0:n, 256:512].rearrange("p (h d) -> p h d", h=4)
            A("act", "activation", out=vd3[:, :, 0:64], in_=va3, func=AF.Copy, reads=[bkv[1]], writes=[vaDB[kslot]])
            A("act", "activation", out=vd3[:, :, 64:128], in_=va3, func=AF.Copy, reads=[bkv[1]], writes=[vaDB[kslot]])
            if out_v is not None:
                A("act", "activation", out=vaf[0:n, :], in_=bkv[0][0:n, 256:512], func=AF.Copy, reads=[bkv[1]],
                  writes=[B["vaf"]])
                out_dmas.append(DMA("act", out_v, vaf[0:n, :], s_misc[7], reads=[B["vaf"]]))

            def src(c0_, c1_):
                return None
            if want_q:
                gi = b["gi"]
                A("act", "activation", out=sqnT[0:n, sbi, 0:1024], in_=qraw[0:n, gi, :], func=AF.Square,
                  reads=qrawBs(gi), writes=[B["sqn%d" % sbi]])
            A("act", "activation", out=sqnT[0:n, sbi, 1024:1280], in_=qkf[0:n, 0:256], func=AF.Square,
              reads=[B["qkf"]], writes=[B["sqn%d" % sbi]])
            A("dve", "tensor_reduce", out=small[0:n, 0:nh], in_=sqnT[0:n, sbi, lo:1280].rearrange("p (h d) -> p h d", d=64),
              axis=AX.X, op=ALU.add, reads=[B["sqn%d" % sbi]], writes=[B["small"]])
            A("act", "activation", out=small[0:n, 0:nh], in_=small[0:n, 0:nh], func=AF.Sqrt, scale=1.0 / HD,
              bias=epsc[0:n, 0:1], reads=[B["small"], B["epsc"]], writes=[B["small"]])
            A("dve", "reciprocal", out=small[0:n, 0:nh], in_=small[0:n, 0:nh], reads=[B["small"]],
              writes=[B["small"]])
            if want_q:
                gi = b["gi"]
                A("dve", "tensor_tensor", out=sqnT[0:n, sbi, 0:1024].rearrange("p (h d) -> p h d", d=64),
                  in0=qraw[0:n, gi, :].rearrange("p (h d) -> p h d", d=64),
                  in1=small[0:n, 0:16].unsqueeze(2).to_broadcast([n, 16, 64]), op=ALU.mult,
                  reads=qrawBs(gi) + [B["small"]], writes=[B["sqn%d" % sbi]])
                A("pool", "tensor_tensor", out=sqnT[0:n, sbi, 0:1024].rearrange("p (h d) -> p h d", d=64),
                  in0=sqnT[0:n, sbi, 0:1024].rearrange("p (h d) -> p h d", d=64),
                  in1=vecs[0:n, vo["gq"]:vo["gq"] + 64].unsqueeze(1).to_broadcast([n, 16, 64]), op=ALU.mult,
                  reads=[B["sqn%d" % sbi], B["vecs"]], writes=[B["sqn%d" % sbi]])
            A("dve", "tensor_tensor", out=sqnT[0:n, sbi, 1024:1280].rearrange("p (h d) -> p h d", d=64),
              in0=qkf[0:n, 0:256].rearrange("p (h d) -> p h d", d=64),
              in1=small[0:n, nh - 4:nh].unsqueeze(2).to_broadcast([n, 4, 64]), op=ALU.mult,
              reads=[B["qkf"], B["small"]], writes=[B["sqn%d" % sbi]])
            A("pool", "tensor_tensor", out=sqnT[0:n, sbi, 1024:1280].rearrange("p (h d) -> p h d", d=64),
              in0=sqnT[0:n, sbi, 1024:1280].rearrange("p (h d) -> p h d", d=64),
              in1=vecs[0:n, vo["gk"]:vo["gk"] + 64].unsqueeze(1).to_broadcast([n, 4, 64]), op=ALU.mult,
              reads=[B["sqn%d" % sbi], B["vecs"]], writes=[B["sqn%d" % sbi]])

        def qkB(b, kslot, want_q, sbi, out_k=None):
            n, kind, jg = b["n"], b["kind"], b["jg"]
            lo = 0 if want_q else 1024
            nh = (1280 - lo) // 64
            X = sqnT[0:n, sbi, lo:1280].rearrange("p (h d) -> p h d", d=64)
            x1, x2 = X[:, :, 0:8], X[:, :, 8:16]
            cosA, sinA = rope_ap(kind, jg, n)
            cb = cosA.unsqueeze(1).to_broadcast([n, nh, 8])
            sbb = sinA.unsqueeze(1).to_broadcast([n, nh, 8])
            rt = lambda i: ropet[0:n, i, 0:nh * 8].rearrange("p (h d) -> p h d", d=8)
            rd = [B["sqn%d" % sbi], B["vecs"]]
            A("dve", "tensor_tensor", out=rt(0), in0=x1, in1=cb, op=ALU.mult, reads=rd, writes=[B["ropet"]])
            A("dve", "tensor_tensor", out=rt(1), in0=x2, in1=sbb, op=ALU.mult, reads=rd, writes=[B["ropet"]])
            A("dve", "tensor_tensor", out=rt(2), in0=x2, in1=cb, op=ALU.mult, reads=rd, writes=[B["ropet"]])
            A("dve", "tensor_tensor", out=rt(3), in0=x1, in1=sbb, op=ALU.mult, reads=rd, writes=[B["ropet"]])
            A("dve", "tensor_tensor", out=x1, in0=rt(0), in1=rt(1), op=ALU.subtract, reads=[B["ropet"]],
              writes=[B["sqn%d" % sbi]])
            A("dve", "tensor_tensor", out=x2, in0=rt(2), in1=rt(3), op=ALU.add, reads=[B["ropet"]],
              writes=[B["sqn%d" % sbi]])
            A("act", "activation", out=qkb[0:n, lo:1280], in_=sqnT[0:n, sbi, lo:1280], func=AF.Copy, reads=[B["sqn%d" % sbi]],
              writes=[B["qkb"]])
            if out_k is not None:
                out_dmas.append(DMA("act", out_k, sqnT[0:n, sbi, 1024:1280], s_misc[8], reads=[B["sqn%d" % sbi]]))
            qslot = None
            if want_q:
                qslot = st["blk"] % 2
                st["blk"] += 1
                for half in range(2):
                    bk, bkB = alloc()
                    bkb_ = bk[:, :].bitcast(BF16)
                    for hh in range(8):
                        h = half * 8 + hh
                        A("pe", "transpose", bkb_[0:64, hh * n:(hh + 1) * n], qkb[0:n, h * 64:(h + 1) * 64],
                          identb[0:n, 0:n], reads=[B["qkb"], B["identb"]], writes=[bkB])
                    A("act", "activation", out=qTv(qslot)[0:64, half * 8 * n:(half + 1) * 8 * n],
                      in_=bkb_[0:64, 0:8 * n], func=AF.Copy, reads=[bkB], writes=qTB(qslot))
            bk, bkB = alloc()
            bkb_ = bk[:, :].bitcast(BF16)
            for h in range(NKV):
                A("pe", "transpose", bkb_[0:64, h * n:(h + 1) * n], qkb[0:n, 1024 + h * 64:1024 + (h + 1) * 64],
                  identb[0:n, 0:n], reads=[B["qkb"], B["identb"]], writes=[bkB])
            A("act", "activation", out=kTr[0:64, kslot, 0:4 * n], in_=bkb_[0:64, 0:4 * n], func=AF.Copy, reads=[bkB],
              writes=[kTB[kslot]])
            return qslot

        def v_process2(b):
            n, gi, vslot = b["n"], b["gi"], b["vslot"]
            is_s = b["kind"] == "S"
            A("dve", "bn_stats", out=small[0:n, 32:38], in_=gv[0:n, gi, 0:512], reads=[gvB[gi]], writes=[B["ss"]])
            A("dve", "bn_stats", out=small[0:n, 38:44], in_=gv[0:n, gi, 512:1024], reads=[gvB[gi]], writes=[B["ss"]])
            A("dve", "bn_aggr", out=small[0:n, 44:46], in_=small[0:n, 32:44], reads=[B["ss"]], writes=[B["ss"]])
            A("act", "activation", out=small[0:n, 46:47], in_=small[0:n, 45:46], func=AF.Sqrt, scale=1.0,
              bias=epsc[0:n, 0:1], reads=[B["ss"], B["epsc"]], writes=[B["ss"]])
            A("dve", "reciprocal", out=small[0:n, 46:47], in_=small[0:n, 46:47], reads=[B["ss"]], writes=[B["ss"]])
            if is_s:
                A("dve", "tensor_scalar", out=gv[0:n, gi, :], in0=gv[0:n, gi, :], scalar1=small[0:n, 44:45],
                  scalar2=small[0:n, 46:47], op0=ALU.subtract, op1=ALU.mult, reads=[gvB[gi], B["ss"]],
                  writes=[gvB[gi]])
                A("dve", "tensor_copy", out=vnb(vslot)[0:n, :], in_=gv[0:n, gi, :], reads=[gvB[gi]],
                  writes=vnbB(vslot))
                bk, bkB = alloc()
                for g in range(8):
                    A("pe", "matmul", bk[:, g * n:(g + 1) * n], lhsT=gv[0:n, gi, g * 128:(g + 1) * 128],
                      rhs=identf[0:n, 0:n], start=True, stop=True, reads=[gvB[gi], B["identf"]], writes=[bkB])
                for g in range(8):
                    A("act", "activation", out=vnTs[:, g, :], in_=bk[:, g * n:(g + 1) * n], func=AF.Identity,
                      scale=lng(g), bias=lnb(g), reads=[bkB, B["vecs"]], writes=[B["vnTs"]])
                out_dmas.append(DMA("act", vnT_d.rearrange("(g p) t -> p g t", p=128), vnTs[:], s_misc[9],
                                    reads=[B["vnTs"]]))
            else:
                A("dve", "scalar_tensor_tensor", out=small[0:n, 47:48], in0=small[0:n, 44:45], scalar=-1.0,
                  in1=small[0:n, 46:47], op0=ALU.mult, op1=ALU.mult, reads=[B["ss"]], writes=[B["ss"]])
                A("act", "activation", out=vnb(vslot)[0:n, :], in_=gv[0:n, gi, :], func=AF.Identity,
                  scale=small[0:n, 46:47], bias=small[0:n, 47:48], reads=[gvB[gi], B["ss"]], writes=vnbB(vslot))

        def mixer_rest(pss, T, Tm, mseg, blocks, first_p, last, konly=False):
            full_blocks = [b for b in blocks if b["full"]]
            cur = Cur(pss, "MIXU") if not konly else None
            for c in range(8 if not konly else 0):
                bk, bkB = alloc()
                for kc in range(KC):
                    w, wB = cur.blk(c * KC + kc)
                    A("pe", "matmul", bk[:, 0:T], lhsT=w, rhs=hT[:, kc, 0:T], start=(kc == 0), stop=(kc == KC - 1),
                      reads=[wB, hB[kc]], writes=[bkB])
                    if (c * KC + kc) % UB == UB - 1:
                        ring_release()
                A("act", "activation", out=gT[:, c, 0:T], in_=bk[:, 0:T], func=AF.Gelu, reads=[bkB], writes=[gB[c]])
            curk = Cur(pss, "MIXK")
            def stageA(b, i):
                bkp = alloc()
                for kc in range(KC):
                    w, wB = curk.blk(kc * 4, 4)
                    A("pe", "matmul", bkp[0][0:b["n"], :], lhsT=hT[:, kc, b["c0"]:b["c0"] + b["n"]], rhs=w,
                      start=(kc == 0), stop=(kc == KC - 1), reads=[wB, hB[kc]], writes=[bkp[1]])
                b["cg4"] = bkp
                if b is blocks[-1]:
                    ring_release()
                    ring_release()
                ks = st["kv"] % 4
                st["kv"] += 1
                b["kslot"] = ks
                b["sbi"] = i % 2
                if b["kind"] == "H":
                    qkA(b, ks, True, b["sbi"])
                    st["prev_k"] = ks
                    b["attn"] = False
                    return
                v_process2(b)
                spatial_gate(b["n"], b["vslot"], b["c0"], b["kind"] == "S")
                b["attn"] = True
                if b["kind"] == "S":
                    qkA(b, ks, True, b["sbi"], out_v=vs_d[96:128, :])
                    b["out_k"] = ks_d[96:128, :]
                    b["tiles"] = [(kTc[:, :], B["kTc"], vaDc[:, :], B["vaDc"], 128, "full"),
                                  (kTr[:, ks, :], kTB[ks], vaDr[:, ks, :], vaDB[ks], 32, "full")]
                else:
                    is_last = last and b is blocks[-1]
                    qkA(b, ks, True, b["sbi"], out_v=vlast_d if is_last else None)
                    b["out_k"] = klast_d if is_last else None
                    pk = st["prev_k"]
                    b["tiles"] = [(kTr[:, pk, :], kTB[pk], vaDr[:, pk, :], vaDB[pk], 128, "prev"),
                                  (kTr[:, ks, :], kTB[ks], vaDr[:, ks, :], vaDB[ks], 128, "cur")]
                    st["prev_k"] = ks

            def stageB(b):
                b["qs"] = qkB(b, b["kslot"], True, b["sbi"], out_k=b.get("out_k"))

            def back(b):
                if not b["attn"]:
                    return
                if b["kind"] == "S":
                    attention(32, b["qs"], b["tiles"], b["c0"])
                else:
                    attention(128, b["qs"], b["tiles"], b["c0"], halo=(first_p and b is blocks[0]))

            nb_ = len(blocks)
            for i in range(nb_ + 2):
                if i < nb_:
                    stageA(blocks[i], i)
                if 0 <= i - 2 < nb_:
                    back(blocks[i - 2])
                if 0 <= i - 1 < nb_:
                    stageB(blocks[i - 1])
            if konly:
                return
            (m0, m1, mr) = mseg
            cur = Cur(pss, "MIXM")
            for dc in range(KC):
                bga, bgaB = alloc()
                bpa, bpaB = alloc()
                bgb, bgbB = alloc()
                bpb, bpbB = alloc()
                base = dc * 48
                for kc in range(KC):
                    w, wB = cur.blk(base + kc)
                    A("pe", "matmul", bga[:, m0:m1], lhsT=w, rhs=hT[:, kc, m0:m1], start=(kc == 0), stop=(kc == KC - 1),
                      reads=[wB, hB[kc]], writes=[bgaB])
                for c in range(8):
                    w, wB = cur.blk(base + 16 + c)
                    A("pe", "matmul", bpa[:, m0:m1], lhsT=w, rhs=gT[:, c, m0:m1], start=(c == 0), stop=(c == 7),
                      reads=[wB, gB[c]], writes=[bpaB])
                for kc in range(KC):
                    w, wB = cur.blk(base + 24 + kc)
                    A("pe", "matmul", bgb[:, m0:m1], lhsT=w, rhs=hT[:, kc, m0:m1], start=(kc == 0), stop=(kc == KC - 1),
                      reads=[wB, hB[kc]], writes=[bgbB])
                for c in range(8):
                    w, wB = cur.blk(base + 40 + c)
                    A("pe", "matmul", bpb[:, m0:m1], lhsT=w, rhs=gT[:, 8 + c, m0:m1], start=(c == 0), stop=(c == 7),
                      reads=[wB, gB[8 + c]], writes=[bpbB])
                if dc % 2 == 1:
                    ring_release()
                    ring_release()
                    ring_release()
                t1 = talloc()
                A("act", "activation", out=tmp[:, t1, m0:m1], in_=bga[:, m0:m1], func=AF.Sigmoid, reads=[bgaB],
                  writes=[tmpB[t1]])
                A("dve", "tensor_tensor", out=tmp[:, t1, m0:m1], in0=bpa[:, m0:m1], in1=tmp[:, t1, m0:m1], op=ALU.mult,
                  reads=[bpaB, tmpB[t1]], writes=[tmpB[t1]])
                t2 = talloc()
                A("act", "activation", out=tmp[:, t2, m0:m1], in_=bgb[:, m0:m1], func=AF.Sigmoid, reads=[bgbB],
                  writes=[tmpB[t2]])
                A("dve", "tensor_tensor", out=tmp[:, t2, m0:m1], in0=bpb[:, m0:m1], in1=tmp[:, t2, m0:m1], op=ALU.mult,
                  reads=[bpbB, tmpB[t2]], writes=[tmpB[t2]])
                A("dve", "tensor_tensor", out=gT[:, 16 + dc, m0:m1], in0=tmp[:, t1, m0:m1], in1=tmp[:, t2, m0:m1],
                  op=ALU.add, reads=[tmpB[t1], tmpB[t2]], writes=[gB[16 + dc]])
            cur = Cur(pss, "MIXO")
            for dc in range(KC):
                bk, bkB = alloc()
                for kc in range(KC):
                    w, wB = cur.blk(dc * KC + kc)
                    A("pe", "matmul", bk[:, m0:m1], lhsT=w, rhs=gT[:, 16 + kc, m0:m1], start=(kc == 0),
                      stop=(kc == KC - 1), reads=[wB, gB[16 + kc]], writes=[bkB])
                if dc % 2 == 1:
                    ring_release()
                A("dve", "scalar_tensor_tensor", out=xT[:, dc, m0:m1], in0=bk[:, m0:m1], scalar=sc(1, mr, 2, dc),
                  in1=xT[:, dc, m0:m1], op0=ALU.mult, op1=ALU.add, reads=[bkB, xB[dc], B["scal"]], writes=[xB[dc]])

        STOP = int(os.environ.get("KSTOP", "99"))
        class _Stop(Exception):
            pass
        def stop_if(k):
            if STOP == k:
                raise _Stop()
        try:
            segs0 = [(0, TH, 1)]
            DMA("sp", xT[:, :, 0:TH], xsh_d.rearrange("(dc p) t -> p dc t", p=128), s_misc[10], writes=xB)
            ada(0, 0)
            ffn(0, 0, TH, segs0, "F1A", "F1B")
            ada(0, 1)
            blocks0 = [dict(kind="H", c0=0, n=TH, jg=0, full=True)]
            mixer2(0, TH, TH, segs0, (0, TH, 1), blocks0, False, False)
            mixer_rest(0, TH, TH, (0, TH, 1), blocks0, False, False, konly=True)

            segsP = [(0, TP, 1)]
            for t in range(1, NT + 1):
                t0 = (t - 1) * TP
                for dc in range(KC):
                    DMA("sp", xT[:, dc, :], xT_d[dc * 128:(dc + 1) * 128, t0:t0 + TP], s_x[dc], writes=[xB[dc]])
                ffn(t, 0, TP, segsP, "F1A", "F1B")
                blocksP = [dict(kind="P", c0=j * 128, n=128, jg=(t - 1) * 4 + j, full=True) for j in range(4)]
                mixer2(t, TP, TP, segsP, (0, TP, 1), blocksP, t == 1, t == NT)
                mixer_rest(t, TP, TP, (0, TP, 1), blocksP, t == 1, t == NT)
                if t == 1:
                    ada(1, 2)
                ffn(t, 2, TP, segsP, "F2A", "F2B",
                    store=lambda dc, t0=t0: yT_d[dc * 128:(dc + 1) * 128, t0:t0 + TP])

            ps_ = NT + 1
            segsS = [(0, TS, 0)]
            DMA("sp", xT[:, :, 0:TS], xs_d.rearrange("(dc p) t -> p dc t", p=128), s_misc[12], writes=xB)
            ffn(ps_, 0, TS, segsS, "F1A", "F1B")
            blocksS = [dict(kind="S", c0=0, n=TS, jg=0, full=True)]
            mixer2(ps_, TS, TS, segsS, (0, TS, 0), blocksS, False, False)
            mixer_rest(ps_, TS, TS, (0, TS, 0), blocksS, False, False)
            ffn(ps_, 2, TS, segsS, "F2A", "F2B")
            out_dmas.append(DMA("act", yTs_d.rearrange("(dc p) t -> p dc t", p=128), xT[:, :, 0:TS], s_misc[11],
                                reads=xB))

        except _Stop:
            pass
        if STOP == 99:
            assert rs["released"] == NSEQ, (rs, NSEQ)
        for e in ("act", "sp"):
            P.add(e, lambda h: None, extra_deps=[i for i in out_dmas])
        P.emit(blk, s_eng)
    return nc


def _blocks(inp, NF):
    def r4(w, a, b):
        K, N = w.shape
        return w.reshape(K // 128, 128, N // 128, 128)
    w_ada = inp["w_ada"][0]
    w_in = inp["w_in"][0]
    out = []
    def ada(s):
        w = r4(w_ada[:, s * 6144:(s + 1) * 6144], 0, 0)
        return w.transpose(2, 0, 1, 3).reshape(-1, 128, 128)
    def ffa(w1, w3):
        a = r4(w1, 0, 0).transpose(2, 0, 1, 3)
        b = r4(w3, 0, 0).transpose(2, 0, 1, 3)
        return np.concatenate([a, b], axis=1).reshape(-1, 128, 128)
    def ffb(w2):
        return r4(w2, 0, 0).transpose(2, 0, 1, 3).reshape(-1, 128, 128)
    def mixa():
        cols = [np.arange(1024, 1536), np.arange(1536, 2048), np.arange(2048, 2560), np.arange(2560, 3072)]
        res = []
        for c in cols:
            w = w_in[:, c].reshape(16, 128, 4, 128).transpose(0, 2, 1, 3)
            res.append(w.reshape(-1, 128, 128))
        return np.concatenate(res, 0)
    def mixk():
        w = w_in[:, 3072:3584].reshape(16, 128, 4, 128).transpose(0, 2, 1, 3)
        return w.reshape(-1, 128, 128)
    def mixu():
        return r4(w_in[:, 0:1024], 0, 0).transpose(2, 0, 1, 3).reshape(-1, 128, 128)
    def mixm():
        ga = r4(w_in[:, 3584:5632], 0, 0).transpose(2, 0, 1, 3)
        gb = r4(w_in[:, 5632:7680], 0, 0).transpose(2, 0, 1, 3)
        pa = r4(inp["w_pa"][0], 0, 0).transpose(2, 0, 1, 3)
        pb = r4(inp["w_pb"][0], 0, 0).transpose(2, 0, 1, 3)
        return np.concatenate([ga, pa, gb, pb], axis=1).reshape(-1, 128, 128)
    def mixo():
        return r4(inp["w_o"][0], 0, 0).transpose(2, 0, 1, 3).reshape(-1, 128, 128)
    parts = [ada(0), ffa(inp["w1_ffn1"][0], inp["w3_ffn1"][0]), ffb(inp["w2_ffn1"][0]), ada(1), mixa(), mixu(), mixk(), mixm(),
             mixo(), ada(2), ffa(inp["w1_ffn2"][0], inp["w3_ffn2"][0]), ffb(inp["w2_ffn2"][0])]
    allb = np.concatenate(parts, 0)
    NB = allb.shape[0]
    assert NB % UB == 0
    u = allb.reshape(NB // UB, UB, 128, 128).transpose(0, 2, 1, 3).reshape(NB // UB, 128, UB * 128)
    return np.ascontiguousarray(u, dtype=np.float32)


def _rope_tab(pos):
    inv = ROPE_THETA ** (-np.arange(0, 16, 2, dtype=np.float32) / np.float32(16))
    ang = pos.astype(np.float32)[:, None] * inv.astype(np.float32)[None, :]
    return np.concatenate([np.cos(ang), np.sin(ang)], axis=1).astype(np.float32)


_CACHE = {}


def run(inp, NT, NF):
    inp = {k: np.asarray(v, dtype=np.float32) for k, v in inp.items()}
    key = (NT, NF)
    if key not in _CACHE:
        _CACHE[key] = build(NT, NF)
    nc = _CACHE[key]
    vo, NV = vec_layout(NT)
    wsrc = _blocks(inp, NF)
    xp = inp["x_prompt"]
    xs = inp["x_sample"]
    Bp, SEQ, _ = xp.shape
    HALF = NT * TP
    assert SEQ == 2 * HALF and Bp == 4 and xs.shape[0] == 8
    wsT = np.ascontiguousarray(inp["w_s"][0].transpose(2, 0, 1).reshape(128, 1024))
    bsrow = np.ascontiguousarray(inp["b_s"][0].reshape(1, 1024))
    in_maps = []
    for c in range(8):
        b, hf = c // 2, c % 2
        xT = np.ascontiguousarray(xp[b, hf * HALF:(hf + 1) * HALF, :].T)
        if hf == 1:
            halo = xp[b, HALF - TH:HALF, :]
        else:
            halo = np.zeros((TH, D), np.float32)
        xsh = np.ascontiguousarray(halo.T)
        xs_t = np.ascontiguousarray(xs[c].T)
        vecs = np.zeros((128, NV), np.float32)
        cc = np.stack([inp["c_sample"][c], inp["c_prompt"][b]], 1)
        vecs[:, vo["c"]:vo["c"] + 32] = cc.reshape(16, 128, 2).transpose(1, 0, 2).reshape(128, 32)
        vecs[:, vo["bada"]:vo["bada"] + 144] = inp["b_ada"][0].reshape(144, 128).T
        for s, nm in enumerate(("g_ffn1", "g_mix", "g_ffn2")):
            vecs[:, vo["g"] + s * 16:vo["g"] + s * 16 + 16] = inp[nm][0].reshape(16, 128).T
        vecs[:, vo["halo"]] = 0.0 if hf == 1 else -30000.0
        vecs[:, vo["sink"]:vo["sink"] + 16] = inp["sinks"][0][None, :]
        vecs[:, vo["gq"]:vo["gq"] + 64] = inp["g_q"][0][None, :]
        vecs[:, vo["gk"]:vo["gk"] + 64] = inp["g_k"][0][None, :]
        vecs[:, vo["lng"]:vo["lng"] + 8] = inp["ln_v_g"][0].reshape(8, 128).T
        vecs[:, vo["lnb"]:vo["lnb"] + 8] = inp["ln_v_b"][0].reshape(8, 128).T
        posP = hf * HALF + np.arange(HALF)
        tabP = _rope_tab(posP).reshape(NT * 4, 128, 16).transpose(1, 0, 2).reshape(128, NT * 4 * 16)
        vecs[:, vo["ropeP"]:vo["ropeP"] + NT * 4 * 16] = tabP
        vecs[0:TS, vo["ropeS"]:vo["ropeS"] + 16] = _rope_tab(PAST_LEN + np.arange(TS))
        vecs[:, vo["ropeH"]:vo["ropeH"] + 16] = _rope_tab(np.maximum(hf * HALF - TH + np.arange(TH), 0))
        in_maps.append({
            "wsrc": wsrc, "xT": xT, "xsh": xsh, "xs": xs_t, "vecs": vecs, "wsT": wsT, "bsrow": bsrow,
            "cache_k": np.ascontiguousarray(inp["cache_swa_k"][0, c].reshape(128, 256)),
            "cache_v": np.ascontiguousarray(inp["cache_swa_v"][0, c].reshape(128, 256)),
        })
    res = run_bass_kernel_spmd(nc, in_maps, core_ids=list(range(8)))
    R = res.results
    y_p = np.empty((4, SEQ, D), np.float32)
    y_s = np.empty((8, TS, D), np.float32)
    kp = np.empty((1, 4, 128, 4, 64), np.float32)
    vp = np.empty((1, 4, 128, 4, 64), np.float32)
    ks = np.empty((1, 8, 128, 4, 64), np.float32)
    vs = np.empty((1, 8, 128, 4, 64), np.float32)
    gvs = np.empty((1, 8, TS, DA), np.float32)
    for c in range(8):
        b, hf = c // 2, c % 2
        if R[c]["yT"] is None:
            continue
        y_p[b, hf * HALF:(hf + 1) * HALF, :] = R[c]["yT"].T
        y_s[c] = R[c]["yTs"].T
        if hf == 1:
            kp[0, b] = R[c]["klast"].reshape(128, 4, 64)
            vp[0, b] = R[c]["vlast"].reshape(128, 4, 64)
        ks[0, c] = R[c]["ks"].reshape(128, 4, 64)
        vs[0, c] = R[c]["vs"].reshape(128, 4, 64)
        gvs[0, c] = R[c]["vnT"].T
    return (y_p, y_s, kp, vp, ks, vs, gvs)


def kernel(**inputs):
    return run(inputs, 8, 44)
```
